# Optimizing a Trainium2 kernel written in Bass

```python
import jax, jax.numpy as jnp
from jax import lax
import numpy as np

D_MODEL = 1024
BATCH = 8
SEQ = 2048
DEPTH = 4
DEC_BATCH = 128
DEC_SEQ = 4
PAST_LEN = 16384
PAGE_SIZE = 128

D_SC = D_MODEL // 2
SC_WIDTH = 3
D_RK = D_MODEL
RK_HEAD_DIM = 64
RK_HEADS = D_RK // RK_HEAD_DIM
DECAY_LORA = 64
AAA_LORA = 64
MV_LORA = 32
GATE_LORA = 128
RK_PROJ = 3 * D_RK + DECAY_LORA + AAA_LORA + MV_LORA + GATE_LORA
RK_SPLITS = [D_RK, 2 * D_RK, 3 * D_RK, 3 * D_RK + DECAY_LORA, 3 * D_RK + DECAY_LORA + AAA_LORA,
             3 * D_RK + DECAY_LORA + AAA_LORA + MV_LORA]
D_CF = D_MODEL // 2
CF_WIDTH = 31
N_BRANCH = 3
N_IN = 3 * D_SC + RK_PROJ + 2 * D_CF + N_BRANCH * D_MODEL
IN_SPLITS = [3 * D_SC, 3 * D_SC + RK_PROJ, 3 * D_SC + RK_PROJ + 2 * D_CF]
D_FF = 4 * D_MODEL
D_PLE = 256
RMS_EPS = 1e-6
LN_EPS = 1e-5
GN_EPS = 64e-5

kernel_name = "hybrid_sconv_rwkv7_conformer_decode_step"


def rmsnorm(x, g):
    xf = x.astype(jnp.float32)
    y = xf * lax.rsqrt(jnp.mean(xf * xf, axis=-1, keepdims=True) + RMS_EPS)
    return (y * g.astype(jnp.float32)).astype(x.dtype)


def layernorm(x, g, b, eps):
    xf = x.astype(jnp.float32)
    mu = jnp.mean(xf, axis=-1, keepdims=True)
    xc = xf - mu
    var = jnp.mean(xc * xc, axis=-1, keepdims=True)
    return (xc * lax.rsqrt(var + eps) * g.astype(jnp.float32) + b.astype(jnp.float32)).astype(x.dtype)


def causal_dwconv(x, buf, w):
    kw = w.shape[0]
    xp = jnp.concatenate([buf.astype(x.dtype), x], axis=1)
    y = lax.conv_general_dilated(xp, w[:, None, :].astype(x.dtype), window_strides=(1,), padding='VALID',
                                 dimension_numbers=('NWC', 'WIO', 'NWC'), feature_group_count=x.shape[-1])
    return y, xp[:, xp.shape[1] - (kw - 1):]


def wkv7_scan(r, w, k, v, kk, a, s0):
    def step(S, inp):
        r_t, w_t, k_t, v_t, kk_t, a_t = inp
        sa = jnp.einsum('bhij,bhj->bhi', S, -kk_t)
        S = (S * w_t[:, :, None, :] + sa[..., None] * (kk_t * a_t)[:, :, None, :]
             + v_t[..., None] * k_t[:, :, None, :])
        o = jnp.einsum('bhij,bhj->bhi', S, r_t)
        return S, o
    xs = tuple(jnp.moveaxis(t, 1, 0) for t in (r, w, k, v, kk, a))
    s_fin, o = lax.scan(step, s0, xs)
    return jnp.moveaxis(o, 0, 1), s_fin


def trunk(x, p, sc0, shift0, wkv0, cf0, W):
    f32 = jnp.float32
    bsz, T, _ = x.shape
    h = x
    v_first = None
    new_sc, new_shift, new_wkv, new_cf = [], [], [], []
    for i in range(DEPTH):
        xn = rmsnorm(h, W['norm1_g'][i])
        z = xn @ W['w_in'][i]
        z_sc, z_rk, z_cf, z_gate = jnp.split(z, IN_SPLITS, axis=-1)

        b_sc, c_sc, u_sc = jnp.split(z_sc, 3, axis=-1)
        conv_a, sc_buf = causal_dwconv(c_sc * u_sc, sc0[i], W['sc_conv_w'][i])
        y_a = (b_sc * conv_a) @ W['sc_w_out'][i]
        new_sc.append(sc_buf.astype(sc0.dtype))

        z_prev = jnp.concatenate([shift0[i][:, None].astype(z_rk.dtype), z_rk[:, :-1]], axis=1)
        zm = z_rk + (z_prev - z_rk) * W['rk_mu'][i]
        new_shift.append(z_rk[:, -1].astype(shift0.dtype))
        r, k, v, w_lo, a_lo, v_lo, g_lo = jnp.split(zm, RK_SPLITS, axis=-1)
        w_log = -jax.nn.softplus(-(W['rk_w0'][i] + jnp.tanh(w_lo) @ W['rk_w2'][i])) - 0.5
        decay = jnp.exp(-jnp.exp(w_log.astype(f32)))
        a = jax.nn.sigmoid(W['rk_a0'][i] + a_lo @ W['rk_a2'][i])
        if i == 0:
            v_first = v
        else:
            v = v + (v_first - v) * jax.nn.sigmoid(W['rk_v0'][i] + v_lo @ W['rk_v2'][i])
        g = jax.nn.sigmoid(g_lo) @ W['rk_g2'][i]
        heads = lambda t: t.reshape(bsz, T, RK_HEADS, RK_HEAD_DIM).astype(f32)
        kk = heads(k * W['rk_k_k'][i])
        kk = kk / jnp.maximum(jnp.sqrt(jnp.sum(kk * kk, axis=-1, keepdims=True)), 1e-12)
        k = k * (1.0 + (a - 1.0) * W['rk_k_a'][i])
        rh, kh, vh, ah, wh = heads(r), heads(k), heads(v), heads(a), heads(decay)
        o, s_fin = wkv7_scan(rh, wh, kh, vh, kk, ah, wkv0[i].astype(f32))
        new_wkv.append(s_fin.astype(wkv0.dtype))
        o = layernorm(o, W['rk_ln_g'][i].reshape(RK_HEADS, RK_HEAD_DIM),
                      W['rk_ln_b'][i].reshape(RK_HEADS, RK_HEAD_DIM), GN_EPS)
        o = o + jnp.sum(rh * kh * W['rk_r_k'][i].astype(f32), axis=-1, keepdims=True) * vh
        y_b = (o.reshape(bsz, T, D_RK).astype(x.dtype) * g) @ W['rk_w_out'][i]

        zc = z_cf + W['cf_b_in'][i]
        glu = zc[..., :D_CF] * jax.nn.sigmoid(zc[..., D_CF:])
        conv_c, cf_buf = causal_dwconv(glu, cf0[i], W['cf_dw_w'][i])
        new_cf.append(cf_buf.astype(cf0.dtype))
        c = jax.nn.silu(layernorm(conv_c + W['cf_dw_b'][i], W['cf_ln_g'][i], W['cf_ln_b'][i], LN_EPS))
        y_c = c @ W['cf_w_out'][i] + W['cf_b_out'][i]

        g_a, g_b, g_c = jnp.split(jax.nn.sigmoid(z_gate), N_BRANCH, axis=-1)
        h = h + (g_a * y_a + g_b * y_b + g_c * y_c) @ W['w_o'][i]

        hn = rmsnorm(h, W['norm2_g'][i])
        h = h + jnp.square(jax.nn.relu(hn @ W['mlp_w1'][i])) @ W['mlp_w2'][i]

        gate = jax.nn.sigmoid(rmsnorm(h, W['ple_norm_g'][i]) @ W['ple_gate_w'][i])
        h = h + gate * (p[i] @ W['ple_w'][i])
    y = rmsnorm(h, W['final_norm_g'])
    return y, jnp.stack(new_sc), jnp.stack(new_shift), jnp.stack(new_wkv), jnp.stack(new_cf)


def setup_inputs(seed: int = 0) -> dict:
    key = jax.random.key(seed)
    ks = iter(jax.random.split(key, 64))
    nrm = lambda shape, s: jax.random.normal(next(ks), shape, jnp.float32) * s
    gain = lambda shape: 1.0 + nrm(shape, 0.02)
    L = DEPTH
    return {
        'x_prompt': nrm((BATCH, SEQ, D_MODEL), 1.0),
        'x_sample': nrm((DEC_BATCH, DEC_SEQ, D_MODEL), 1.0),
        'p_prompt': nrm((DEPTH, BATCH, SEQ, D_PLE), 1.0),
        'p_sample': nrm((DEPTH, DEC_BATCH, DEC_SEQ, D_PLE), 1.0),
        'state_sconv': nrm((DEPTH, DEC_BATCH, SC_WIDTH - 1, D_SC), 1.0),
        'state_shift': nrm((DEPTH, DEC_BATCH, RK_PROJ), 1.0),
        'state_wkv': nrm((DEPTH, DEC_BATCH, RK_HEADS, RK_HEAD_DIM, RK_HEAD_DIM), 0.1),
        'state_cconv': nrm((DEPTH, DEC_BATCH, CF_WIDTH - 1, D_CF), 1.0),
        'norm1_g': gain((L, D_MODEL)),
        'w_in': nrm((L, D_MODEL, N_IN), D_MODEL ** -0.5),
        'sc_conv_w': nrm((L, SC_WIDTH, D_SC), SC_WIDTH ** -0.5),
        'sc_w_out': nrm((L, D_SC, D_MODEL), D_SC ** -0.5),
        'rk_mu': jax.random.uniform(next(ks), (L, RK_PROJ), jnp.float32),
        'rk_w0': -1.0 + nrm((L, D_RK), 0.5),
        'rk_w2': nrm((L, DECAY_LORA, D_RK), 0.1 * DECAY_LORA ** -0.5),
        'rk_a0': nrm((L, D_RK), 0.1),
        'rk_a2': nrm((L, AAA_LORA, D_RK), 0.5 * AAA_LORA ** -0.5),
        'rk_v0': 1.0 + nrm((L, D_RK), 0.1),
        'rk_v2': nrm((L, MV_LORA, D_RK), 0.5 * MV_LORA ** -0.5),
        'rk_g2': nrm((L, GATE_LORA, D_RK), GATE_LORA ** -0.5),
        'rk_k_k': 0.85 + nrm((L, D_RK), 0.05),
        'rk_k_a': 1.0 + nrm((L, D_RK), 0.05),
        'rk_r_k': nrm((L, RK_HEADS, RK_HEAD_DIM), 0.1),
        'rk_ln_g': gain((L, D_RK)),
        'rk_ln_b': nrm((L, D_RK), 0.02),
        'rk_w_out': nrm((L, D_RK, D_MODEL), D_RK ** -0.5),
        'cf_b_in': nrm((L, 2 * D_CF), 0.02),
        'cf_dw_w': nrm((L, CF_WIDTH, D_CF), CF_WIDTH ** -0.5),
        'cf_dw_b': nrm((L, D_CF), 0.02),
        'cf_ln_g': gain((L, D_CF)),
        'cf_ln_b': nrm((L, D_CF), 0.02),
        'cf_w_out': nrm((L, D_CF, D_MODEL), D_CF ** -0.5),
        'cf_b_out': nrm((L, D_MODEL), 0.02),
        'w_o': nrm((L, D_MODEL, D_MODEL), (N_BRANCH * D_MODEL) ** -0.5),
        'norm2_g': gain((L, D_MODEL)),
        'mlp_w1': nrm((L, D_MODEL, D_FF), D_MODEL ** -0.5),
        'mlp_w2': nrm((L, D_FF, D_MODEL), D_FF ** -0.5),
        'ple_w': nrm((L, D_PLE, D_MODEL), D_PLE ** -0.5),
        'ple_gate_w': nrm((L, D_MODEL, D_MODEL), D_MODEL ** -0.5),
        'ple_norm_g': gain((L, D_MODEL)),
        'final_norm_g': gain((D_MODEL,)),
    }


def reference(x_prompt, x_sample, p_prompt, p_sample, state_sconv, state_shift, state_wkv, state_cconv,
              norm1_g, w_in, sc_conv_w, sc_w_out, rk_mu, rk_w0, rk_w2, rk_a0, rk_a2, rk_v0, rk_v2, rk_g2,
              rk_k_k, rk_k_a, rk_r_k, rk_ln_g, rk_ln_b, rk_w_out, cf_b_in, cf_dw_w, cf_dw_b, cf_ln_g, cf_ln_b,
              cf_w_out, cf_b_out, w_o, norm2_g, mlp_w1, mlp_w2, ple_w, ple_gate_w, ple_norm_g, final_norm_g):
    W = dict(norm1_g=norm1_g, w_in=w_in, sc_conv_w=sc_conv_w, sc_w_out=sc_w_out, rk_mu=rk_mu, rk_w0=rk_w0,
             rk_w2=rk_w2, rk_a0=rk_a0, rk_a2=rk_a2, rk_v0=rk_v0, rk_v2=rk_v2, rk_g2=rk_g2, rk_k_k=rk_k_k,
             rk_k_a=rk_k_a, rk_r_k=rk_r_k, rk_ln_g=rk_ln_g, rk_ln_b=rk_ln_b, rk_w_out=rk_w_out, cf_b_in=cf_b_in,
             cf_dw_w=cf_dw_w, cf_dw_b=cf_dw_b, cf_ln_g=cf_ln_g, cf_ln_b=cf_ln_b, cf_w_out=cf_w_out,
             cf_b_out=cf_b_out, w_o=w_o, norm2_g=norm2_g, mlp_w1=mlp_w1, mlp_w2=mlp_w2, ple_w=ple_w,
             ple_gate_w=ple_gate_w, ple_norm_g=ple_norm_g, final_norm_g=final_norm_g)
    sc0 = jnp.zeros((DEPTH, BATCH, SC_WIDTH - 1, D_SC), state_sconv.dtype)
    sh0 = jnp.zeros((DEPTH, BATCH, RK_PROJ), state_shift.dtype)
    wkv0 = jnp.zeros((DEPTH, BATCH, RK_HEADS, RK_HEAD_DIM, RK_HEAD_DIM), state_wkv.dtype)
    cf0 = jnp.zeros((DEPTH, BATCH, CF_WIDTH - 1, D_CF), state_cconv.dtype)
    y_prompt, sc_p, sh_p, wkv_p, cf_p = trunk(x_prompt, p_prompt, sc0, sh0, wkv0, cf0, W)
    y_sample, sc_s, sh_s, wkv_s, cf_s = trunk(x_sample, p_sample, state_sconv, state_shift, state_wkv,
                                              state_cconv, W)
    return (y_prompt, y_sample, sc_p, sh_p, wkv_p, cf_p, sc_s, sh_s, wkv_s, cf_s)
```

```python
from contextlib import ExitStack
import numpy as np
import concourse.bass as bass
import concourse.mybir as mybir
from concourse.bass_utils import run_bass_kernel_spmd

F32 = mybir.dt.float32
BF16 = mybir.dt.bfloat16
AF = mybir.ActivationFunctionType
ALU = mybir.AluOpType

D = 1024
KT = 8
DSC = 512
DCF = 512
DFF = 4096
DPLE = 256
NIN = 8992
RKP = 3360
H = 16
NCORES = 8
SC_OFF, RK_OFF, CF_OFF, GATE_OFF = 0, 1536, 4896, 5920
R_OFF, K_OFF, V_OFF, WLO_OFF, ALO_OFF, VLO_OFF, GLO_OFF = 0, 1024, 2048, 3072, 3136, 3200, 3232
RMS_EPS, LN_EPS, GN_EPS = 1e-6, 1e-5, 64e-5
EXPM05 = float(np.exp(-0.5))
EPOCH = 30000
import os
SCAN_SEQ = os.environ.get('SCAN_SEQ', '0') == '1'
NBANKROT = int(os.environ.get('NBANKROT', '6'))
PSUM_EXCL_READ = os.environ.get('PSUM_EXCL_READ', '1') == '1'
POOL_ELEM = os.environ.get('POOL_ELEM', '0') == '1'
CHAIN_PULL = int(os.environ.get('CHAIN_PULL', '2'))

SH_GROUPS = [(i * 128, 128) for i in range(24)] + [(WLO_OFF, 64), (ALO_OFF, 64), (VLO_OFF, 32), (GLO_OFF, 128)]
NSHG = len(SH_GROUPS)

PV_SPEC = [
    ("norm1_g", 8), ("sc_conv_w", 12), ("mu_rkv", 24), ("mu_wlo", 1), ("mu_alo", 1), ("mu_vlo", 1), ("mu_glo", 1),
    ("rk_w0", 8), ("rk_a0", 8), ("rk_v0", 8), ("rk_k_k", 8), ("rk_k_a", 8), ("rk_r_k", 8), ("rk_ln_g", 8),
    ("rk_ln_b", 8), ("cf_b_in", 8), ("cf_dw_w", 124), ("cf_dw_b", 4), ("cf_ln_g", 4), ("cf_ln_b", 4),
    ("cf_b_out", 8), ("norm2_g", 8), ("ple_norm_g", 8), ("omka", 8),
]
PV_OFF = {}
_o = 0
for _k, _n in PV_SPEC:
    PV_OFF[_k] = _o
    _o += _n
NPV = _o


class Dep:
    __slots__ = ("w", "r")

    def __init__(self):
        self.w = None
        self.r = {}


class DSem:
    def __init__(self, sem, glob=False):
        self.sem = sem
        self.val = 0
        self.glob = glob


class Tile:
    def __init__(self, kb, handle, P, F, dtype):
        self.kb = kb
        self.t = handle
        self.P = P
        self.F = F
        self.dtype = dtype
        self.regs = {}
        self.psum = False
        self.glob = False

    def v(self, off=0, dims=None, p0=0, np_=None):
        if np_ is None:
            np_ = self.P - p0
        if dims is None:
            dims = [[1, self.F - off]]
        ext = 1
        for s, c in dims:
            ext += abs(s) * (c - 1)
        ap = bass.AP(self.t, p0 * self.F + off, [[self.F, np_]] + [list(d) for d in dims])
        return View(self, ap, off, off + ext)


class View:
    __slots__ = ("tile", "ap", "c0", "c1")

    def __init__(self, tile, ap, c0, c1):
        self.tile = tile
        self.ap = ap
        self.c0 = c0
        self.c1 = c1


class KB:
    ENG = ("pe", "act", "dve", "pool", "sp")

    def __init__(self, nc):
        self.nc = nc
        self.eng = {"pe": nc.tensor, "act": nc.scalar, "dve": nc.vector, "pool": nc.gpsimd, "sp": nc.sync}
        self.cnt = {e: 0 for e in self.ENG}
        self.sems = {e: [] for e in self.ENG}
        self.known = {e: {} for e in self.ENG}
        self.stack = ExitStack()
        self.dsems = []
        self.dsem_by_key = {}
        self.nuid = 0
        self.rr = 0
        self.pending = {}

    def uid(self, s):
        self.nuid += 1
        return "%s_%d" % (s, self.nuid)

    def sb(self, stack, name, P, F, dtype):
        h = stack.enter_context(self.nc.sbuf_tensor(self.uid(name), [P, F], dtype))
        return Tile(self, h, P, F, dtype)

    def ps(self, stack, name, F, dtype):
        h = stack.enter_context(self.nc.psum_tensor(self.uid(name), [128, F], dtype))
        t = Tile(self, h, 128, F, dtype)
        t.psum = True
        return t

    def new_dsem(self, name):
        if name in self.dsem_by_key:
            return self.dsem_by_key[name]
        s = self.stack.enter_context(self.nc.semaphore(self.uid(name)))
        d = DSem(s, glob=name.startswith(("wsem", "wback")))
        self.dsems.append(d)
        self.dsem_by_key[name] = d
        return d

    def _sem(self, e, ep):
        while len(self.sems[e]) <= ep:
            self.sems[e].append(self.stack.enter_context(self.nc.semaphore(self.uid("s" + e))))
        return self.sems[e][ep]

    def _deps(self, view):
        t = view.tile
        key = (0, t.F) if t.psum else (view.c0, view.c1)
        own = t.regs.get(key)
        if own is None:
            own = t.regs[key] = Dep()
        over = [d for (a, b), d in t.regs.items() if a < view.c1 and view.c0 < b]
        return own, over

    def _wait_tok(self, e, tok):
        if tok[0] == "e":
            _, f, n = tok
            if self.known[e].get(f, 0) >= n:
                return
            self.known[e][f] = n
            ep = (n - 1) // EPOCH
            self.eng[e].wait_ge(self._sem(f, ep), (n - 1) % EPOCH + 1)
        else:
            _, ds, val = tok
            key = id(ds)
            if self.known[e].get(key, 0) >= val:
                return
            self.known[e][key] = val
            self.eng[e].wait_ge(ds.sem, val)

    def _hazards(self, e, rd, wr):
        toks = []
        rdo, wro = [], []
        for v in rd:
            own, over = self._deps(v)
            rdo.append(own)
            for d in over:
                if d.w is not None:
                    toks.append(d.w)
                if v.tile.psum and PSUM_EXCL_READ:
                    for tk in d.r.values():
                        if not (tk[0] == "e" and tk[1] == e):
                            toks.append(tk)
        for v in wr:
            own, over = self._deps(v)
            wro.append(own)
            for d in over:
                if d.w is not None and not (e == "pe" and d.w[0] == "e" and d.w[1] == e):
                    toks.append(d.w)
                for tk in d.r.values():
                    toks.append(tk)
        for tk in toks:
            self._wait_tok(e, tk)
        return rdo, wro

    def op(self, e, fn, rd=(), wr=()):
        rdo, wro = self._hazards(e, rd, wr)
        ins = fn(self.eng[e])
        n = self.cnt[e] = self.cnt[e] + 1
        ins.then_inc(self._sem(e, (n - 1) // EPOCH), 1)
        tok = ("e", e, n)
        for d in rdo:
            d.r[e] = tok
        for d in wro:
            d.w = tok
            d.r = {}
        return ins

    def dma(self, q, out_ap, in_ap, ds, rd=(), wr=()):
        if isinstance(ds, str):
            ds = self.new_dsem(ds)
        rdo, wro = self._hazards(q, rd, wr)
        ins = self.eng[q].dma_start(out=out_ap, in_=in_ap)
        ds.val += 16
        ins.then_inc(ds.sem, 16)
        tok = ("d", ds, ds.val)
        for d in rdo:
            d.r[("d", id(ds))] = tok
        for d in wro:
            d.w = tok
            d.r = {}
        return ins

    def barrier(self):
        for e in self.ENG:
            for f in self.ENG:
                if f != e and self.cnt[f] > 0:
                    self._wait_tok(e, ("e", f, self.cnt[f]))
            for ds in self.dsems:
                if ds.val > 0:
                    self._wait_tok(e, ("d", ds, ds.val))

    def mm(self, out, lhsT, rhs, start=True, stop=True):
        self.op("pe", lambda e: e.matmul(out.ap, lhsT.ap, rhs.ap, start=start, stop=stop), rd=[lhsT, rhs], wr=[out])

    def tr(self, out, in_, ident):
        self.op("pe", lambda e: e.transpose(out.ap, in_.ap, ident.ap), rd=[in_, ident], wr=[out])

    def act(self, out, in_, func, bias=None, scale=1.0):
        rd = [in_]
        kw = {}
        if isinstance(bias, View):
            rd.append(bias)
            kw["bias"] = bias.ap
        elif bias is not None:
            kw["bias"] = bias
        if isinstance(scale, View):
            rd.append(scale)
            kw["scale"] = scale.ap
        else:
            kw["scale"] = scale
        if func == AF.Copy and (isinstance(bias, View) or isinstance(scale, View)):
            func = AF.Identity
        self.op("act", lambda e: e.activation(out=out.ap, in_=in_.ap, func=func, **kw), rd=rd, wr=[out])

    def tt(self, eng, out, a, b, op):
        self.op(eng, lambda e: e.tensor_tensor(out=out.ap, in0=a.ap, in1=b.ap, op=op), rd=[a, b], wr=[out])

    def ts(self, eng, out, a, s1, op0, s2=None, op1=None):
        rd = [a]
        a1 = s1
        a2 = s2
        if isinstance(s1, View):
            rd.append(s1)
            a1 = s1.ap
        if isinstance(s2, View):
            rd.append(s2)
            a2 = s2.ap
        if op1 is None:
            self.op(eng, lambda e: e.tensor_scalar(out=out.ap, in0=a.ap, scalar1=a1, scalar2=None, op0=op0), rd=rd, wr=[out])
        else:
            self.op(eng, lambda e: e.tensor_scalar(out=out.ap, in0=a.ap, scalar1=a1, scalar2=a2, op0=op0, op1=op1), rd=rd, wr=[out])

    def stt(self, out, a, s, b, op0, op1):
        rd = [a, b]
        sa = s
        if isinstance(s, View):
            rd.append(s)
            sa = s.ap
        self.op("dve", lambda e: e.scalar_tensor_tensor(out=out.ap, in0=a.ap, scalar=sa, in1=b.ap, op0=op0, op1=op1), rd=rd, wr=[out])

    def copy(self, eng, out, in_):
        if eng == "act":
            self.act(out, in_, AF.Copy)
        else:
            self.op(eng, lambda e: e.tensor_copy(out=out.ap, in_=in_.ap), rd=[in_], wr=[out])

    def evac(self, out, in_):
        self.rr ^= 1
        self.copy("act" if self.rr else "dve", out, in_)

    def memset(self, eng, out, val):
        self.op(eng, lambda e: e.memset(out.ap, val), wr=[out])


class Cfg:
    def __init__(self, L=4, SEQ=2048, TP=512):
        self.L = L
        self.SEQ = SEQ
        self.TP = TP
        self.NPP = SEQ // TP


def build(cfg):
    L, SEQ, TP = cfg.L, cfg.SEQ, cfg.TP
    nc = bass.Bass("TRN2", target_bir_lowering=False)
    kb = KB(nc)
    dram = {}

    def din(name, shape):
        dram[name] = nc.dram_tensor(name, list(shape), F32, kind="ExternalInput").ap()
        return dram[name]

    def dout(name, shape):
        dram[name] = nc.dram_tensor(name, list(shape), F32, kind="ExternalOutput").ap()
        return dram[name]

    xp = din("xp", [SEQ, D])
    xs = din("xs", [64, D])
    pp = din("pp", [L, SEQ, DPLE])
    psm = din("ps", [L, 64, DPLE])
    ssc = din("ssc", [L, 32, DSC])
    ssh = din("ssh", [L, 16, RKP])
    swkv = din("swkv", [L, 16, H, 64, 64])
    scf = din("scf", [L, 480, DCF])
    W = {}
    for name, shape in [
        ("norm1_g", [L, D]), ("w_in", [L, D, NIN]), ("sc_conv_w", [L, 3, DSC]), ("sc_w_out", [L, DSC, D]),
        ("rk_mu", [L, RKP]), ("rk_w0", [L, D]), ("rk_w2", [L, 64, D]), ("rk_a0", [L, D]), ("rk_a2", [L, 64, D]),
        ("rk_v0", [L, D]), ("rk_v2", [L, 32, D]), ("rk_g2", [L, 128, D]), ("rk_k_k", [L, D]), ("rk_k_a", [L, D]),
        ("rk_r_k", [L, D]), ("rk_ln_g", [L, D]), ("rk_ln_b", [L, D]), ("rk_w_out", [L, D, D]),
        ("cf_b_in", [L, 2 * DCF]), ("cf_dw_w", [L, 31, DCF]), ("cf_dw_b", [L, DCF]), ("cf_ln_g", [L, DCF]),
        ("cf_ln_b", [L, DCF]), ("cf_w_out", [L, DCF, D]), ("cf_b_out", [L, D]), ("w_o", [L, D, D]),
        ("norm2_g", [L, D]), ("mlp_w1", [L, D, DFF]), ("mlp_w2", [L, DFF, D]), ("ple_w", [L, DPLE, D]),
        ("ple_gate_w", [L, D, D]), ("ple_norm_g", [L, D]), ("final_norm_g", [1, D]),
    ]:
        W[name] = din(name, shape)
    yp = dout("yp", [SEQ, D])
    ys = dout("ys", [64, D])
    o_scp = dout("o_scp", [L, 2, DSC])
    o_shp = dout("o_shp", [L, 1, RKP])
    o_wkvp = dout("o_wkvp", [L, H, 64, 64])
    o_cfp = dout("o_cfp", [L, 30, DCF])
    o_scs = dout("o_scs", [L, 32, DSC])
    o_shs = dout("o_shs", [L, 16, RKP])
    o_wkvs = dout("o_wkvs", [L, 16, H, 64, 64])
    o_cfs = dout("o_cfs", [L, 480, DCF])

    G = kb.stack
    with G:
        PB = [kb.ps(G, "pb%d" % i, 512, F32) for i in range(6)]
        BFA = kb.ps(G, "bfa", 1024, BF16)
        BFB = kb.ps(G, "bfb", 1024, BF16)
        BFS = [BFA, BFB]
        state = {"pbi": 0}

        def bank():
            if state.get("force") is not None:
                return state["force"]
            state["pbi"] = (state["pbi"] + 1) % NBANKROT
            return PB[state["pbi"]]

        identF = kb.sb(G, "identF", 128, 128, F32)
        identB = kb.sb(G, "identB", 128, 128, BF16)
        onesB = kb.sb(G, "onesB", 128, 128, BF16)
        onesF = kb.sb(G, "onesF", 128, 128, F32)
        blk1 = kb.sb(G, "blk1", 128, 128, BF16)
        hm = kb.sb(G, "hm", 128, 2, F32)
        CON = kb.sb(G, "con", 128, 8, F32)
        kb.memset("pool", identF.v(), 1.0)
        kb.op("pool", lambda e: e.affine_select(out=identF.v().ap, in_=identF.v().ap, pattern=[[1, 128]],
                                                compare_op=ALU.is_equal, fill=0.0, base=0, channel_multiplier=-1),
              rd=[identF.v()], wr=[identF.v()])
        kb.copy("dve", identB.v(), identF.v())
        kb.memset("pool", onesB.v(), 1.0)
        kb.memset("pool", onesF.v(), 1.0)
        kb.memset("pool", blk1.v(), 0.0)
        kb.memset("pool", blk1.v(0, [[1, 64]], 0, 64), 1.0)
        kb.memset("pool", blk1.v(64, [[1, 64]], 64, 64), 1.0)
        kb.memset("pool", hm.v(), 0.0)
        kb.memset("pool", hm.v(0, [[1, 1]], 0, 64), 1.0)
        kb.memset("pool", hm.v(1, [[1, 1]], 64, 64), 1.0)
        for i, val in enumerate([0.0, 1.0, RMS_EPS, LN_EPS, GN_EPS]):
            kb.memset("pool", CON.v(i, [[1, 1]]), val)

        def con(i):
            return CON.v(i, [[1, 1]])

        def make_masks(GS):
            NB = 2 * GS
            mU = kb.sb(G, "mU%d" % GS, NB, 2 * NB, F32)
            mL = kb.sb(G, "mL%d" % GS, NB, NB, F32)
            kb.memset("pool", mU.v(), 1.0)
            kb.memset("pool", mL.v(), 1.0)
            kb.op("pool", lambda e: e.affine_select(out=mU.v(0, [[1, NB]]).ap, in_=mU.v(0, [[1, NB]]).ap, pattern=[[1, NB]],
                                                    compare_op=ALU.is_gt, fill=0.0, base=0, channel_multiplier=-1),
                  rd=[mU.v()], wr=[mU.v()])
            kb.op("pool", lambda e: e.affine_select(out=mU.v(NB, [[1, NB]]).ap, in_=mU.v(NB, [[1, NB]]).ap, pattern=[[1, NB]],
                                                    compare_op=ALU.is_ge, fill=0.0, base=0, channel_multiplier=-1),
                  rd=[mU.v()], wr=[mU.v()])
            kb.op("pool", lambda e: e.affine_select(out=mL.v().ap, in_=mL.v().ap, pattern=[[-1, NB]],
                                                    compare_op=ALU.is_gt, fill=0.0, base=0, channel_multiplier=1),
                  rd=[mL.v()], wr=[mL.v()])
            kb.memset("pool", mU.v(GS, [[1, GS]], 0, GS), 0.0)
            kb.memset("pool", mU.v(NB + GS, [[1, GS]], 0, GS), 0.0)
            kb.memset("pool", mL.v(0, [[1, GS]], GS, GS), 0.0)
            return mU, mL

        masks = {64: make_masks(64), 32: make_masks(32)}

        PV = kb.sb(G, "pv", 128, L * NPV, F32)
        finalg = kb.sb(G, "finalg", 128, 8, F32)

        def pv(l, key, i=0):
            return PV.v(l * NPV + PV_OFF[key] + i, [[1, 1]])

        with ExitStack() as st:
            rows = [kb.sb(st, "prow%d" % i, 128, 128, F32) for i in range(3)]
            for l in range(L):
                for rt in rows:
                    kb.memset("dve", rt.v(), 0.0)
                srcs = {
                    "norm1_g": W["norm1_g"][l].rearrange("(r c) -> r c", c=128),
                    "sc_conv_w": W["sc_conv_w"][l].rearrange("k (t c) -> (k t) c", c=128),
                    "mu_rkv": W["rk_mu"][l, 0:3072].rearrange("(r c) -> r c", c=128),
                    "mu_wlo": W["rk_mu"][l:l + 1, WLO_OFF:WLO_OFF + 64],
                    "mu_alo": W["rk_mu"][l:l + 1, ALO_OFF:ALO_OFF + 64],
                    "mu_vlo": W["rk_mu"][l:l + 1, VLO_OFF:VLO_OFF + 32],
                    "mu_glo": W["rk_mu"][l:l + 1, GLO_OFF:GLO_OFF + 128],
                    "cf_dw_w": W["cf_dw_w"][l].rearrange("k (t c) -> (k t) c", c=128),
                }
                for key in ("rk_w0", "rk_a0", "rk_v0", "rk_k_k", "rk_k_a", "rk_r_k", "rk_ln_g", "rk_ln_b", "cf_b_in",
                            "cf_dw_b", "cf_ln_g", "cf_ln_b", "cf_b_out", "norm2_g", "ple_norm_g"):
                    srcs[key] = W[key][l].rearrange("(r c) -> r c", c=128)
                r0 = 0
                for key, n in PV_SPEC:
                    if key == "omka":
                        continue
                    src = srcs[key]
                    done = 0
                    while done < n:
                        ti, ro = divmod(r0 + done, 128)
                        m = min(n - done, 128 - ro)
                        ncol = src.shape[1]
                        kb.dma("sp", rows[ti].v(0, [[1, ncol]], ro, m).ap, src[done:done + m, :], "prow%d" % ti,
                               wr=[rows[ti].v()])
                        done += m
                    r0 += n
                nrow = r0
                for ti in range(3):
                    m = min(128, nrow - ti * 128)
                    if m <= 0:
                        break
                    pb = bank()
                    kb.tr(pb.v(0, [[1, m]]), rows[ti].v(0, [[1, 128]], 0, m), identF.v(0, [[1, m]], 0, m))
                    kb.copy("dve", PV.v(l * NPV + ti * 128, [[1, m]]), pb.v(0, [[1, m]]))
                kb.ts("dve", PV.v(l * NPV + PV_OFF["omka"], [[1, 8]]), PV.v(l * NPV + PV_OFF["rk_k_a"], [[1, 8]]),
                      -1.0, ALU.mult, 1.0, ALU.add)
            kb.memset("dve", rows[0].v(), 0.0)
            kb.dma("sp", rows[0].v(0, [[1, 128]], 0, 8).ap, W["final_norm_g"][0].rearrange("(r c) -> r c", c=128), "prow0",
                   wr=[rows[0].v()])
            pb = bank()
            kb.tr(pb.v(0, [[1, 8]]), rows[0].v(0, [[1, 128]], 0, 8), identF.v(0, [[1, 8]], 0, 8))
            kb.copy("dve", finalg.v(), pb.v(0, [[1, 8]]))
            kb.barrier()

        NSLOT = 4
        SLOT_E = 4096
        wslots = [kb.sb(G, "wslot%d" % i, 128, SLOT_E, BF16) for i in range(NSLOT)]
        for t_ in wslots:
            t_.glob = True
        wsems = [kb.new_dsem("wsem%d" % i) for i in range(NSLOT)]
        wstate = {"i": 0, "n": 0, "first": True}

        scratch = {}

        def wload(mat2d, k0, nk, c0, ncols, krows=128):
            assert nk * ncols <= SLOT_E and krows == 128
            i = wstate["i"] = (wstate["i"] + 1) % NSLOT
            n = wstate["n"] = wstate["n"] + 1
            sl = wslots[i]
            dst = sl.v(0, [[ncols, nk], [1, ncols]])
            flat = sl.v(0, [[1, nk * ncols]])
            if wstate["first"]:
                src = mat2d.rearrange("(kt p) n -> p kt n", p=128)[:, k0:k0 + nk, c0:c0 + ncols]
                kb.dma("pool", dst.ap, src, wsems[i], wr=[sl.v()])
                scratch[n] = nc.dram_tensor("wscr%d" % n, [128, nk * ncols], BF16, kind="Internal").ap()
                kb.dma("sp", scratch[n], flat.ap, "wback%d" % i, rd=[sl.v()])
            else:
                kb.dma("sp", flat.ap, scratch[n], "wsemH%d" % i, wr=[sl.v()])

            def acc(kt, m0, msz):
                return sl.v(kt * ncols + m0, [[1, msz]], 0, krows)
            return acc

        HSTP = kb.sb(G, "hstp", 128, L * 8 * 64, F32)
        CSC = kb.sb(G, "csc", 128, L * 4 * 2, F32)
        CCF = kb.sb(G, "ccf", 128, L * 4 * 30, F32)
        CSH = kb.sb(G, "csh", 128, L * NSHG, F32)
        for t_ in (HSTP, CSC, CCF, CSH):
            kb.memset("dve", t_.v(), 0.0)

        def run_pass(kind, t0):
            sample = kind == "S"
            nseq, T = (16, 4) if sample else (1, TP)
            NT = nseq * T
            GS = 32 if sample else 64
            NV = 4 if sample else 64
            NB = 2 * GS
            LEV = 2 if sample else 6
            nu = 16 if sample else TP // 64
            mU, mL = masks[GS]
            last_prompt = (not sample) and (t0 + TP == SEQ)
            EP = "dve" if (wstate["first"] or not POOL_ELEM) else "pool"
            x_rows = xs if sample else xp[t0:t0 + TP]
            y_rows = ys if sample else yp[t0:t0 + TP]
            with ExitStack() as PS:
                h = kb.sb(PS, "h", 128, 8 * NT, F32)
                xn = kb.sb(PS, "xn", 128, 8 * NT, BF16)
                vf = kb.sb(PS, "vf", 128, 8 * NT, F32)
                rstd = kb.sb(PS, "rstd", 128, NT, F32)
                rm = kb.sb(PS, "rm", 128, NT, F32)
                cptbox = {}
                kb.memset("dve", rm.v(), 1.0)
                kb.memset("dve", rm.v(0, [[NV, nu], [1, 1]]), 0.0)
                HS = kb.sb(PS, "hs", 128, 8 * 16 * 64, F32) if sample else None

                def hk(t_, kt, c0=0, n=None):
                    return t_.v(kt * NT + c0, [[1, NT - c0 if n is None else n]])

                with ExitStack() as st:
                    xst = [kb.sb(st, "xst%d" % i, 128, D, F32) for i in range(2)]
                    for gi, r0 in enumerate(range(0, NT, 128)):
                        nr = min(128, NT - r0)
                        xt = xst[gi % 2]
                        kb.dma("sp", xt.v(0, [[1, D]], 0, nr).ap, x_rows[r0:r0 + nr, :], "xst%d" % (gi % 2), wr=[xt.v()])
                        for half in range(2):
                            pb = bank()
                            for j in range(4):
                                kt = half * 4 + j
                                kb.tr(pb.v(j * 128, [[1, nr]]), xt.v(kt * 128, [[1, 128]], 0, nr), identF.v(0, [[1, nr]], 0, nr))
                            kb.evac(h.v(half * 4 * NT + r0, [[NT, 4], [1, nr]]), pb.v(0, [[128, 4], [1, nr]]))
                    kb.barrier()

                def rmsnorm_to(dst, gkey, l, fin=False):
                    kb.act(dst.v(), h.v(), AF.Square)
                    pb = bank()
                    for kt in range(8):
                        kb.mm(pb.v(0, [[1, NT]]), onesB.v(), hk(dst, kt), start=(kt == 0), stop=(kt == 7))
                    kb.act(rstd.v(), pb.v(0, [[1, NT]]), AF.Sqrt, bias=con(2), scale=1.0 / D)
                    kb.op("dve", lambda e: e.reciprocal(out=rstd.v().ap, in_=rstd.v().ap), rd=[rstd.v()], wr=[rstd.v()])
                    for kt in range(8):
                        g = finalg.v(kt, [[1, 1]]) if fin else pv(l, gkey, kt)
                        kb.stt(hk(dst, kt), hk(h, kt), g, rstd.v(), ALU.mult, ALU.mult)

                def proj(out_ps, wacc, m0, msz, nk, rhs_t, start=True, stop=True):
                    for kt in range(nk):
                        kb.mm(out_ps, wacc(kt, m0, msz), hk(rhs_t, kt), start=(start and kt == 0), stop=(stop and kt == nk - 1))

                def fm_rows_out(src_views, dst_rows_ap, nrows, stg):
                    pb = bank()
                    for ct, sv in enumerate(src_views):
                        kb.copy("dve", cptbox["t"].v(ct * 128, [[1, nrows]]), sv)
                        kb.tr(pb.v(ct * 128, [[1, 128]], 0, nrows), cptbox["t"].v(ct * 128, [[1, nrows]]), identF.v())
                    kb.evac(stg.v(0, [[1, 512]], 0, nrows), pb.v(0, [[1, 512]], 0, nrows))
                    kb.dma("sp", dst_rows_ap, stg.v(0, [[1, 512]], 0, nrows).ap, "rowstg", rd=[stg.v()])

                for l in range(L):
                    w_in = W["w_in"][l]
                    with ExitStack() as BR:
                        rows_stg = kb.sb(BR, "rowstg", 128, 512, F32)
                        NWS = 4 if sample else 1
                        wk_stg = [rows_stg] + [kb.sb(BR, "wkstg%d" % i_, 64, 512, F32) for i_ in range(1, NWS)]
                        wk_key = ["rowstg"] + ["wkstg%d" % i_ for i_ in range(1, NWS)]
                        og = kb.sb(BR, "og", 128, 8 * NT, BF16)
                        rmsnorm_to(xn, "norm1_g", l)

                        with ExitStack() as st:
                            ZW = nseq * (1 + T)
                            Z = kb.sb(st, "Z", 128, ZW, F32)
                            ZL = kb.sb(st, "ZL", 128, NSHG * nseq, F32)
                            lora = {k_: kb.sb(st, "lo" + k_, 128, NT, BF16) for k_ in ("w", "a", "v", "g")}
                            names32 = ["Dd", "R", "Kx", "E", "Lc", "Ep", "Emp", "A", "SV", "T2"]
                            B32 = {n_: kb.sb(st, n_, 128, NT, F32) for n_ in names32}
                            for alias, base_ in (("Lp", "Dd"), ("KP", "E"), ("KK", "Lc"), ("SD", "SV"), ("Tt", "SV")):
                                B32[alias] = B32[base_]
                            names16 = ["ksq", "at", "bt", "kt", "rt"]
                            B16 = {n_: kb.sb(st, n_, 128, NT, BF16) for n_ in names16}
                            HPS = []
                            for i_ in range(2):
                                P_ = dict(
                                    Em=kb.sb(st, "Em", 128, NT, F32), Vx=kb.sb(st, "Vx", 128, NT, F32), Gt=kb.sb(st, "Gt", 128, NT, F32),
                                    T2p=kb.sb(st, "T2p", 128, NT, F32), rkb=kb.sb(st, "rkb", 128, NT, BF16),
                                    vbP=kb.sb(st, "vbP", 128, nu * GS, BF16), ARB=kb.sb(st, "ARB", 128, nu * 2 * NB, BF16),
                                    BBt=kb.sb(st, "BBt", 128, nu * NB, BF16), KBt=kb.sb(st, "KBt", 128, nu * NB, BF16),
                                    Otok=kb.sb(st, "Otok", NB, nu * 64, F32), Osq=kb.sb(st, "Osq", 128, max(NT, nu * 64), F32),
                                    Onorm=kb.sb(st, "Onorm", NB, nu * 64, BF16), stats=kb.sb(st, "stats", NB, 4 * nu, F32))
                                for t_ in (P_["vbP"], P_["ARB"], P_["BBt"], P_["KBt"]):
                                    kb.memset("dve", t_.v(), 0.0)
                                HPS.append(P_)
                            NSETS = 4 if sample else 3
                            S1BANKS = [PB[0], PB[1], PB[2], PB[5]]
                            USETS = []
                            for si in range(NSETS):
                                USETS.append(dict(
                                    NA=kb.sb(st, "NA", NB, 2 * NB, BF16), AK=kb.sb(st, "AK", NB, 2 * NB, BF16),
                                    M0=kb.sb(st, "M0", NB, NB, BF16),
                                    SP=[kb.sb(st, "SP", NB, 3 * NB, BF16) for _ in range(2)],
                                    Vs=kb.sb(st, "Vs", NB, 64, BF16),
                                    Xb=kb.sb(st, "Xb", NB, 64, BF16), Ub=kb.sb(st, "Ub", NB, 64, BF16),
                                    BKT=kb.sb(st, "BKT", NB, 256, BF16),
                                    Ht=kb.sb(st, "Ht", 128, 64, F32),
                                    S1=S1BANKS[si], si=si,
                                ))
                            HBF = [kb.sb(st, "Hbf", 128, 64, BF16) for _ in range(2)]
                            shst = kb.sb(st, "shst", 16, 512, F32) if sample else None
                            SH = kb.sb(st, "SH", 128, NSHG * 16, F32) if sample else None

                            if sample:
                                for c0 in range(0, RKP, 512):
                                    ncol = min(512, RKP - c0)
                                    kb.dma("sp", shst.v(0, [[1, ncol]]).ap, ssh[l, :, c0:c0 + ncol], "shst", wr=[shst.v()])
                                    pb = bank()
                                    for gi, (g0, msz) in enumerate(SH_GROUPS):
                                        if c0 <= g0 < c0 + 512:
                                            kb.tr(pb.v(0, [[1, 16]], 0, msz), shst.v(g0 - c0, [[1, msz]]), identF.v(0, [[1, 16]], 0, 16))
                                            kb.evac(SH.v(gi * 16, [[1, 16]], 0, msz), pb.v(0, [[1, 16]], 0, msz))
                                for s in range(16):
                                    for hh in range(2):
                                        wi_ = (s * 2 + hh) % NWS
                                        wst_ = wk_stg[wi_]
                                        kb.dma("sp", wst_.v(0, [[64, 8], [1, 64]], 0, 64).ap,
                                               swkv[l, s, hh * 8:hh * 8 + 8].rearrange("h i j -> i h j"), wk_key[wi_], wr=[wst_.v()])
                                        pb = bank()
                                        for q in range(4):
                                            kb.tr(pb.v(q * 64, [[1, 64]]), wst_.v(q * 128, [[1, 128]], 0, 64), identF.v(0, [[1, 64]], 0, 64))
                                        kb.evac(HS.v(s * 64 + hh * 4 * 16 * 64, [[16 * 64, 4], [1, 64]]), pb.v(0, [[64, 4], [1, 64]]))

                            def zproj(gi, wacc, m0, msz, dst, mukey, mui, func=None):
                                pz = bank()
                                proj(pz.v(0, [[1, NT]], 0, msz), wacc, m0, msz, 8, xn)
                                if sample:
                                    kb.copy("dve", Z.v(0, [[1 + T, 16], [1, 1]], 0, msz), SH.v(gi * 16, [[1, 16], [1, 1]], 0, msz))
                                else:
                                    kb.copy("dve", Z.v(0, [[1, 1]], 0, msz), CSH.v(l * NSHG + gi, [[1, 1]], 0, msz))
                                cur = Z.v(1, [[1 + T, nseq], [1, T]], 0, msz)
                                prev = Z.v(0, [[1 + T, nseq], [1, T]], 0, msz)
                                kb.copy("act", cur, pz.v(0, [[T, nseq], [1, T]], 0, msz))
                                Dd = B32["Dd"].v(0, [[T, nseq], [1, T]], 0, msz)
                                kb.tt(EP, Dd, prev, cur, ALU.subtract)
                                mu = PV.v(l * NPV + PV_OFF[mukey] + mui, [[1, 1]], 0, msz)
                                if func is None:
                                    kb.stt(dst.tile.v(dst.c0, [[T, nseq], [1, T]], 0, msz), Dd, mu, cur, ALU.mult, ALU.add)
                                else:
                                    kb.stt(Dd, Dd, mu, cur, ALU.mult, ALU.add)
                                    kb.act(dst, B32["Dd"].v(0, [[1, NT]], 0, msz), func)
                                kb.copy("dve", ZL.v(gi * nseq, [[1, nseq]], 0, msz), Z.v(T, [[1 + T, nseq]], 0, msz))
                                if not sample:
                                    kb.copy("dve", CSH.v(l * NSHG + gi, [[1, 1]], 0, msz), Z.v(T, [[1, 1]], 0, msz))

                            wl = wload(w_in, 0, 8, RK_OFF + WLO_OFF, RKP - WLO_OFF)
                            zproj(24, wl, 0, 64, lora["w"].v(0, [[1, NT]], 0, 64), "mu_wlo", 0, AF.Tanh)
                            zproj(25, wl, ALO_OFF - WLO_OFF, 64, lora["a"].v(0, [[1, NT]], 0, 64), "mu_alo", 0, AF.Copy)
                            zproj(26, wl, VLO_OFF - WLO_OFF, 32, lora["v"].v(0, [[1, NT]], 0, 32), "mu_vlo", 0, AF.Copy)
                            zproj(27, wl, GLO_OFF - WLO_OFF, 128, lora["g"].v(0, [[1, NT]]), "mu_glo", 0, AF.Sigmoid)
                            lw = kb.sb(st, "lw", 128, 4 * 1024, BF16)
                            for i_, (nm_, kr) in enumerate([("rk_w2", 64), ("rk_a2", 64), ("rk_v2", 32), ("rk_g2", 128)]):
                                kb.dma("pool", lw.v(i_ * 1024, [[1, 1024]], 0, kr).ap, W[nm_][l][0:kr, :], "lw", wr=[lw.v()])
                            wrkv = {}

                            def prep_gen(hp):
                                P = HPS[hp % 2]
                                ARB, BBt, KBt, vbP = P["ARB"], P["BBt"], P["KBt"], P["vbP"]
                                if hp % 4 == 0:
                                    for nm, off in (("r", R_OFF), ("k", K_OFF), ("v", V_OFF)):
                                        wrkv[nm] = wload(w_in, 0, 8, RK_OFF + off + (hp // 4) * 512, 512)
                                R, Kx, E, Lc, Lp, Ep, Emp, A, SV, SD, KK, Tt, KP, T2 = [
                                    B32[n_].v() for n_ in ["R", "Kx", "E", "Lc", "Lp", "Ep", "Emp", "A", "SV", "SD", "KK", "Tt", "KP", "T2"]]
                                Em, Gt = P["Em"].v(), P["Gt"].v()
                                Vx = hk(vf, hp) if l == 0 else P["Vx"].v()
                                m0 = (hp % 4) * 128
                                zproj(hp, wrkv["r"], m0, 128, R, "mu_rkv", hp)
                                yield
                                zproj(8 + hp, wrkv["k"], m0, 128, Kx, "mu_rkv", 8 + hp)
                                yield
                                zproj(16 + hp, wrkv["v"], m0, 128, Vx, "mu_rkv", 16 + hp)
                                yield
                                pw = bank()
                                kb.mm(pw.v(0, [[1, NT]]), lw.v(0 * 1024 + hp * 128, [[1, 128]], 0, 64), lora["w"].v(0, [[1, NT]], 0, 64))
                                yield
                                kb.act(E, pw.v(0, [[1, NT]]), AF.Sigmoid, bias=pv(l, "rk_w0", hp))
                                yield
                                kb.op("dve", lambda e: e.tensor_tensor_scan(out=Lc.ap, data0=rm.v().ap, data1=E.ap, initial=0.0,
                                                                            op0=ALU.mult, op1=ALU.add), rd=[rm.v(), E], wr=[Lc])
                                yield
                                kb.act(Ep, Lc, AF.Exp, scale=EXPM05)
                                yield
                                kb.act(Em, Lc, AF.Exp, scale=-EXPM05)
                                yield
                                kb.tt(EP, Lp, Lc, E, ALU.subtract)
                                yield
                                kb.act(Emp, Lp, AF.Exp, scale=-EXPM05)
                                yield
                                yield
                                pa = bank()
                                kb.mm(pa.v(0, [[1, NT]]), lw.v(1 * 1024 + hp * 128, [[1, 128]], 0, 64), lora["a"].v(0, [[1, NT]], 0, 64))
                                yield
                                kb.act(A, pa.v(0, [[1, NT]]), AF.Sigmoid, bias=pv(l, "rk_a0", hp))
                                yield
                                if l > 0:
                                    pvv = bank()
                                    kb.mm(pvv.v(0, [[1, NT]]), lw.v(2 * 1024 + hp * 128, [[1, 128]], 0, 32), lora["v"].v(0, [[1, NT]], 0, 32))
                                    kb.act(SV, pvv.v(0, [[1, NT]]), AF.Sigmoid, bias=pv(l, "rk_v0", hp))
                                    kb.tt("dve", T2, hk(vf, hp), Vx, ALU.subtract)
                                    kb.tt("dve", T2, T2, SV, ALU.mult)
                                    kb.tt("dve", Vx, Vx, T2, ALU.add)
                                pgt = bank()
                                kb.mm(pgt.v(0, [[1, NT]]), lw.v(3 * 1024 + hp * 128, [[1, 128]]), lora["g"].v())
                                yield
                                kb.copy("act", Gt, pgt.v(0, [[1, NT]]))
                                yield
                                yield
                                kb.act(B16["ksq"].v(), Kx, AF.Square, scale=pv(l, "rk_k_k", hp))
                                yield
                                pn = bank()
                                kb.mm(pn.v(0, [[1, NT]]), blk1.v(), B16["ksq"].v())
                                yield
                                kb.act(SD, pn.v(0, [[1, NT]]), AF.Sqrt)
                                yield
                                kb.ts("dve", SD, SD, 1e-12, ALU.max)
                                yield
                                kb.op("dve", lambda e: e.reciprocal(out=SD.ap, in_=SD.ap), rd=[SD], wr=[SD])
                                yield
                                kb.stt(KK, Kx, pv(l, "rk_k_k", hp), SD, ALU.mult, ALU.mult)
                                yield
                                yield
                                kb.ts("dve", Tt, A, pv(l, "rk_k_a", hp), ALU.mult, pv(l, "omka", hp), ALU.add)
                                yield
                                kb.tt(EP, KP, Kx, Tt, ALU.mult)
                                yield
                                kb.stt(B16["at"].v(), KK, -1.0, Emp, ALU.mult, ALU.mult)
                                yield
                                kb.tt(EP, T2, KK, A, ALU.mult)
                                yield
                                kb.tt(EP, B16["bt"].v(), T2, Ep, ALU.mult)
                                yield
                                yield
                                kb.tt(EP, B16["kt"].v(), KP, Ep, ALU.mult)
                                yield
                                kb.tt(EP, B16["rt"].v(), R, Em, ALU.mult)
                                yield
                                kb.stt(P["rkb"].v(), R, pv(l, "rk_r_k", hp), KP, ALU.mult, ALU.mult)
                                yield
                                kb.copy("act", vbP.v(0, [[GS, nu], [1, NV]]), Vx.tile.v(Vx.c0, [[NV, nu], [1, NV]]))
                                yield
                                yield
                                for src, dstt, doff in ((B16["at"], ARB, 0), (B16["rt"], ARB, NB)):
                                    for g in range(2):
                                        kb.copy(EP if g == 0 else "act",
                                                dstt.v(doff + g * GS, [[2 * NB, nu], [1, NV]], 64 * g, 64),
                                                src.v(0, [[NV, nu], [1, NV]], 64 * g, 64))
                                for src, dstt in ((B16["bt"], BBt), (B16["kt"], KBt)):
                                    for g in range(2):
                                        kb.copy(EP if g == 0 else "act",
                                                dstt.v(g * GS, [[NB, nu], [1, NV]], 64 * g, 64),
                                                src.v(0, [[NV, nu], [1, NV]], 64 * g, 64))

                                yield

                            def units_gen(hp):
                                P = HPS[hp % 2]
                                ARB, BBt, KBt, vbP, Otok = P["ARB"], P["BBt"], P["KBt"], P["vbP"], P["Otok"]

                                def part_a(u, S):
                                    si = S["si"]
                                    S1 = S["S1"]
                                    ev = "act" if si % 2 == 0 else "dve"
                                    aB = ARB.v(u * 2 * NB, [[1, NB]])
                                    arB = ARB.v(u * 2 * NB, [[1, 2 * NB]])
                                    bB = BBt.v(u * NB, [[1, NB]])
                                    kB_ = KBt.v(u * NB, [[1, NB]])
                                    NA, AK, M0 = S["NA"], S["AK"], S["M0"]
                                    P1 = S1.v(0, [[1, 2 * NB]], 0, NB)
                                    P2 = S1.v(256, [[1, 2 * NB]], 0, NB)
                                    P3 = PB[3].v(si * 128 + (64 if sample else 0), [[1, NB]], 0, NB)
                                    kb.mm(P1, bB, arB)
                                    kb.mm(P2, kB_, arB)
                                    kb.mm(P3, aB, bB)
                                    bfo = (si % 3) * 320
                                    BF = BFS[1] if si < 3 else BFS[0]
                                    for g in range(2):
                                        kb.tr(BF.v(bfo, [[1, 64]], g * GS, GS), vbP.v(u * GS, [[1, GS]], 64 * g, 64),
                                              identB.v(64 * g, [[1, 64]], 64 * g, 64))
                                    kb.tr(BF.v(bfo + 64, [[1, 128]], 0, NB), bB, identB.v())
                                    kb.tr(BF.v(bfo + 192, [[1, 128]], 0, NB), kB_, identB.v())
                                    kb.tt("dve", NA.v(), P1, mU.v(), ALU.mult)
                                    kb.tt("dve", AK.v(), P2, mU.v(), ALU.mult)
                                    kb.tt("dve", M0.v(), P3, mL.v(), ALU.mult)
                                    kb.copy("act", S["Vs"].v(), BF.v(bfo, [[1, 64]], 0, NB))
                                    kb.copy("act", S["BKT"].v(), BF.v(bfo + 64, [[1, 256]], 0, NB))
                                    yield
                                    Nk, Mk = NA.v(0, [[1, NB]]), M0.v()
                                    SP = S["SP"][1]
                                    kb.mm(S1.v(NB, [[1, NB]], 0, NB), Mk, Nk)
                                    kb.mm(S1.v(2 * NB, [[1, NB]], 0, NB), Nk, Mk)
                                    kb.tt("dve", SP.v(0, [[1, NB]]), Nk, identB.v(0, [[1, NB]], 0, NB), ALU.add)
                                    kb.copy(ev, SP.v(NB, [[1, 2 * NB]]), S1.v(NB, [[1, 2 * NB]], 0, NB))
                                    yield
                                    for hh in range(2, LEV + 1):
                                        last = hh == LEV
                                        SN, Nk, Mk = SP.v(0, [[1, NB]]), SP.v(NB, [[1, NB]]), SP.v(2 * NB, [[1, NB]])
                                        kb.mm(S1.v(0, [[1, NB]], 0, NB), identB.v(0, [[1, NB]], 0, NB), SN, start=True, stop=False)
                                        kb.mm(S1.v(0, [[1, NB]], 0, NB), Mk, SN, start=False, stop=True)
                                        SPn = S["SP"][hh % 2]
                                        if last:
                                            kb.copy(ev, SPn.v(0, [[1, NB]]), S1.v(0, [[1, NB]], 0, NB))
                                        else:
                                            kb.mm(S1.v(NB, [[1, NB]], 0, NB), Mk, Nk)
                                            kb.mm(S1.v(2 * NB, [[1, NB]], 0, NB), Nk, Mk)
                                            kb.copy(ev, SPn.v(), S1.v(0, [[1, 3 * NB]], 0, NB))
                                        SP = SPn
                                        yield
                                    S["TT"] = SP.v(0, [[1, NB]])

                                def part_b(u, S):
                                    hoff = u * 64 + hp * 16 * 64 if sample else (l * 8 + hp) * 64
                                    Hst = (HS if sample else HSTP).v(hoff, [[1, 64]])
                                    aB = ARB.v(u * 2 * NB, [[1, NB]])
                                    rB = ARB.v(u * 2 * NB + NB, [[1, NB]])
                                    NA, AK, Vs, Xb, Ub, Ht, BKT = (S[k_] for k_ in ("NA", "AK", "Vs", "Xb", "Ub", "Ht", "BKT"))
                                    Hbf = HBF[u % 2]
                                    PBX = PB[4]
                                    qo = (u % 2) * 256
                                    if sample or u == 0:
                                        kb.copy("act", Hbf.v(), Hst)
                                    PX = PBX.v(qo, [[1, 64]], 0, NB)
                                    kb.mm(PX, aB, Hbf.v(), start=True, stop=False)
                                    kb.mm(PX, AK.v(0, [[1, NB]]), Vs.v(), start=False, stop=True)
                                    kb.copy("act", Xb.v(), PX)
                                    yield
                                    PU = PBX.v(qo + 64, [[1, 64]], 0, NB)
                                    kb.mm(PU, S["TT"], Xb.v())
                                    kb.copy("act", Ub.v(), PU)
                                    yield
                                    PH = PBX.v(qo + 128, [[1, 64]])
                                    kb.mm(PH, BKT.v(0, [[1, 128]]), Ub.v(), start=True, stop=False)
                                    kb.mm(PH, BKT.v(128, [[1, 128]]), Vs.v(), start=False, stop=True)
                                    PO = PBX.v(qo + 192, [[1, 64]], 0, NB)
                                    kb.mm(PO, rB, Hbf.v(), start=True, stop=False)
                                    kb.mm(PO, NA.v(NB, [[1, NB]]), Ub.v(), start=False, stop=False)
                                    kb.mm(PO, AK.v(NB, [[1, NB]]), Vs.v(), start=False, stop=True)
                                    kb.tt("dve", Ht.v(), PH, Hst, ALU.add)
                                    wc = P["Em"].v(u * NV + NV - 1, [[1, 1]])
                                    if not sample and u + 1 < nu:
                                        kb.act(HBF[(u + 1) % 2].v(), Ht.v(), AF.Copy, scale=wc)
                                    kb.act(Hst, Ht.v(), AF.Copy, scale=wc)
                                    kb.copy("act", Otok.v(u * 64, [[1, 64]]), PO)
                                    yield

                                a_next = 0
                                b_next = 0
                                b_fin = 0
                                a_act = []
                                a_done = set()
                                b_act = []
                                nbmax = 2 if sample else 1
                                while b_fin < nu:
                                    while len(a_act) < 2 and a_next < nu and a_next < b_fin + NSETS - len(b_act):
                                        a_act.append((a_next, part_a(a_next, USETS[a_next % NSETS])))
                                        a_next += 1
                                    for (ua, ga) in list(a_act):
                                        try:
                                            next(ga)
                                        except StopIteration:
                                            a_act.remove((ua, ga))
                                            a_done.add(ua)
                                    while len(b_act) < nbmax and b_next < nu and b_next in a_done:
                                        b_act.append(part_b(b_next, USETS[b_next % NSETS]))
                                        b_next += 1
                                    for gb in list(b_act):
                                        try:
                                            next(gb)
                                        except StopIteration:
                                            b_act.remove(gb)
                                            b_fin += 1
                                    yield

                            def post_gen(hp):
                                P = HPS[hp % 2]
                                Otok, Osq, Onorm, stats = P["Otok"], P["Osq"], P["Onorm"], P["stats"]
                                Gt, T2 = P["Gt"].v(), P["T2p"].v()
                                onf = P["Osq"].v(0, [[1, NT]])
                                Vx = hk(vf, hp) if l == 0 else P["Vx"].v()
                                s1 = stats.v(0, [[1, nu]])
                                s2 = stats.v(nu, [[1, nu]])
                                mn = stats.v(2 * nu, [[1, nu]])
                                rsd = stats.v(3 * nu, [[1, nu]])
                                O3 = Otok.v(0, [[64, nu], [1, 64]])
                                kb.op("dve", lambda e: e.tensor_reduce(out=s1.ap, in_=O3.ap, axis=mybir.AxisListType.X, op=ALU.add),
                                      rd=[O3], wr=[s1])
                                yield
                                kb.act(Osq.v(0, [[1, nu * 64]], 0, NB), Otok.v(), AF.Square)
                                yield
                                Q3 = Osq.v(0, [[64, nu], [1, 64]], 0, NB)
                                kb.op("dve", lambda e: e.tensor_reduce(out=s2.ap, in_=Q3.ap, axis=mybir.AxisListType.X, op=ALU.add),
                                      rd=[Q3], wr=[s2])
                                yield
                                kb.ts("dve", mn, s1, 1.0 / 64, ALU.mult)
                                yield
                                kb.tt("dve", s1, mn, mn, ALU.mult)
                                yield
                                kb.stt(s2, s2, 1.0 / 64, s1, ALU.mult, ALU.subtract)
                                yield
                                kb.act(rsd, s2, AF.Sqrt, bias=CON.v(4, [[1, 1]], 0, NB))
                                yield
                                kb.op("dve", lambda e: e.reciprocal(out=rsd.ap, in_=rsd.ap), rd=[rsd], wr=[rsd])
                                yield
                                yield
                                kb.tt("dve", Osq.v(0, [[64, nu], [1, 64]], 0, NB), O3, stats.v(2 * nu, [[1, nu], [0, 64]]), ALU.subtract)
                                yield
                                kb.tt("dve", Onorm.v(0, [[64, nu], [1, 64]]), Osq.v(0, [[64, nu], [1, 64]], 0, NB),
                                      stats.v(3 * nu, [[1, nu], [0, 64]]), ALU.mult)
                                yield
                                yield
                                for u in range(nu):
                                    for g in range(2):
                                        kb.tr(BFS[0].v(512 + u * GS, [[1, GS]], 64 * g, 64), Onorm.v(u * 64, [[1, 64]], g * GS, GS),
                                              identB.v(g * GS, [[1, GS]], g * GS, GS))
                                kb.act(onf.tile.v(0, [[NV, nu], [1, NV]]), BFS[0].v(512, [[GS, nu], [1, NV]]), AF.Identity,
                                       bias=pv(l, "rk_ln_b", hp), scale=pv(l, "rk_ln_g", hp))
                                yield
                                yield
                                pbn = bank()
                                kb.mm(pbn.v(0, [[1, NT]]), blk1.v(), P["rkb"].v())
                                yield
                                kb.tt("dve", T2, pbn.v(0, [[1, NT]]), Vx, ALU.mult)
                                yield
                                kb.tt("dve", T2, T2, onf, ALU.add)
                                yield
                                kb.tt("dve", hk(og, hp), T2, Gt, ALU.mult)
                                yield
                                yield

                            state["force"] = PB[3] if sample else PB[5]
                            for stg_ in range(10):
                                gens = []
                                if 0 <= stg_ - 1 < 8:
                                    gens.append(units_gen(stg_ - 1))
                                def chain_(stg_=stg_):
                                    if 0 <= stg_ - 2 < 8:
                                        yield from post_gen(stg_ - 2)
                                    if stg_ < 8:
                                        yield from prep_gen(stg_)
                                gens.append(chain_())
                                ug_ = gens[0] if len(gens) == 2 else None
                                cg_ = gens[-1]
                                while ug_ is not None or cg_ is not None:
                                    if ug_ is not None:
                                        try:
                                            next(ug_)
                                        except StopIteration:
                                            ug_ = None
                                    for _ in range(CHAIN_PULL if ug_ is not None else 1000000):
                                        if cg_ is None:
                                            break
                                        try:
                                            next(cg_)
                                        except StopIteration:
                                            cg_ = None
                            state["force"] = None

                            if sample or last_prompt:
                                dst = o_shs if sample else o_shp
                                nr = nseq
                                for c0 in range(0, RKP, 512):
                                    ncol = min(512, RKP - c0)
                                    pb = bank()
                                    for gi, (g0, msz) in enumerate(SH_GROUPS):
                                        if c0 <= g0 < c0 + 512:
                                            kb.tr(pb.v(g0 - c0, [[1, msz]], 0, nr), ZL.v(gi * nseq, [[1, nseq]], 0, msz),
                                                  identF.v(0, [[1, msz]], 0, msz))
                                    kb.evac(rows_stg.v(0, [[1, ncol]], 0, nr), pb.v(0, [[1, ncol]], 0, nr))
                                    kb.dma("sp", dst[l, :, c0:c0 + ncol], rows_stg.v(0, [[1, ncol]], 0, nr).ap, "rowstg", rd=[rows_stg.v()])
                            if sample or last_prompt:
                                for s in range(nseq):
                                    for hh in range(2):
                                        pb = bank()
                                        for q in range(4):
                                            hp = hh * 4 + q
                                            src = HS.v(s * 64 + hp * 16 * 64, [[1, 64]]) if sample else HSTP.v((l * 8 + hp) * 64, [[1, 64]])
                                            kb.tr(pb.v(q * 128, [[1, 128]], 0, 64), src, identF.v())
                                        wi_ = (s * 2 + hh) % NWS
                                        wst_ = wk_stg[wi_]
                                        kb.evac(wst_.v(0, [[1, 512]], 0, 64), pb.v(0, [[1, 512]], 0, 64))
                                        d_ = (o_wkvs[l, s] if sample else o_wkvp[l])[hh * 8:hh * 8 + 8].rearrange("h i j -> i h j")
                                        kb.dma("sp", d_, wst_.v(0, [[64, 8], [1, 64]], 0, 64).ap, wk_key[wi_], rd=[wst_.v()])

                        MS = BR
                        macc = kb.sb(MS, "macc", 128, 8 * NT, F32)
                        mbf = kb.sb(MS, "mbf", 128, 8 * NT, BF16)
                        tmpA = kb.sb(MS, "tmpA", 128, NT, F32)
                        tmpB = kb.sb(MS, "tmpB", 128, max(NT, 512), F32)
                        cptbox["t"] = tmpB

                        def merge(xin, nkx, wmat, gate_off, bias_key, first, last):
                            for half in range(2):
                                wy = wload(wmat, 0, nkx, half * 512, 512) if nkx == 8 else None
                                if wy is None and half == 0:
                                    wy_full = wload(wmat, 0, nkx, 0, 1024)
                                wg = wload(w_in, 0, 8, GATE_OFF + gate_off + half * 512, 512)
                                for j in range(4):
                                    ot = half * 4 + j
                                    py = bank()
                                    pg = bank()
                                    if wy is not None:
                                        proj(py.v(0, [[1, NT]]), wy, j * 128, 128, nkx, xin)
                                    else:
                                        proj(py.v(0, [[1, NT]]), wy_full, ot * 128, 128, nkx, xin)
                                    proj(pg.v(0, [[1, NT]]), wg, j * 128, 128, 8, xn)
                                    kb.act(tmpA.v(), pg.v(0, [[1, NT]]), AF.Sigmoid)
                                    b = pv(l, bias_key, ot) if bias_key else 0.0
                                    if first:
                                        kb.stt(hk(macc, ot), py.v(0, [[1, NT]]), b, tmpA.v(), ALU.add, ALU.mult)
                                    else:
                                        kb.stt(tmpB.v(0, [[1, NT]]), py.v(0, [[1, NT]]), b, tmpA.v(), ALU.add, ALU.mult)
                                        kb.tt("dve", hk(mbf if last else macc, ot), hk(macc, ot), tmpB.v(0, [[1, NT]]), ALU.add)

                        kb.barrier()
                        merge(og, 8, W["rk_w_out"][l], D, None, True, False)

                        with ExitStack() as st:
                            gA = kb.sb(st, "gA", 128, 4 * NT, BF16)
                            cub = kb.sb(st, "cub", 128, 4 * nseq * (2 + T), F32)
                            csb = kb.sb(st, "csb", 128, NT, F32)
                            cv = kb.sb(st, "cv", 128, NT, F32)
                            if sample:
                                kb.dma("sp", rows_stg.v(0, [[1, 512]], 0, 32).ap, ssc[l], "rowstg", wr=[rows_stg.v()])
                                pb = bank()
                                for ct in range(4):
                                    kb.tr(pb.v(ct * 32, [[1, 32]]), rows_stg.v(ct * 128, [[1, 128]], 0, 32), identF.v(0, [[1, 32]], 0, 32))
                                    kb.evac(cub.v(ct * nseq * (2 + T), [[2 + T, 16], [1, 2]]), pb.v(ct * 32, [[2, 16], [1, 2]]))
                            else:
                                for ct in range(4):
                                    kb.copy("dve", cub.v(ct * (2 + T), [[1, 2]]), CSC.v((l * 4 + ct) * 2, [[1, 2]]))
                            wz = [wload(w_in, 0, 8, SC_OFF + i * 512, 512) for i in range(3)]
                            for ct in range(4):
                                pbb, pc, pu = bank(), bank(), bank()
                                proj(pbb.v(0, [[1, NT]]), wz[0], ct * 128, 128, 8, xn)
                                proj(pc.v(0, [[1, NT]]), wz[1], ct * 128, 128, 8, xn)
                                proj(pu.v(0, [[1, NT]]), wz[2], ct * 128, 128, 8, xn)
                                kb.copy("act", csb.v(), pc.v(0, [[1, NT]]))
                                base = ct * nseq * (2 + T)
                                cur = cub.v(base + 2, [[2 + T, nseq], [1, T]])
                                kb.tt("dve", cur, csb.v(0, [[T, nseq], [1, T]]), pu.v(0, [[T, nseq], [1, T]]), ALU.mult)
                                cvv = cv.v(0, [[T, nseq], [1, T]])
                                kb.ts("dve", cvv, cub.v(base + 0, [[2 + T, nseq], [1, T]]), pv(l, "sc_conv_w", 0 * 4 + ct), ALU.mult)
                                kb.stt(cvv, cub.v(base + 1, [[2 + T, nseq], [1, T]]), pv(l, "sc_conv_w", 1 * 4 + ct), cvv, ALU.mult, ALU.add)
                                kb.stt(cvv, cur, pv(l, "sc_conv_w", 2 * 4 + ct), cvv, ALU.mult, ALU.add)
                                kb.tt("dve", hk(gA, ct), cv.v(), pbb.v(0, [[1, NT]]), ALU.mult)
                                if not sample:
                                    kb.copy("dve", CSC.v((l * 4 + ct) * 2, [[1, 2]]), cub.v(base + T, [[1, 2]]))
                            if sample:
                                fm_rows_out([cub.v(ct * nseq * (2 + T) + T, [[2 + T, 16], [1, 2]]) for ct in range(4)],
                                            o_scs[l], 32, rows_stg)
                            elif last_prompt:
                                fm_rows_out([cub.v(ct * (2 + T) + T, [[1, 2]]) for ct in range(4)], o_scp[l], 2, rows_stg)
                            merge(gA, 4, W["sc_w_out"][l], 0, None, False, False)

                        with ExitStack() as st:
                            gC = kb.sb(st, "gC", 128, 4 * NT, BF16)
                            HT = 30 + T
                            glu = kb.sb(st, "glu", 128, 4 * nseq * HT, F32)
                            glb = kb.sb(st, "glb", 128, 4 * nseq * HT, BF16)
                            cc = kb.sb(st, "cc", 128, 4 * NT, F32)
                            ccq = kb.sb(st, "ccq", 128, 4 * NT, F32)
                            dg = kb.sb(st, "dg", 128, 31 * 128, BF16)
                            sgt = kb.sb(st, "sgt", 128, NT, F32)
                            mean = kb.sb(st, "mean", 128, NT, F32)
                            rs2 = kb.sb(st, "rs2", 128, NT, F32)
                            if sample:
                                for q in range(4):
                                    kb.dma("sp", rows_stg.v(0, [[1, 512]], 0, 120).ap, scf[l, q * 120:(q + 1) * 120, :], "rowstg",
                                           wr=[rows_stg.v()])
                                    pb = bank()
                                    for ct in range(4):
                                        kb.tr(pb.v(ct * 120, [[1, 120]]), rows_stg.v(ct * 128, [[1, 128]], 0, 120),
                                              identF.v(0, [[1, 120]], 0, 120))
                                        kb.evac(glu.v((ct * nseq + q * 4) * HT, [[HT, 4], [1, 30]]), pb.v(ct * 120, [[30, 4], [1, 30]]))
                            else:
                                for ct in range(4):
                                    kb.copy("dve", glu.v(ct * HT, [[1, 30]]), CCF.v((l * 4 + ct) * 30, [[1, 30]]))
                            wz = [wload(w_in, 0, 8, CF_OFF + i * 512, 512) for i in range(2)]
                            for ct in range(4):
                                p1, p2 = bank(), bank()
                                proj(p1.v(0, [[1, NT]]), wz[0], ct * 128, 128, 8, xn)
                                proj(p2.v(0, [[1, NT]]), wz[1], ct * 128, 128, 8, xn)
                                kb.act(sgt.v(), p2.v(0, [[1, NT]]), AF.Sigmoid, bias=pv(l, "cf_b_in", 4 + ct))
                                base = ct * nseq * HT
                                cur = glu.v(base + 30, [[HT, nseq], [1, T]])
                                kb.stt(cur, p1.v(0, [[T, nseq], [1, T]]), pv(l, "cf_b_in", ct), sgt.v(0, [[T, nseq], [1, T]]), ALU.add, ALU.mult)
                                kb.copy("act", glb.v(base, [[1, nseq * HT]]), glu.v(base, [[1, nseq * HT]]))
                                for k_ in range(31):
                                    kb.ts("dve", dg.v(k_ * 128, [[1, 128]]), identB.v(), pv(l, "cf_dw_w", k_ * 4 + ct), ALU.mult)
                                pcv = bank()
                                for k_ in range(31):
                                    kb.mm(pcv.v(0, [[T, nseq], [1, T]]), dg.v(k_ * 128, [[1, 128]]),
                                          glb.v(base + k_, [[HT, nseq], [1, T]]), start=(k_ == 0), stop=(k_ == 30))
                                kb.act(hk(cc, ct), pcv.v(0, [[1, NT]]), AF.Identity, bias=pv(l, "cf_dw_b", ct))
                                if not sample:
                                    kb.copy("dve", CCF.v((l * 4 + ct) * 30, [[1, 30]]), glu.v(base + T, [[1, 30]]))
                            if sample:
                                for q in range(4):
                                    fm_rows_out([glu.v((ct * nseq + q * 4) * HT + 4, [[HT, 4], [1, 30]]) for ct in range(4)],
                                                o_cfs[l, q * 120:(q + 1) * 120, :], 120, rows_stg)
                            elif last_prompt:
                                fm_rows_out([glu.v(ct * HT + T, [[1, 30]]) for ct in range(4)], o_cfp[l], 30, rows_stg)
                            kb.act(ccq.v(), cc.v(), AF.Square)
                            ps1, ps2 = bank(), bank()
                            for ct in range(4):
                                kb.mm(ps1.v(0, [[1, NT]]), onesF.v(), hk(cc, ct), start=(ct == 0), stop=(ct == 3))
                            for ct in range(4):
                                kb.mm(ps2.v(0, [[1, NT]]), onesF.v(), hk(ccq, ct), start=(ct == 0), stop=(ct == 3))
                            kb.act(mean.v(), ps1.v(0, [[1, NT]]), AF.Copy, scale=1.0 / DCF)
                            kb.tt("dve", rs2.v(), mean.v(), mean.v(), ALU.mult)
                            kb.stt(rs2.v(), ps2.v(0, [[1, NT]]), 1.0 / DCF, rs2.v(), ALU.mult, ALU.subtract)
                            kb.act(rs2.v(), rs2.v(), AF.Sqrt, bias=con(3))
                            kb.op("dve", lambda e: e.reciprocal(out=rs2.v().ap, in_=rs2.v().ap), rd=[rs2.v()], wr=[rs2.v()])
                            for ct in range(4):
                                kb.tt("dve", sgt.v(), hk(cc, ct), mean.v(), ALU.subtract)
                                kb.tt("dve", sgt.v(), sgt.v(), rs2.v(), ALU.mult)
                                kb.act(hk(gC, ct), sgt.v(), AF.Silu, bias=pv(l, "cf_ln_b", ct), scale=pv(l, "cf_ln_g", ct))
                            merge(gC, 4, W["cf_w_out"][l], 2 * D, "cf_b_out", False, True)

                        for half in range(2):
                            wo = wload(W["w_o"][l], 0, 8, half * 512, 512)
                            for j in range(4):
                                ot = half * 4 + j
                                po = bank()
                                proj(po.v(0, [[1, NT]]), wo, j * 128, 128, 8, mbf)
                                kb.tt("dve", hk(h, ot), hk(h, ot), po.v(0, [[1, NT]]), ALU.add)
                    kb.barrier()

                    with ExitStack() as st:
                        f = kb.sb(st, "f", 128, 32 * NT, BF16)
                        rl = kb.sb(st, "rl", 128, NT, F32)
                        rmsnorm_to(xn, "norm2_g", l)
                        for c in range(8):
                            w1 = wload(W["mlp_w1"][l], 0, 8, c * 512, 512)
                            for j in range(4):
                                ft = c * 4 + j
                                pf = bank()
                                proj(pf.v(0, [[1, NT]]), w1, j * 128, 128, 8, xn)
                                kb.act(rl.v(), pf.v(0, [[1, NT]]), AF.Relu)
                                kb.tt("dve", hk(f, ft), rl.v(), pf.v(0, [[1, NT]]), ALU.mult)
                        for og_ in range(2):
                            pbs = [PB[0], PB[1], PB[2], PB[3]]
                            for kg in range(4):
                                w2 = wload(W["mlp_w2"][l], kg * 8, 8, og_ * 512, 512)
                                for j in range(4):
                                    for kt in range(8):
                                        kb.mm(pbs[j].v(0, [[1, NT]]), w2(kt, j * 128, 128), hk(f, kg * 8 + kt),
                                              start=(kg == 0 and kt == 0), stop=(kg == 3 and kt == 7))
                            for j in range(4):
                                ot = og_ * 4 + j
                                kb.tt("dve", hk(h, ot), hk(h, ot), pbs[j].v(0, [[1, NT]]), ALU.add)
                    kb.barrier()

                    with ExitStack() as st:
                        pst = kb.sb(st, "pst", 128, DPLE, F32)
                        pT = kb.sb(st, "pT", 128, 2 * NT, BF16)
                        sg2 = kb.sb(st, "sg2", 128, NT, F32)
                        p_rows = psm[l] if sample else pp[l, t0:t0 + TP]
                        for r0 in range(0, NT, 128):
                            nr = min(128, NT - r0)
                            kb.dma("sp", pst.v(0, [[1, DPLE]], 0, nr).ap, p_rows[r0:r0 + nr, :], "pst", wr=[pst.v()])
                            pb = bank()
                            for j in range(2):
                                kb.tr(pb.v(j * 128, [[1, nr]]), pst.v(j * 128, [[1, 128]], 0, nr), identF.v(0, [[1, nr]], 0, nr))
                            kb.evac(pT.v(r0, [[NT, 2], [1, nr]]), pb.v(0, [[128, 2], [1, nr]]))
                        rmsnorm_to(xn, "ple_norm_g", l)
                        wple = wload(W["ple_w"][l], 0, 2, 0, 1024)
                        for half in range(2):
                            wg = wload(W["ple_gate_w"][l], 0, 8, half * 512, 512)
                            for j in range(4):
                                ot = half * 4 + j
                                pg, pq = bank(), bank()
                                proj(pg.v(0, [[1, NT]]), wg, j * 128, 128, 8, xn)
                                proj(pq.v(0, [[1, NT]]), wple, ot * 128, 128, 2, pT)
                                kb.act(sg2.v(), pg.v(0, [[1, NT]]), AF.Sigmoid)
                                kb.tt("dve", sg2.v(), sg2.v(), pq.v(0, [[1, NT]]), ALU.mult)
                                kb.tt("dve", hk(h, ot), hk(h, ot), sg2.v(), ALU.add)
                    kb.barrier()

                with ExitStack() as st:
                    yfm = kb.sb(st, "yfm", 128, 8 * NT, F32)
                    yst = [kb.sb(st, "yst%d" % i, 128, D, F32) for i in range(2)]
                    kb.act(xn.v(), h.v(), AF.Square)
                    pb = bank()
                    for kt in range(8):
                        kb.mm(pb.v(0, [[1, NT]]), onesB.v(), hk(xn, kt), start=(kt == 0), stop=(kt == 7))
                    kb.act(rstd.v(), pb.v(0, [[1, NT]]), AF.Sqrt, bias=con(2), scale=1.0 / D)
                    kb.op("dve", lambda e: e.reciprocal(out=rstd.v().ap, in_=rstd.v().ap), rd=[rstd.v()], wr=[rstd.v()])
                    for kt in range(8):
                        kb.stt(hk(yfm, kt), hk(h, kt), finalg.v(kt, [[1, 1]]), rstd.v(), ALU.mult, ALU.mult)
                    for gi, r0 in enumerate(range(0, NT, 128)):
                        nr = min(128, NT - r0)
                        yt = yst[gi % 2]
                        for half in range(2):
                            pb = bank()
                            for j in range(4):
                                kt = half * 4 + j
                                kb.tr(pb.v(j * 128, [[1, 128]], 0, nr), yfm.v(kt * NT + r0, [[1, nr]]), identF.v())
                            kb.evac(yt.v(half * 512, [[1, 512]], 0, nr), pb.v(0, [[1, 512]], 0, nr))
                        kb.dma("sp", y_rows[r0:r0 + nr, :], yt.v(0, [[1, D]], 0, nr).ap, "yst%d" % (gi % 2), rd=[yt.v()])
                kb.barrier()

        for p_ in range(cfg.NPP):
            wstate["first"] = p_ == 0
            wstate["n"] = 0
            run_pass("P", p_ * TP)
            if p_ == 0:
                wstate["first"] = False
                wstate["n"] = 0
                run_pass("S", 0)
        kb.barrier()
    return nc


_OUT_NAMES = ["yp", "ys", "o_scp", "o_shp", "o_wkvp", "o_cfp", "o_scs", "o_shs", "o_wkvs", "o_cfs"]


def run(cfg, inputs):
    L, SEQ = cfg.L, cfg.SEQ
    nc = build(cfg)
    f32 = lambda a: np.ascontiguousarray(np.asarray(a, dtype=np.float32))
    in_maps = []
    wkeys = ["norm1_g", "w_in", "sc_conv_w", "sc_w_out", "rk_mu", "rk_w0", "rk_w2", "rk_a0", "rk_a2", "rk_v0", "rk_v2",
             "rk_g2", "rk_k_k", "rk_k_a", "rk_ln_g", "rk_ln_b", "rk_w_out", "cf_b_in", "cf_dw_w", "cf_dw_b", "cf_ln_g",
             "cf_ln_b", "cf_w_out", "cf_b_out", "w_o", "norm2_g", "mlp_w1", "mlp_w2", "ple_w", "ple_gate_w", "ple_norm_g"]
    shared = {k: f32(inputs[k]) for k in wkeys}
    shared["rk_r_k"] = f32(inputs["rk_r_k"]).reshape(L, D)
    shared["final_norm_g"] = f32(inputs["final_norm_g"]).reshape(1, D)
    for c in range(NCORES):
        b0 = c * 16
        m = dict(shared)
        m["xp"] = f32(inputs["x_prompt"][c])
        m["xs"] = f32(inputs["x_sample"][b0:b0 + 16]).reshape(64, D)
        m["pp"] = f32(inputs["p_prompt"][:, c])
        m["ps"] = f32(inputs["p_sample"][:, b0:b0 + 16]).reshape(L, 64, DPLE)
        m["ssc"] = f32(inputs["state_sconv"][:, b0:b0 + 16]).reshape(L, 32, DSC)
        m["ssh"] = f32(inputs["state_shift"][:, b0:b0 + 16])
        m["swkv"] = f32(inputs["state_wkv"][:, b0:b0 + 16])
        m["scf"] = f32(inputs["state_cconv"][:, b0:b0 + 16]).reshape(L, 480, DCF)
        in_maps.append(m)
    res = run_bass_kernel_spmd(nc, in_maps, core_ids=list(range(NCORES)))
    R = res.results
    y_prompt = np.stack([R[c]["yp"] for c in range(NCORES)], 0)
    y_sample = np.concatenate([R[c]["ys"].reshape(16, 4, D) for c in range(NCORES)], 0)
    sc_p = np.stack([R[c]["o_scp"] for c in range(NCORES)], 1)
    sh_p = np.stack([R[c]["o_shp"].reshape(L, RKP) for c in range(NCORES)], 1)
    wkv_p = np.stack([R[c]["o_wkvp"] for c in range(NCORES)], 1)
    cf_p = np.stack([R[c]["o_cfp"] for c in range(NCORES)], 1)
    sc_s = np.concatenate([R[c]["o_scs"].reshape(L, 16, 2, DSC) for c in range(NCORES)], 1)
    sh_s = np.concatenate([R[c]["o_shs"] for c in range(NCORES)], 1)
    wkv_s = np.concatenate([R[c]["o_wkvs"] for c in range(NCORES)], 1)
    cf_s = np.concatenate([R[c]["o_cfs"].reshape(L, 16, 30, DCF) for c in range(NCORES)], 1)
    outs = (y_prompt, y_sample, sc_p, sh_p, wkv_p, cf_p, sc_s, sh_s, wkv_s, cf_s)
    return tuple(np.ascontiguousarray(o, dtype=np.float32) for o in outs)


def kernel(**inputs):
    return run(Cfg(L=4, SEQ=2048, TP=512), inputs)
```

```python
from contextlib import ExitStack
import numpy as np
import concourse.bass as bass
import concourse.mybir as mybir
from concourse.bass_utils import run_bass_kernel_spmd

F32 = mybir.dt.float32
BF16 = mybir.dt.bfloat16
AF = mybir.ActivationFunctionType
ALU = mybir.AluOpType

D = 1024
KT = 8
DSC = 512
DCF = 512
DFF = 4096
DPLE = 256
NIN = 8992
RKP = 3360
H = 16
NCORES = 8
SC_OFF, RK_OFF, CF_OFF, GATE_OFF = 0, 1536, 4896, 5920
R_OFF, K_OFF, V_OFF, WLO_OFF, ALO_OFF, VLO_OFF, GLO_OFF = 0, 1024, 2048, 3072, 3136, 3200, 3232
RMS_EPS, LN_EPS, GN_EPS = 1e-6, 1e-5, 64e-5
EXPM05 = float(np.exp(-0.5))
EPOCH = 30000
import os
SCAN_SEQ = os.environ.get('SCAN_SEQ', '0') == '1'
NBANKROT = int(os.environ.get('NBANKROT', '6'))
PSUM_EXCL_READ = os.environ.get('PSUM_EXCL_READ', '1') == '1'
POOL_ELEM = os.environ.get('POOL_ELEM', '0') == '1'
CHAIN_PULL = int(os.environ.get('CHAIN_PULL', '2'))

SH_GROUPS = [(i * 128, 128) for i in range(24)] + [(WLO_OFF, 64), (ALO_OFF, 64), (VLO_OFF, 32), (GLO_OFF, 128)]
NSHG = len(SH_GROUPS)

PV_SPEC = [
    ("norm1_g", 8), ("sc_conv_w", 12), ("mu_rkv", 24), ("mu_wlo", 1), ("mu_alo", 1), ("mu_vlo", 1), ("mu_glo", 1),
    ("rk_w0", 8), ("rk_a0", 8), ("rk_v0", 8), ("rk_k_k", 8), ("rk_k_a", 8), ("rk_r_k", 8), ("rk_ln_g", 8),
    ("rk_ln_b", 8), ("cf_b_in", 8), ("cf_dw_w", 124), ("cf_dw_b", 4), ("cf_ln_g", 4), ("cf_ln_b", 4),
    ("cf_b_out", 8), ("norm2_g", 8), ("ple_norm_g", 8), ("omka", 8),
]
PV_OFF = {}
_o = 0
for _k, _n in PV_SPEC:
    PV_OFF[_k] = _o
    _o += _n
NPV = _o


class Dep:
    __slots__ = ("w", "r")

    def __init__(self):
        self.w = None
        self.r = {}


class DSem:
    def __init__(self, sem, glob=False):
        self.sem = sem
        self.val = 0
        self.glob = glob


class Tile:
    def __init__(self, kb, handle, P, F, dtype):
        self.kb = kb
        self.t = handle
        self.P = P
        self.F = F
        self.dtype = dtype
        self.regs = {}
        self.psum = False
        self.glob = False

    def v(self, off=0, dims=None, p0=0, np_=None):
        if np_ is None:
            np_ = self.P - p0
        if dims is None:
            dims = [[1, self.F - off]]
        ext = 1
        for s, c in dims:
            ext += abs(s) * (c - 1)
        ap = bass.AP(self.t, p0 * self.F + off, [[self.F, np_]] + [list(d) for d in dims])
        return View(self, ap, off, off + ext)


class View:
    __slots__ = ("tile", "ap", "c0", "c1")

    def __init__(self, tile, ap, c0, c1):
        self.tile = tile
        self.ap = ap
        self.c0 = c0
        self.c1 = c1


class KB:
    ENG = ("pe", "act", "dve", "pool", "sp")

    def __init__(self, nc):
        self.nc = nc
        self.eng = {"pe": nc.tensor, "act": nc.scalar, "dve": nc.vector, "pool": nc.gpsimd, "sp": nc.sync}
        self.cnt = {e: 0 for e in self.ENG}
        self.sems = {e: [] for e in self.ENG}
        self.known = {e: {} for e in self.ENG}
        self.stack = ExitStack()
        self.dsems = []
        self.dsem_by_key = {}
        self.nuid = 0
        self.rr = 0
        self.pending = {}

    def uid(self, s):
        self.nuid += 1
        return "%s_%d" % (s, self.nuid)

    def sb(self, stack, name, P, F, dtype):
        h = stack.enter_context(self.nc.sbuf_tensor(self.uid(name), [P, F], dtype))
        return Tile(self, h, P, F, dtype)

    def ps(self, stack, name, F, dtype):
        h = stack.enter_context(self.nc.psum_tensor(self.uid(name), [128, F], dtype))
        t = Tile(self, h, 128, F, dtype)
        t.psum = True
        return t

    def new_dsem(self, name):
        if name in self.dsem_by_key:
            return self.dsem_by_key[name]
        s = self.stack.enter_context(self.nc.semaphore(self.uid(name)))
        d = DSem(s, glob=name.startswith(("wsem", "wback")))
        self.dsems.append(d)
        self.dsem_by_key[name] = d
        return d

    def _sem(self, e, ep):
        while len(self.sems[e]) <= ep:
            self.sems[e].append(self.stack.enter_context(self.nc.semaphore(self.uid("s" + e))))
        return self.sems[e][ep]

    def _deps(self, view):
        t = view.tile
        key = (0, t.F) if t.psum else (view.c0, view.c1)
        own = t.regs.get(key)
        if own is None:
            own = t.regs[key] = Dep()
        over = [d for (a, b), d in t.regs.items() if a < view.c1 and view.c0 < b]
        return own, over

    def _wait_tok(self, e, tok):
        if tok[0] == "e":
            _, f, n = tok
            if self.known[e].get(f, 0) >= n:
                return
            self.known[e][f] = n
            ep = (n - 1) // EPOCH
            self.eng[e].wait_ge(self._sem(f, ep), (n - 1) % EPOCH + 1)
        else:
            _, ds, val = tok
            key = id(ds)
            if self.known[e].get(key, 0) >= val:
                return
            self.known[e][key] = val
            self.eng[e].wait_ge(ds.sem, val)

    def _hazards(self, e, rd, wr):
        toks = []
        rdo, wro = [], []
        for v in rd:
            own, over = self._deps(v)
            rdo.append(own)
            for d in over:
                if d.w is not None:
                    toks.append(d.w)
                if v.tile.psum and PSUM_EXCL_READ:
                    for tk in d.r.values():
                        if not (tk[0] == "e" and tk[1] == e):
                            toks.append(tk)
        for v in wr:
            own, over = self._deps(v)
            wro.append(own)
            for d in over:
                if d.w is not None and not (e == "pe" and d.w[0] == "e" and d.w[1] == e):
                    toks.append(d.w)
                for tk in d.r.values():
                    toks.append(tk)
        for tk in toks:
            self._wait_tok(e, tk)
        return rdo, wro

    def op(self, e, fn, rd=(), wr=()):
        rdo, wro = self._hazards(e, rd, wr)
        ins = fn(self.eng[e])
        n = self.cnt[e] = self.cnt[e] + 1
        ins.then_inc(self._sem(e, (n - 1) // EPOCH), 1)
        tok = ("e", e, n)
        for d in rdo:
            d.r[e] = tok
        for d in wro:
            d.w = tok
            d.r = {}
        return ins

    def dma(self, q, out_ap, in_ap, ds, rd=(), wr=()):
        if isinstance(ds, str):
            ds = self.new_dsem(ds)
        rdo, wro = self._hazards(q, rd, wr)
        ins = self.eng[q].dma_start(out=out_ap, in_=in_ap)
        ds.val += 16
        ins.then_inc(ds.sem, 16)
        tok = ("d", ds, ds.val)
        for d in rdo:
            d.r[("d", id(ds))] = tok
        for d in wro:
            d.w = tok
            d.r = {}
        return ins

    def barrier(self):
        for e in self.ENG:
            for f in self.ENG:
                if f != e and self.cnt[f] > 0:
                    self._wait_tok(e, ("e", f, self.cnt[f]))
            for ds in self.dsems:
                if ds.val > 0:
                    self._wait_tok(e, ("d", ds, ds.val))

    def mm(self, out, lhsT, rhs, start=True, stop=True):
        self.op("pe", lambda e: e.matmul(out.ap, lhsT.ap, rhs.ap, start=start, stop=stop), rd=[lhsT, rhs], wr=[out])

    def tr(self, out, in_, ident):
        self.op("pe", lambda e: e.transpose(out.ap, in_.ap, ident.ap), rd=[in_, ident], wr=[out])

    def act(self, out, in_, func, bias=None, scale=1.0):
        rd = [in_]
        kw = {}
        if isinstance(bias, View):
            rd.append(bias)
            kw["bias"] = bias.ap
        elif bias is not None:
            kw["bias"] = bias
        if isinstance(scale, View):
            rd.append(scale)
            kw["scale"] = scale.ap
        else:
            kw["scale"] = scale
        if func == AF.Copy and (isinstance(bias, View) or isinstance(scale, View)):
            func = AF.Identity
        self.op("act", lambda e: e.activation(out=out.ap, in_=in_.ap, func=func, **kw), rd=rd, wr=[out])

    def tt(self, eng, out, a, b, op):
        self.op(eng, lambda e: e.tensor_tensor(out=out.ap, in0=a.ap, in1=b.ap, op=op), rd=[a, b], wr=[out])

    def ts(self, eng, out, a, s1, op0, s2=None, op1=None):
        rd = [a]
        a1 = s1
        a2 = s2
        if isinstance(s1, View):
            rd.append(s1)
            a1 = s1.ap
        if isinstance(s2, View):
            rd.append(s2)
            a2 = s2.ap
        if op1 is None:
            self.op(eng, lambda e: e.tensor_scalar(out=out.ap, in0=a.ap, scalar1=a1, scalar2=None, op0=op0), rd=rd, wr=[out])
        else:
            self.op(eng, lambda e: e.tensor_scalar(out=out.ap, in0=a.ap, scalar1=a1, scalar2=a2, op0=op0, op1=op1), rd=rd, wr=[out])

    def stt(self, out, a, s, b, op0, op1):
        rd = [a, b]
        sa = s
        if isinstance(s, View):
            rd.append(s)
            sa = s.ap
        self.op("dve", lambda e: e.scalar_tensor_tensor(out=out.ap, in0=a.ap, scalar=sa, in1=b.ap, op0=op0, op1=op1), rd=rd, wr=[out])

    def copy(self, eng, out, in_):
        if eng == "act":
            self.act(out, in_, AF.Copy)
        else:
            self.op(eng, lambda e: e.tensor_copy(out=out.ap, in_=in_.ap), rd=[in_], wr=[out])

    def evac(self, out, in_):
        self.rr ^= 1
        self.copy("act" if self.rr else "dve", out, in_)

    def memset(self, eng, out, val):
        self.op(eng, lambda e: e.memset(out.ap, val), wr=[out])


class Cfg:
    def __init__(self, L=4, SEQ=2048, TP=512):
        self.L = L
        self.SEQ = SEQ
        self.TP = TP
        self.NPP = SEQ // TP


def build(cfg):
    L, SEQ, TP = cfg.L, cfg.SEQ, cfg.TP
    nc = bass.Bass("TRN2", target_bir_lowering=False)
    kb = KB(nc)
    dram = {}

    def din(name, shape):
        dram[name] = nc.dram_tensor(name, list(shape), F32, kind="ExternalInput").ap()
        return dram[name]

    def dout(name, shape):
        dram[name] = nc.dram_tensor(name, list(shape), F32, kind="ExternalOutput").ap()
        return dram[name]

    xp = din("xp", [SEQ, D])
    xs = din("xs", [64, D])
    pp = din("pp", [L, SEQ, DPLE])
    psm = din("ps", [L, 64, DPLE])
    ssc = din("ssc", [L, 32, DSC])
    ssh = din("ssh", [L, 16, RKP])
    swkv = din("swkv", [L, 16, H, 64, 64])
    scf = din("scf", [L, 480, DCF])
    W = {}
    for name, shape in [
        ("norm1_g", [L, D]), ("w_in", [L, D, NIN]), ("sc_conv_w", [L, 3, DSC]), ("sc_w_out", [L, DSC, D]),
        ("rk_mu", [L, RKP]), ("rk_w0", [L, D]), ("rk_w2", [L, 64, D]), ("rk_a0", [L, D]), ("rk_a2", [L, 64, D]),
        ("rk_v0", [L, D]), ("rk_v2", [L, 32, D]), ("rk_g2", [L, 128, D]), ("rk_k_k", [L, D]), ("rk_k_a", [L, D]),
        ("rk_r_k", [L, D]), ("rk_ln_g", [L, D]), ("rk_ln_b", [L, D]), ("rk_w_out", [L, D, D]),
        ("cf_b_in", [L, 2 * DCF]), ("cf_dw_w", [L, 31, DCF]), ("cf_dw_b", [L, DCF]), ("cf_ln_g", [L, DCF]),
        ("cf_ln_b", [L, DCF]), ("cf_w_out", [L, DCF, D]), ("cf_b_out", [L, D]), ("w_o", [L, D, D]),
        ("norm2_g", [L, D]), ("mlp_w1", [L, D, DFF]), ("mlp_w2", [L, DFF, D]), ("ple_w", [L, DPLE, D]),
        ("ple_gate_w", [L, D, D]), ("ple_norm_g", [L, D]), ("final_norm_g", [1, D]),
    ]:
        W[name] = din(name, shape)
    yp = dout("yp", [SEQ, D])
    ys = dout("ys", [64, D])
    o_scp = dout("o_scp", [L, 2, DSC])
    o_shp = dout("o_shp", [L, 1, RKP])
    o_wkvp = dout("o_wkvp", [L, H, 64, 64])
    o_cfp = dout("o_cfp", [L, 30, DCF])
    o_scs = dout("o_scs", [L, 32, DSC])
    o_shs = dout("o_shs", [L, 16, RKP])
    o_wkvs = dout("o_wkvs", [L, 16, H, 64, 64])
    o_cfs = dout("o_cfs", [L, 480, DCF])

    G = kb.stack
    with G:
        PB = [kb.ps(G, "pb%d" % i, 512, F32) for i in range(6)]
        BFA = kb.ps(G, "bfa", 1024, BF16)
        BFB = kb.ps(G, "bfb", 1024, BF16)
        BFS = [BFA, BFB]
        state = {"pbi": 0}

        def bank():
            if state.get("force") is not None:
                return state["force"]
            state["pbi"] = (state["pbi"] + 1) % NBANKROT
            return PB[state["pbi"]]

        identF = kb.sb(G, "identF", 128, 128, F32)
        identB = kb.sb(G, "identB", 128, 128, BF16)
        onesB = kb.sb(G, "onesB", 128, 128, BF16)
        onesF = kb.sb(G, "onesF", 128, 128, F32)
        blk1 = kb.sb(G, "blk1", 128, 128, BF16)
        hm = kb.sb(G, "hm", 128, 2, F32)
        CON = kb.sb(G, "con", 128, 8, F32)
        kb.memset("pool", identF.v(), 1.0)
        kb.op("pool", lambda e: e.affine_select(out=identF.v().ap, in_=identF.v().ap, pattern=[[1, 128]],
                                                compare_op=ALU.is_equal, fill=0.0, base=0, channel_multiplier=-1),
              rd=[identF.v()], wr=[identF.v()])
        kb.copy("dve", identB.v(), identF.v())
        kb.memset("pool", onesB.v(), 1.0)
        kb.memset("pool", onesF.v(), 1.0)
        kb.memset("pool", blk1.v(), 0.0)
        kb.memset("pool", blk1.v(0, [[1, 64]], 0, 64), 1.0)
        kb.memset("pool", blk1.v(64, [[1, 64]], 64, 64), 1.0)
        kb.memset("pool", hm.v(), 0.0)
        kb.memset("pool", hm.v(0, [[1, 1]], 0, 64), 1.0)
        kb.memset("pool", hm.v(1, [[1, 1]], 64, 64), 1.0)
        for i, val in enumerate([0.0, 1.0, RMS_EPS, LN_EPS, GN_EPS]):
            kb.memset("pool", CON.v(i, [[1, 1]]), val)

        def con(i):
            return CON.v(i, [[1, 1]])

        def make_masks(GS):
            NB = 2 * GS
            mU = kb.sb(G, "mU%d" % GS, NB, 2 * NB, F32)
            mL = kb.sb(G, "mL%d" % GS, NB, NB, F32)
            kb.memset("pool", mU.v(), 1.0)
            kb.memset("pool", mL.v(), 1.0)
            kb.op("pool", lambda e: e.affine_select(out=mU.v(0, [[1, NB]]).ap, in_=mU.v(0, [[1, NB]]).ap, pattern=[[1, NB]],
                                                    compare_op=ALU.is_gt, fill=0.0, base=0, channel_multiplier=-1),
                  rd=[mU.v()], wr=[mU.v()])
            kb.op("pool", lambda e: e.affine_select(out=mU.v(NB, [[1, NB]]).ap, in_=mU.v(NB, [[1, NB]]).ap, pattern=[[1, NB]],
                                                    compare_op=ALU.is_ge, fill=0.0, base=0, channel_multiplier=-1),
                  rd=[mU.v()], wr=[mU.v()])
            kb.op("pool", lambda e: e.affine_select(out=mL.v().ap, in_=mL.v().ap, pattern=[[-1, NB]],
                                                    compare_op=ALU.is_gt, fill=0.0, base=0, channel_multiplier=1),
                  rd=[mL.v()], wr=[mL.v()])
            kb.memset("pool", mU.v(GS, [[1, GS]], 0, GS), 0.0)
            kb.memset("pool", mU.v(NB + GS, [[1, GS]], 0, GS), 0.0)
            kb.memset("pool", mL.v(0, [[1, GS]], GS, GS), 0.0)
            return mU, mL

        masks = {64: make_masks(64), 32: make_masks(32)}

        PV = kb.sb(G, "pv", 128, L * NPV, F32)
        finalg = kb.sb(G, "finalg", 128, 8, F32)

        def pv(l, key, i=0):
            return PV.v(l * NPV + PV_OFF[key] + i, [[1, 1]])

        with ExitStack() as st:
            rows = [kb.sb(st, "prow%d" % i, 128, 128, F32) for i in range(3)]
            for l in range(L):
                for rt in rows:
                    kb.memset("dve", rt.v(), 0.0)
                srcs = {
                    "norm1_g": W["norm1_g"][l].rearrange("(r c) -> r c", c=128),
                    "sc_conv_w": W["sc_conv_w"][l].rearrange("k (t c) -> (k t) c", c=128),
                    "mu_rkv": W["rk_mu"][l, 0:3072].rearrange("(r c) -> r c", c=128),
                    "mu_wlo": W["rk_mu"][l:l + 1, WLO_OFF:WLO_OFF + 64],
                    "mu_alo": W["rk_mu"][l:l + 1, ALO_OFF:ALO_OFF + 64],
                    "mu_vlo": W["rk_mu"][l:l + 1, VLO_OFF:VLO_OFF + 32],
                    "mu_glo": W["rk_mu"][l:l + 1, GLO_OFF:GLO_OFF + 128],
                    "cf_dw_w": W["cf_dw_w"][l].rearrange("k (t c) -> (k t) c", c=128),
                }
                for key in ("rk_w0", "rk_a0", "rk_v0", "rk_k_k", "rk_k_a", "rk_r_k", "rk_ln_g", "rk_ln_b", "cf_b_in",
                            "cf_dw_b", "cf_ln_g", "cf_ln_b", "cf_b_out", "norm2_g", "ple_norm_g"):
                    srcs[key] = W[key][l].rearrange("(r c) -> r c", c=128)
                r0 = 0
                for key, n in PV_SPEC:
                    if key == "omka":
                        continue
                    src = srcs[key]
                    done = 0
                    while done < n:
                        ti, ro = divmod(r0 + done, 128)
                        m = min(n - done, 128 - ro)
                        ncol = src.shape[1]
                        kb.dma("sp", rows[ti].v(0, [[1, ncol]], ro, m).ap, src[done:done + m, :], "prow%d" % ti,
                               wr=[rows[ti].v()])
                        done += m
                    r0 += n
                nrow = r0
                for ti in range(3):
                    m = min(128, nrow - ti * 128)
                    if m <= 0:
                        break
                    pb = bank()
                    kb.tr(pb.v(0, [[1, m]]), rows[ti].v(0, [[1, 128]], 0, m), identF.v(0, [[1, m]], 0, m))
                    kb.copy("dve", PV.v(l * NPV + ti * 128, [[1, m]]), pb.v(0, [[1, m]]))
                kb.ts("dve", PV.v(l * NPV + PV_OFF["omka"], [[1, 8]]), PV.v(l * NPV + PV_OFF["rk_k_a"], [[1, 8]]),
                      -1.0, ALU.mult, 1.0, ALU.add)
            kb.memset("dve", rows[0].v(), 0.0)
            kb.dma("sp", rows[0].v(0, [[1, 128]], 0, 8).ap, W["final_norm_g"][0].rearrange("(r c) -> r c", c=128), "prow0",
                   wr=[rows[0].v()])
            pb = bank()
            kb.tr(pb.v(0, [[1, 8]]), rows[0].v(0, [[1, 128]], 0, 8), identF.v(0, [[1, 8]], 0, 8))
            kb.copy("dve", finalg.v(), pb.v(0, [[1, 8]]))
            kb.barrier()

        NSLOT = 4
        SLOT_E = 4096
        wslots = [kb.sb(G, "wslot%d" % i, 128, SLOT_E, BF16) for i in range(NSLOT)]
        for t_ in wslots:
            t_.glob = True
        wsems = [kb.new_dsem("wsem%d" % i) for i in range(NSLOT)]
        wstate = {"i": 0, "n": 0, "first": True}

        scratch = {}

        def wload(mat2d, k0, nk, c0, ncols, krows=128):
            assert nk * ncols <= SLOT_E and krows == 128
            i = wstate["i"] = (wstate["i"] + 1) % NSLOT
            n = wstate["n"] = wstate["n"] + 1
            sl = wslots[i]
            dst = sl.v(0, [[ncols, nk], [1, ncols]])
            flat = sl.v(0, [[1, nk * ncols]])
            if wstate["first"]:
                src = mat2d.rearrange("(kt p) n -> p kt n", p=128)[:, k0:k0 + nk, c0:c0 + ncols]
                kb.dma("pool", dst.ap, src, wsems[i], wr=[sl.v()])
                scratch[n] = nc.dram_tensor("wscr%d" % n, [128, nk * ncols], BF16, kind="Internal").ap()
                kb.dma("sp", scratch[n], flat.ap, "wback%d" % i, rd=[sl.v()])
            else:
                kb.dma("sp", flat.ap, scratch[n], "wsemH%d" % i, wr=[sl.v()])

            def acc(kt, m0, msz):
                return sl.v(kt * ncols + m0, [[1, msz]], 0, krows)
            return acc

        HSTP = kb.sb(G, "hstp", 128, L * 8 * 64, F32)
        CSC = kb.sb(G, "csc", 128, L * 4 * 2, F32)
        CCF = kb.sb(G, "ccf", 128, L * 4 * 30, F32)
        CSH = kb.sb(G, "csh", 128, L * NSHG, F32)
        for t_ in (HSTP, CSC, CCF, CSH):
            kb.memset("dve", t_.v(), 0.0)

        def run_pass(kind, t0):
            sample = kind == "S"
            nseq, T = (16, 4) if sample else (1, TP)
            NT = nseq * T
            GS = 32 if sample else 64
            NV = 4 if sample else 64
            NB = 2 * GS
            LEV = 2 if sample else 6
            nu = 16 if sample else TP // 64
            mU, mL = masks[GS]
            last_prompt = (not sample) and (t0 + TP == SEQ)
            EP = "dve" if (wstate["first"] or not POOL_ELEM) else "pool"
            x_rows = xs if sample else xp[t0:t0 + TP]
            y_rows = ys if sample else yp[t0:t0 + TP]
            with ExitStack() as PS:
                h = kb.sb(PS, "h", 128, 8 * NT, F32)
                xn = kb.sb(PS, "xn", 128, 8 * NT, BF16)
                vf = kb.sb(PS, "vf", 128, 8 * NT, F32)
                rstd = kb.sb(PS, "rstd", 128, NT, F32)
                rm = kb.sb(PS, "rm", 128, NT, F32)
                cptbox = {}
                kb.memset("dve", rm.v(), 1.0)
                kb.memset("dve", rm.v(0, [[NV, nu], [1, 1]]), 0.0)
                HS = kb.sb(PS, "hs", 128, 8 * 16 * 64, F32) if sample else None

                def hk(t_, kt, c0=0, n=None):
                    return t_.v(kt * NT + c0, [[1, NT - c0 if n is None else n]])

                with ExitStack() as st:
                    xst = [kb.sb(st, "xst%d" % i, 128, D, F32) for i in range(2)]
                    for gi, r0 in enumerate(range(0, NT, 128)):
                        nr = min(128, NT - r0)
                        xt = xst[gi % 2]
                        kb.dma("sp", xt.v(0, [[1, D]], 0, nr).ap, x_rows[r0:r0 + nr, :], "xst%d" % (gi % 2), wr=[xt.v()])
                        for half in range(2):
                            pb = bank()
                            for j in range(4):
                                kt = half * 4 + j
                                kb.tr(pb.v(j * 128, [[1, nr]]), xt.v(kt * 128, [[1, 128]], 0, nr), identF.v(0, [[1, nr]], 0, nr))
                            kb.evac(h.v(half * 4 * NT + r0, [[NT, 4], [1, nr]]), pb.v(0, [[128, 4], [1, nr]]))
                    kb.barrier()

                def rmsnorm_to(dst, gkey, l, fin=False):
                    kb.act(dst.v(), h.v(), AF.Square)
                    pb = bank()
                    for kt in range(8):
                        kb.mm(pb.v(0, [[1, NT]]), onesB.v(), hk(dst, kt), start=(kt == 0), stop=(kt == 7))
                    kb.act(rstd.v(), pb.v(0, [[1, NT]]), AF.Sqrt, bias=con(2), scale=1.0 / D)
                    kb.op("dve", lambda e: e.reciprocal(out=rstd.v().ap, in_=rstd.v().ap), rd=[rstd.v()], wr=[rstd.v()])
                    for kt in range(8):
                        g = finalg.v(kt, [[1, 1]]) if fin else pv(l, gkey, kt)
                        kb.stt(hk(dst, kt), hk(h, kt), g, rstd.v(), ALU.mult, ALU.mult)

                def proj(out_ps, wacc, m0, msz, nk, rhs_t, start=True, stop=True):
                    for kt in range(nk):
                        kb.mm(out_ps, wacc(kt, m0, msz), hk(rhs_t, kt), start=(start and kt == 0), stop=(stop and kt == nk - 1))

                def fm_rows_out(src_views, dst_rows_ap, nrows, stg):
                    pb = bank()
                    for ct, sv in enumerate(src_views):
                        kb.copy("dve", cptbox["t"].v(ct * 128, [[1, nrows]]), sv)
                        kb.tr(pb.v(ct * 128, [[1, 128]], 0, nrows), cptbox["t"].v(ct * 128, [[1, nrows]]), identF.v())
                    kb.evac(stg.v(0, [[1, 512]], 0, nrows), pb.v(0, [[1, 512]], 0, nrows))
                    kb.dma("sp", dst_rows_ap, stg.v(0, [[1, 512]], 0, nrows).ap, "rowstg", rd=[stg.v()])

                for l in range(L):
                    w_in = W["w_in"][l]
                    with ExitStack() as BR:
                        rows_stg = kb.sb(BR, "rowstg", 128, 512, F32)
                        NWS = 4 if sample else 1
                        wk_stg = [rows_stg] + [kb.sb(BR, "wkstg%d" % i_, 64, 512, F32) for i_ in range(1, NWS)]
                        wk_key = ["rowstg"] + ["wkstg%d" % i_ for i_ in range(1, NWS)]
                        og = kb.sb(BR, "og", 128, 8 * NT, BF16)
                        rmsnorm_to(xn, "norm1_g", l)

                        with ExitStack() as st:
                            ZW = nseq * (1 + T)
                            Z = kb.sb(st, "Z", 128, ZW, F32)
                            ZL = kb.sb(st, "ZL", 128, NSHG * nseq, F32)
                            lora = {k_: kb.sb(st, "lo" + k_, 128, NT, BF16) for k_ in ("w", "a", "v", "g")}
                            names32 = ["Dd", "R", "Kx", "E", "Lc", "Ep", "Emp", "A", "SV", "T2", "KK"]
                            B32 = {n_: kb.sb(st, n_, 128, NT, F32) for n_ in names32}
                            for alias, base_ in (("Lp", "Dd"), ("KP", "E"), ("SD", "SV"), ("Tt", "SV")):
                                B32[alias] = B32[base_]
                            names16 = ["ksq", "at", "bt", "kt", "rt"]
                            B16 = {n_: kb.sb(st, n_, 128, NT, BF16) for n_ in names16}
                            HPS = []
                            for i_ in range(2):
                                P_ = dict(
                                    Em=kb.sb(st, "Em", 128, NT, F32), Vx=kb.sb(st, "Vx", 128, NT, F32), Gt=kb.sb(st, "Gt", 128, NT, F32),
                                    T2p=kb.sb(st, "T2p", 128, NT, F32), rkb=kb.sb(st, "rkb", 128, NT, BF16),
                                    vbP=kb.sb(st, "vbP", 128, nu * GS, BF16), ARB=kb.sb(st, "ARB", 128, nu * 2 * NB, BF16),
                                    BBt=kb.sb(st, "BBt", 128, nu * NB, BF16), KBt=kb.sb(st, "KBt", 128, nu * NB, BF16),
                                    Otok=kb.sb(st, "Otok", NB, nu * 64, F32), Osq=kb.sb(st, "Osq", 128, max(NT, nu * 64), F32),
                                    Onorm=kb.sb(st, "Onorm", NB, nu * 64, BF16), stats=kb.sb(st, "stats", NB, 4 * nu, F32))
                                for t_ in (P_["vbP"], P_["ARB"], P_["BBt"], P_["KBt"]):
                                    kb.memset("dve", t_.v(), 0.0)
                                HPS.append(P_)
                            NSETS = 4 if sample else 3
                            S1BANKS = [PB[0], PB[1], PB[2], PB[5]]
                            USETS = []
                            for si in range(NSETS):
                                USETS.append(dict(
                                    NA=kb.sb(st, "NA", NB, 2 * NB, BF16), AK=kb.sb(st, "AK", NB, 2 * NB, BF16),
                                    M0=kb.sb(st, "M0", NB, NB, BF16),
                                    SP=[kb.sb(st, "SP", NB, 3 * NB, BF16) for _ in range(2)],
                                    Vs=kb.sb(st, "Vs", NB, 64, BF16),
                                    Xb=kb.sb(st, "Xb", NB, 64, BF16), Ub=kb.sb(st, "Ub", NB, 64, BF16),
                                    BKT=kb.sb(st, "BKT", NB, 256, BF16),
                                    Ht=kb.sb(st, "Ht", 128, 64, F32),
                                    S1=S1BANKS[si], si=si,
                                ))
                            HBF = [kb.sb(st, "Hbf", 128, 64, BF16) for _ in range(2)]
                            shst = kb.sb(st, "shst", 16, 512, F32) if sample else None
                            SH = kb.sb(st, "SH", 128, NSHG * 16, F32) if sample else None

                            if sample:
                                for c0 in range(0, RKP, 512):
                                    ncol = min(512, RKP - c0)
                                    kb.dma("sp", shst.v(0, [[1, ncol]]).ap, ssh[l, :, c0:c0 + ncol], "shst", wr=[shst.v()])
                                    pb = bank()
                                    for gi, (g0, msz) in enumerate(SH_GROUPS):
                                        if c0 <= g0 < c0 + 512:
                                            kb.tr(pb.v(0, [[1, 16]], 0, msz), shst.v(g0 - c0, [[1, msz]]), identF.v(0, [[1, 16]], 0, 16))
                                            kb.evac(SH.v(gi * 16, [[1, 16]], 0, msz), pb.v(0, [[1, 16]], 0, msz))
                                for s in range(16):
                                    for hh in range(2):
                                        wi_ = (s * 2 + hh) % NWS
                                        wst_ = wk_stg[wi_]
                                        kb.dma("sp", wst_.v(0, [[64, 8], [1, 64]], 0, 64).ap,
                                               swkv[l, s, hh * 8:hh * 8 + 8].rearrange("h i j -> i h j"), wk_key[wi_], wr=[wst_.v()])
                                        pb = bank()
                                        for q in range(4):
                                            kb.tr(pb.v(q * 64, [[1, 64]]), wst_.v(q * 128, [[1, 128]], 0, 64), identF.v(0, [[1, 64]], 0, 64))
                                        kb.evac(HS.v(s * 64 + hh * 4 * 16 * 64, [[16 * 64, 4], [1, 64]]), pb.v(0, [[64, 4], [1, 64]]))

                            def zproj(gi, wacc, m0, msz, dst, mukey, mui, func=None):
                                pz = bank()
                                proj(pz.v(0, [[1, NT]], 0, msz), wacc, m0, msz, 8, xn)
                                if sample:
                                    kb.copy("dve", Z.v(0, [[1 + T, 16], [1, 1]], 0, msz), SH.v(gi * 16, [[1, 16], [1, 1]], 0, msz))
                                else:
                                    kb.copy("dve", Z.v(0, [[1, 1]], 0, msz), CSH.v(l * NSHG + gi, [[1, 1]], 0, msz))
                                cur = Z.v(1, [[1 + T, nseq], [1, T]], 0, msz)
                                prev = Z.v(0, [[1 + T, nseq], [1, T]], 0, msz)
                                kb.copy("act", cur, pz.v(0, [[T, nseq], [1, T]], 0, msz))
                                Dd = B32["Dd"].v(0, [[T, nseq], [1, T]], 0, msz)
                                kb.tt(EP, Dd, prev, cur, ALU.subtract)
                                mu = PV.v(l * NPV + PV_OFF[mukey] + mui, [[1, 1]], 0, msz)
                                if func is None:
                                    kb.stt(dst.tile.v(dst.c0, [[T, nseq], [1, T]], 0, msz), Dd, mu, cur, ALU.mult, ALU.add)
                                else:
                                    kb.stt(Dd, Dd, mu, cur, ALU.mult, ALU.add)
                                    kb.act(dst, B32["Dd"].v(0, [[1, NT]], 0, msz), func)
                                kb.copy("dve", ZL.v(gi * nseq, [[1, nseq]], 0, msz), Z.v(T, [[1 + T, nseq]], 0, msz))
                                if not sample:
                                    kb.copy("dve", CSH.v(l * NSHG + gi, [[1, 1]], 0, msz), Z.v(T, [[1, 1]], 0, msz))

                            wl = wload(w_in, 0, 8, RK_OFF + WLO_OFF, RKP - WLO_OFF)
                            zproj(24, wl, 0, 64, lora["w"].v(0, [[1, NT]], 0, 64), "mu_wlo", 0, AF.Tanh)
                            zproj(25, wl, ALO_OFF - WLO_OFF, 64, lora["a"].v(0, [[1, NT]], 0, 64), "mu_alo", 0, AF.Copy)
                            zproj(26, wl, VLO_OFF - WLO_OFF, 32, lora["v"].v(0, [[1, NT]], 0, 32), "mu_vlo", 0, AF.Copy)
                            zproj(27, wl, GLO_OFF - WLO_OFF, 128, lora["g"].v(0, [[1, NT]]), "mu_glo", 0, AF.Sigmoid)
                            lw = kb.sb(st, "lw", 128, 4 * 1024, BF16)
                            for i_, (nm_, kr) in enumerate([("rk_w2", 64), ("rk_a2", 64), ("rk_v2", 32), ("rk_g2", 128)]):
                                kb.dma("pool", lw.v(i_ * 1024, [[1, 1024]], 0, kr).ap, W[nm_][l][0:kr, :], "lw", wr=[lw.v()])
                            wrkv = {}

                            def prep_gen(hp):
                                P = HPS[hp % 2]
                                ARB, BBt, KBt, vbP = P["ARB"], P["BBt"], P["KBt"], P["vbP"]
                                if hp % 4 == 0:
                                    for nm, off in (("r", R_OFF), ("k", K_OFF), ("v", V_OFF)):
                                        wrkv[nm] = wload(w_in, 0, 8, RK_OFF + off + (hp // 4) * 512, 512)
                                R, Kx, E, Lc, Lp, Ep, Emp, A, SV, SD, KK, Tt, KP, T2 = [
                                    B32[n_].v() for n_ in ["R", "Kx", "E", "Lc", "Lp", "Ep", "Emp", "A", "SV", "SD", "KK", "Tt", "KP", "T2"]]
                                Em, Gt = P["Em"].v(), P["Gt"].v()
                                Vx = hk(vf, hp) if l == 0 else P["Vx"].v()
                                m0 = (hp % 4) * 128
                                zproj(hp, wrkv["r"], m0, 128, R, "mu_rkv", hp)
                                yield
                                zproj(8 + hp, wrkv["k"], m0, 128, Kx, "mu_rkv", 8 + hp)
                                yield
                                zproj(16 + hp, wrkv["v"], m0, 128, Vx, "mu_rkv", 16 + hp)
                                yield
                                kb.act(B16["ksq"].v(), Kx, AF.Square, scale=pv(l, "rk_k_k", hp))
                                yield
                                pn = bank()
                                kb.mm(pn.v(0, [[1, NT]]), blk1.v(), B16["ksq"].v())
                                yield
                                kb.act(SD, pn.v(0, [[1, NT]]), AF.Sqrt)
                                yield
                                kb.ts("dve", SD, SD, 1e-12, ALU.max)
                                yield
                                kb.op("dve", lambda e: e.reciprocal(out=SD.ap, in_=SD.ap), rd=[SD], wr=[SD])
                                yield
                                kb.stt(KK, Kx, pv(l, "rk_k_k", hp), SD, ALU.mult, ALU.mult)
                                yield
                                yield
                                pw = bank()
                                kb.mm(pw.v(0, [[1, NT]]), lw.v(0 * 1024 + hp * 128, [[1, 128]], 0, 64), lora["w"].v(0, [[1, NT]], 0, 64))
                                yield
                                kb.act(E, pw.v(0, [[1, NT]]), AF.Sigmoid, bias=pv(l, "rk_w0", hp))
                                yield
                                pa = bank()
                                kb.mm(pa.v(0, [[1, NT]]), lw.v(1 * 1024 + hp * 128, [[1, 128]], 0, 64), lora["a"].v(0, [[1, NT]], 0, 64))
                                yield
                                kb.act(A, pa.v(0, [[1, NT]]), AF.Sigmoid, bias=pv(l, "rk_a0", hp))
                                yield
                                if l > 0:
                                    pvv = bank()
                                    kb.mm(pvv.v(0, [[1, NT]]), lw.v(2 * 1024 + hp * 128, [[1, 128]], 0, 32), lora["v"].v(0, [[1, NT]], 0, 32))
                                    kb.act(SV, pvv.v(0, [[1, NT]]), AF.Sigmoid, bias=pv(l, "rk_v0", hp))
                                    kb.tt("dve", T2, hk(vf, hp), Vx, ALU.subtract)
                                    kb.tt("dve", T2, T2, SV, ALU.mult)
                                    kb.tt("dve", Vx, Vx, T2, ALU.add)
                                kb.op("dve", lambda e: e.tensor_tensor_scan(out=Lc.ap, data0=rm.v().ap, data1=E.ap, initial=0.0,
                                                                            op0=ALU.mult, op1=ALU.add), rd=[rm.v(), E], wr=[Lc])
                                yield
                                kb.act(Ep, Lc, AF.Exp, scale=EXPM05)
                                yield
                                kb.act(Em, Lc, AF.Exp, scale=-EXPM05)
                                yield
                                kb.tt(EP, Lp, Lc, E, ALU.subtract)
                                yield
                                kb.act(Emp, Lp, AF.Exp, scale=-EXPM05)
                                yield
                                yield
                                pgt = bank()
                                kb.mm(pgt.v(0, [[1, NT]]), lw.v(3 * 1024 + hp * 128, [[1, 128]]), lora["g"].v())
                                yield
                                kb.copy("act", Gt, pgt.v(0, [[1, NT]]))
                                yield
                                yield
                                kb.ts("dve", Tt, A, pv(l, "rk_k_a", hp), ALU.mult, pv(l, "omka", hp), ALU.add)
                                yield
                                kb.tt(EP, KP, Kx, Tt, ALU.mult)
                                yield
                                kb.stt(B16["at"].v(), KK, -1.0, Emp, ALU.mult, ALU.mult)
                                yield
                                kb.tt(EP, T2, KK, A, ALU.mult)
                                yield
                                kb.tt(EP, B16["bt"].v(), T2, Ep, ALU.mult)
                                yield
                                yield
                                kb.tt(EP, B16["kt"].v(), KP, Ep, ALU.mult)
                                yield
                                kb.tt(EP, B16["rt"].v(), R, Em, ALU.mult)
                                yield
                                kb.stt(P["rkb"].v(), R, pv(l, "rk_r_k", hp), KP, ALU.mult, ALU.mult)
                                yield
                                kb.copy("act", vbP.v(0, [[GS, nu], [1, NV]]), Vx.tile.v(Vx.c0, [[NV, nu], [1, NV]]))
                                yield
                                yield
                                for src, dstt, doff in ((B16["at"], ARB, 0), (B16["rt"], ARB, NB)):
                                    for g in range(2):
                                        kb.copy(EP if g == 0 else "act",
                                                dstt.v(doff + g * GS, [[2 * NB, nu], [1, NV]], 64 * g, 64),
                                                src.v(0, [[NV, nu], [1, NV]], 64 * g, 64))
                                for src, dstt in ((B16["bt"], BBt), (B16["kt"], KBt)):
                                    for g in range(2):
                                        kb.copy(EP if g == 0 else "act",
                                                dstt.v(g * GS, [[NB, nu], [1, NV]], 64 * g, 64),
                                                src.v(0, [[NV, nu], [1, NV]], 64 * g, 64))

                                yield

                            def units_gen(hp):
                                P = HPS[hp % 2]
                                ARB, BBt, KBt, vbP, Otok = P["ARB"], P["BBt"], P["KBt"], P["vbP"], P["Otok"]

                                def part_a(u, S):
                                    si = S["si"]
                                    S1 = S["S1"]
                                    ev = "act" if si % 2 == 0 else "dve"
                                    aB = ARB.v(u * 2 * NB, [[1, NB]])
                                    arB = ARB.v(u * 2 * NB, [[1, 2 * NB]])
                                    bB = BBt.v(u * NB, [[1, NB]])
                                    kB_ = KBt.v(u * NB, [[1, NB]])
                                    NA, AK, M0 = S["NA"], S["AK"], S["M0"]
                                    P1 = S1.v(0, [[1, 2 * NB]], 0, NB)
                                    P2 = S1.v(256, [[1, 2 * NB]], 0, NB)
                                    P3 = PB[3].v(si * 128 + (64 if sample else 0), [[1, NB]], 0, NB)
                                    kb.mm(P1, bB, arB)
                                    kb.mm(P2, kB_, arB)
                                    kb.mm(P3, aB, bB)
                                    bfo = (si % 3) * 320
                                    BF = BFS[1] if si < 3 else BFS[0]
                                    for g in range(2):
                                        kb.tr(BF.v(bfo, [[1, 64]], g * GS, GS), vbP.v(u * GS, [[1, GS]], 64 * g, 64),
                                              identB.v(64 * g, [[1, 64]], 64 * g, 64))
                                    kb.tr(BF.v(bfo + 64, [[1, 128]], 0, NB), bB, identB.v())
                                    kb.tr(BF.v(bfo + 192, [[1, 128]], 0, NB), kB_, identB.v())
                                    kb.tt("dve", NA.v(), P1, mU.v(), ALU.mult)
                                    kb.tt("dve", AK.v(), P2, mU.v(), ALU.mult)
                                    kb.tt("dve", M0.v(), P3, mL.v(), ALU.mult)
                                    kb.copy("act", S["Vs"].v(), BF.v(bfo, [[1, 64]], 0, NB))
                                    kb.copy("act", S["BKT"].v(), BF.v(bfo + 64, [[1, 256]], 0, NB))
                                    yield
                                    Nk, Mk = NA.v(0, [[1, NB]]), M0.v()
                                    SP = S["SP"][1]
                                    kb.mm(S1.v(NB, [[1, NB]], 0, NB), Mk, Nk)
                                    kb.mm(S1.v(2 * NB, [[1, NB]], 0, NB), Nk, Mk)
                                    kb.tt("dve", SP.v(0, [[1, NB]]), Nk, identB.v(0, [[1, NB]], 0, NB), ALU.add)
                                    kb.copy(ev, SP.v(NB, [[1, 2 * NB]]), S1.v(NB, [[1, 2 * NB]], 0, NB))
                                    yield
                                    for hh in range(2, LEV + 1):
                                        last = hh == LEV
                                        SN, Nk, Mk = SP.v(0, [[1, NB]]), SP.v(NB, [[1, NB]]), SP.v(2 * NB, [[1, NB]])
                                        kb.mm(S1.v(0, [[1, NB]], 0, NB), identB.v(0, [[1, NB]], 0, NB), SN, start=True, stop=False)
                                        kb.mm(S1.v(0, [[1, NB]], 0, NB), Mk, SN, start=False, stop=True)
                                        SPn = S["SP"][hh % 2]
                                        if last:
                                            kb.copy(ev, SPn.v(0, [[1, NB]]), S1.v(0, [[1, NB]], 0, NB))
                                        else:
                                            kb.mm(S1.v(NB, [[1, NB]], 0, NB), Mk, Nk)
                                            kb.mm(S1.v(2 * NB, [[1, NB]], 0, NB), Nk, Mk)
                                            kb.copy(ev, SPn.v(), S1.v(0, [[1, 3 * NB]], 0, NB))
                                        SP = SPn
                                        yield
                                    S["TT"] = SP.v(0, [[1, NB]])

                                def part_b(u, S):
                                    hoff = u * 64 + hp * 16 * 64 if sample else (l * 8 + hp) * 64
                                    Hst = (HS if sample else HSTP).v(hoff, [[1, 64]])
                                    aB = ARB.v(u * 2 * NB, [[1, NB]])
                                    rB = ARB.v(u * 2 * NB + NB, [[1, NB]])
                                    NA, AK, Vs, Xb, Ub, Ht, BKT = (S[k_] for k_ in ("NA", "AK", "Vs", "Xb", "Ub", "Ht", "BKT"))
                                    Hbf = HBF[u % 2]
                                    PBX = PB[4]
                                    qo = (u % 2) * 256
                                    if sample or u == 0:
                                        kb.copy("act", Hbf.v(), Hst)
                                    PX = PBX.v(qo, [[1, 64]], 0, NB)
                                    kb.mm(PX, aB, Hbf.v(), start=True, stop=False)
                                    kb.mm(PX, AK.v(0, [[1, NB]]), Vs.v(), start=False, stop=True)
                                    kb.copy("act", Xb.v(), PX)
                                    yield
                                    PU = PBX.v(qo + 64, [[1, 64]], 0, NB)
                                    kb.mm(PU, S["TT"], Xb.v())
                                    kb.copy("act", Ub.v(), PU)
                                    yield
                                    PH = PBX.v(qo + 128, [[1, 64]])
                                    kb.mm(PH, BKT.v(0, [[1, 128]]), Ub.v(), start=True, stop=False)
                                    kb.mm(PH, BKT.v(128, [[1, 128]]), Vs.v(), start=False, stop=True)
                                    PO = PBX.v(qo + 192, [[1, 64]], 0, NB)
                                    kb.mm(PO, rB, Hbf.v(), start=True, stop=False)
                                    kb.mm(PO, NA.v(NB, [[1, NB]]), Ub.v(), start=False, stop=False)
                                    kb.mm(PO, AK.v(NB, [[1, NB]]), Vs.v(), start=False, stop=True)
                                    kb.tt("dve", Ht.v(), PH, Hst, ALU.add)
                                    wc = P["Em"].v(u * NV + NV - 1, [[1, 1]])
                                    if not sample and u + 1 < nu:
                                        kb.act(HBF[(u + 1) % 2].v(), Ht.v(), AF.Copy, scale=wc)
                                    kb.act(Hst, Ht.v(), AF.Copy, scale=wc)
                                    kb.copy("act", Otok.v(u * 64, [[1, 64]]), PO)
                                    yield

                                a_next = 0
                                b_next = 0
                                b_fin = 0
                                a_act = []
                                a_done = set()
                                b_act = []
                                nbmax = 2 if sample else 1
                                while b_fin < nu:
                                    while len(a_act) < 2 and a_next < nu and a_next < b_fin + NSETS - len(b_act):
                                        a_act.append((a_next, part_a(a_next, USETS[a_next % NSETS])))
                                        a_next += 1
                                    for (ua, ga) in list(a_act):
                                        try:
                                            next(ga)
                                        except StopIteration:
                                            a_act.remove((ua, ga))
                                            a_done.add(ua)
                                    while len(b_act) < nbmax and b_next < nu and b_next in a_done:
                                        b_act.append(part_b(b_next, USETS[b_next % NSETS]))
                                        b_next += 1
                                    for gb in list(b_act):
                                        try:
                                            next(gb)
                                        except StopIteration:
                                            b_act.remove(gb)
                                            b_fin += 1
                                    yield

                            def post_gen(hp):
                                P = HPS[hp % 2]
                                Otok, Osq, Onorm, stats = P["Otok"], P["Osq"], P["Onorm"], P["stats"]
                                Gt, T2 = P["Gt"].v(), P["T2p"].v()
                                onf = P["Osq"].v(0, [[1, NT]])
                                Vx = hk(vf, hp) if l == 0 else P["Vx"].v()
                                s1 = stats.v(0, [[1, nu]])
                                s2 = stats.v(nu, [[1, nu]])
                                mn = stats.v(2 * nu, [[1, nu]])
                                rsd = stats.v(3 * nu, [[1, nu]])
                                O3 = Otok.v(0, [[64, nu], [1, 64]])
                                kb.op("dve", lambda e: e.tensor_reduce(out=s1.ap, in_=O3.ap, axis=mybir.AxisListType.X, op=ALU.add),
                                      rd=[O3], wr=[s1])
                                yield
                                kb.act(Osq.v(0, [[1, nu * 64]], 0, NB), Otok.v(), AF.Square)
                                yield
                                Q3 = Osq.v(0, [[64, nu], [1, 64]], 0, NB)
                                kb.op("dve", lambda e: e.tensor_reduce(out=s2.ap, in_=Q3.ap, axis=mybir.AxisListType.X, op=ALU.add),
                                      rd=[Q3], wr=[s2])
                                yield
                                kb.ts("dve", mn, s1, 1.0 / 64, ALU.mult)
                                yield
                                kb.tt("dve", s1, mn, mn, ALU.mult)
                                yield
                                kb.stt(s2, s2, 1.0 / 64, s1, ALU.mult, ALU.subtract)
                                yield
                                kb.act(rsd, s2, AF.Sqrt, bias=CON.v(4, [[1, 1]], 0, NB))
                                yield
                                kb.op("dve", lambda e: e.reciprocal(out=rsd.ap, in_=rsd.ap), rd=[rsd], wr=[rsd])
                                yield
                                yield
                                kb.tt("dve", Osq.v(0, [[64, nu], [1, 64]], 0, NB), O3, stats.v(2 * nu, [[1, nu], [0, 64]]), ALU.subtract)
                                yield
                                kb.tt("dve", Onorm.v(0, [[64, nu], [1, 64]]), Osq.v(0, [[64, nu], [1, 64]], 0, NB),
                                      stats.v(3 * nu, [[1, nu], [0, 64]]), ALU.mult)
                                yield
                                yield
                                for u in range(nu):
                                    for g in range(2):
                                        kb.tr(BFS[0].v(512 + u * GS, [[1, GS]], 64 * g, 64), Onorm.v(u * 64, [[1, 64]], g * GS, GS),
                                              identB.v(g * GS, [[1, GS]], g * GS, GS))
                                kb.act(onf.tile.v(0, [[NV, nu], [1, NV]]), BFS[0].v(512, [[GS, nu], [1, NV]]), AF.Identity,
                                       bias=pv(l, "rk_ln_b", hp), scale=pv(l, "rk_ln_g", hp))
                                yield
                                yield
                                pbn = bank()
                                kb.mm(pbn.v(0, [[1, NT]]), blk1.v(), P["rkb"].v())
                                yield
                                kb.tt("dve", T2, pbn.v(0, [[1, NT]]), Vx, ALU.mult)
                                yield
                                kb.tt("dve", T2, T2, onf, ALU.add)
                                yield
                                kb.tt("dve", hk(og, hp), T2, Gt, ALU.mult)
                                yield
                                yield

                            state["force"] = PB[3] if sample else PB[5]
                            for stg_ in range(10):
                                gens = []
                                if 0 <= stg_ - 1 < 8:
                                    gens.append(units_gen(stg_ - 1))
                                def chain_(stg_=stg_):
                                    if 0 <= stg_ - 2 < 8:
                                        yield from post_gen(stg_ - 2)
                                    if stg_ < 8:
                                        yield from prep_gen(stg_)
                                gens.append(chain_())
                                ug_ = gens[0] if len(gens) == 2 else None
                                cg_ = gens[-1]
                                while ug_ is not None or cg_ is not None:
                                    if ug_ is not None:
                                        try:
                                            next(ug_)
                                        except StopIteration:
                                            ug_ = None
                                    for _ in range(CHAIN_PULL if ug_ is not None else 1000000):
                                        if cg_ is None:
                                            break
                                        try:
                                            next(cg_)
                                        except StopIteration:
                                            cg_ = None
                            state["force"] = None

                            if sample or last_prompt:
                                dst = o_shs if sample else o_shp
                                nr = nseq
                                for c0 in range(0, RKP, 512):
                                    ncol = min(512, RKP - c0)
                                    pb = bank()
                                    for gi, (g0, msz) in enumerate(SH_GROUPS):
                                        if c0 <= g0 < c0 + 512:
                                            kb.tr(pb.v(g0 - c0, [[1, msz]], 0, nr), ZL.v(gi * nseq, [[1, nseq]], 0, msz),
                                                  identF.v(0, [[1, msz]], 0, msz))
                                    kb.evac(rows_stg.v(0, [[1, ncol]], 0, nr), pb.v(0, [[1, ncol]], 0, nr))
                                    kb.dma("sp", dst[l, :, c0:c0 + ncol], rows_stg.v(0, [[1, ncol]], 0, nr).ap, "rowstg", rd=[rows_stg.v()])
                            if sample or last_prompt:
                                for s in range(nseq):
                                    for hh in range(2):
                                        pb = bank()
                                        for q in range(4):
                                            hp = hh * 4 + q
                                            src = HS.v(s * 64 + hp * 16 * 64, [[1, 64]]) if sample else HSTP.v((l * 8 + hp) * 64, [[1, 64]])
                                            kb.tr(pb.v(q * 128, [[1, 128]], 0, 64), src, identF.v())
                                        wi_ = (s * 2 + hh) % NWS
                                        wst_ = wk_stg[wi_]
                                        kb.evac(wst_.v(0, [[1, 512]], 0, 64), pb.v(0, [[1, 512]], 0, 64))
                                        d_ = (o_wkvs[l, s] if sample else o_wkvp[l])[hh * 8:hh * 8 + 8].rearrange("h i j -> i h j")
                                        kb.dma("sp", d_, wst_.v(0, [[64, 8], [1, 64]], 0, 64).ap, wk_key[wi_], rd=[wst_.v()])

                        MS = BR
                        macc = kb.sb(MS, "macc", 128, 8 * NT, F32)
                        mbf = kb.sb(MS, "mbf", 128, 8 * NT, BF16)
                        tmpA = kb.sb(MS, "tmpA", 128, NT, F32)
                        tmpB = kb.sb(MS, "tmpB", 128, max(NT, 512), F32)
                        cptbox["t"] = tmpB

                        def merge(xin, nkx, wmat, gate_off, bias_key, first, last):
                            for half in range(2):
                                wy = wload(wmat, 0, nkx, half * 512, 512) if nkx == 8 else None
                                if wy is None and half == 0:
                                    wy_full = wload(wmat, 0, nkx, 0, 1024)
                                wg = wload(w_in, 0, 8, GATE_OFF + gate_off + half * 512, 512)
                                for j in range(4):
                                    ot = half * 4 + j
                                    py = bank()
                                    pg = bank()
                                    if wy is not None:
                                        proj(py.v(0, [[1, NT]]), wy, j * 128, 128, nkx, xin)
                                    else:
                                        proj(py.v(0, [[1, NT]]), wy_full, ot * 128, 128, nkx, xin)
                                    proj(pg.v(0, [[1, NT]]), wg, j * 128, 128, 8, xn)
                                    kb.act(tmpA.v(), pg.v(0, [[1, NT]]), AF.Sigmoid)
                                    b = pv(l, bias_key, ot) if bias_key else 0.0
                                    if first:
                                        kb.stt(hk(macc, ot), py.v(0, [[1, NT]]), b, tmpA.v(), ALU.add, ALU.mult)
                                    else:
                                        kb.stt(tmpB.v(0, [[1, NT]]), py.v(0, [[1, NT]]), b, tmpA.v(), ALU.add, ALU.mult)
                                        kb.tt("dve", hk(mbf if last else macc, ot), hk(macc, ot), tmpB.v(0, [[1, NT]]), ALU.add)

                        kb.barrier()
                        merge(og, 8, W["rk_w_out"][l], D, None, True, False)

                        with ExitStack() as st:
                            gA = kb.sb(st, "gA", 128, 4 * NT, BF16)
                            cub = kb.sb(st, "cub", 128, 4 * nseq * (2 + T), F32)
                            csb = kb.sb(st, "csb", 128, NT, F32)
                            cv = kb.sb(st, "cv", 128, NT, F32)
                            if sample:
                                kb.dma("sp", rows_stg.v(0, [[1, 512]], 0, 32).ap, ssc[l], "rowstg", wr=[rows_stg.v()])
                                pb = bank()
                                for ct in range(4):
                                    kb.tr(pb.v(ct * 32, [[1, 32]]), rows_stg.v(ct * 128, [[1, 128]], 0, 32), identF.v(0, [[1, 32]], 0, 32))
                                    kb.evac(cub.v(ct * nseq * (2 + T), [[2 + T, 16], [1, 2]]), pb.v(ct * 32, [[2, 16], [1, 2]]))
                            else:
                                for ct in range(4):
                                    kb.copy("dve", cub.v(ct * (2 + T), [[1, 2]]), CSC.v((l * 4 + ct) * 2, [[1, 2]]))
                            wz = [wload(w_in, 0, 8, SC_OFF + i * 512, 512) for i in range(3)]
                            for ct in range(4):
                                pbb, pc, pu = bank(), bank(), bank()
                                proj(pbb.v(0, [[1, NT]]), wz[0], ct * 128, 128, 8, xn)
                                proj(pc.v(0, [[1, NT]]), wz[1], ct * 128, 128, 8, xn)
                                proj(pu.v(0, [[1, NT]]), wz[2], ct * 128, 128, 8, xn)
                                kb.copy("act", csb.v(), pc.v(0, [[1, NT]]))
                                base = ct * nseq * (2 + T)
                                cur = cub.v(base + 2, [[2 + T, nseq], [1, T]])
                                kb.tt("dve", cur, csb.v(0, [[T, nseq], [1, T]]), pu.v(0, [[T, nseq], [1, T]]), ALU.mult)
                                cvv = cv.v(0, [[T, nseq], [1, T]])
                                kb.ts("dve", cvv, cub.v(base + 0, [[2 + T, nseq], [1, T]]), pv(l, "sc_conv_w", 0 * 4 + ct), ALU.mult)
                                kb.stt(cvv, cub.v(base + 1, [[2 + T, nseq], [1, T]]), pv(l, "sc_conv_w", 1 * 4 + ct), cvv, ALU.mult, ALU.add)
                                kb.stt(cvv, cur, pv(l, "sc_conv_w", 2 * 4 + ct), cvv, ALU.mult, ALU.add)
                                kb.tt("dve", hk(gA, ct), cv.v(), pbb.v(0, [[1, NT]]), ALU.mult)
                                if not sample:
                                    kb.copy("dve", CSC.v((l * 4 + ct) * 2, [[1, 2]]), cub.v(base + T, [[1, 2]]))
                            if sample:
                                fm_rows_out([cub.v(ct * nseq * (2 + T) + T, [[2 + T, 16], [1, 2]]) for ct in range(4)],
                                            o_scs[l], 32, rows_stg)
                            elif last_prompt:
                                fm_rows_out([cub.v(ct * (2 + T) + T, [[1, 2]]) for ct in range(4)], o_scp[l], 2, rows_stg)
                            merge(gA, 4, W["sc_w_out"][l], 0, None, False, False)

                        with ExitStack() as st:
                            gC = kb.sb(st, "gC", 128, 4 * NT, BF16)
                            HT = 30 + T
                            glu = kb.sb(st, "glu", 128, 4 * nseq * HT, F32)
                            glb = kb.sb(st, "glb", 128, 4 * nseq * HT, BF16)
                            cc = kb.sb(st, "cc", 128, 4 * NT, F32)
                            ccq = kb.sb(st, "ccq", 128, 4 * NT, F32)
                            dg = kb.sb(st, "dg", 128, 31 * 128, BF16)
                            sgt = kb.sb(st, "sgt", 128, NT, F32)
                            mean = kb.sb(st, "mean", 128, NT, F32)
                            rs2 = kb.sb(st, "rs2", 128, NT, F32)
                            if sample:
                                for q in range(4):
                                    kb.dma("sp", rows_stg.v(0, [[1, 512]], 0, 120).ap, scf[l, q * 120:(q + 1) * 120, :], "rowstg",
                                           wr=[rows_stg.v()])
                                    pb = bank()
                                    for ct in range(4):
                                        kb.tr(pb.v(ct * 120, [[1, 120]]), rows_stg.v(ct * 128, [[1, 128]], 0, 120),
                                              identF.v(0, [[1, 120]], 0, 120))
                                        kb.evac(glu.v((ct * nseq + q * 4) * HT, [[HT, 4], [1, 30]]), pb.v(ct * 120, [[30, 4], [1, 30]]))
                            else:
                                for ct in range(4):
                                    kb.copy("dve", glu.v(ct * HT, [[1, 30]]), CCF.v((l * 4 + ct) * 30, [[1, 30]]))
                            wz = [wload(w_in, 0, 8, CF_OFF + i * 512, 512) for i in range(2)]
                            for ct in range(4):
                                p1, p2 = bank(), bank()
                                proj(p1.v(0, [[1, NT]]), wz[0], ct * 128, 128, 8, xn)
                                proj(p2.v(0, [[1, NT]]), wz[1], ct * 128, 128, 8, xn)
                                kb.act(sgt.v(), p2.v(0, [[1, NT]]), AF.Sigmoid, bias=pv(l, "cf_b_in", 4 + ct))
                                base = ct * nseq * HT
                                cur = glu.v(base + 30, [[HT, nseq], [1, T]])
                                kb.stt(cur, p1.v(0, [[T, nseq], [1, T]]), pv(l, "cf_b_in", ct), sgt.v(0, [[T, nseq], [1, T]]), ALU.add, ALU.mult)
                                kb.copy("act", glb.v(base, [[1, nseq * HT]]), glu.v(base, [[1, nseq * HT]]))
                                for k_ in range(31):
                                    kb.ts("dve", dg.v(k_ * 128, [[1, 128]]), identB.v(), pv(l, "cf_dw_w", k_ * 4 + ct), ALU.mult)
                                pcv = bank()
                                for k_ in range(31):
                                    kb.mm(pcv.v(0, [[T, nseq], [1, T]]), dg.v(k_ * 128, [[1, 128]]),
                                          glb.v(base + k_, [[HT, nseq], [1, T]]), start=(k_ == 0), stop=(k_ == 30))
                                kb.act(hk(cc, ct), pcv.v(0, [[1, NT]]), AF.Identity, bias=pv(l, "cf_dw_b", ct))
                                if not sample:
                                    kb.copy("dve", CCF.v((l * 4 + ct) * 30, [[1, 30]]), glu.v(base + T, [[1, 30]]))
                            if sample:
                                for q in range(4):
                                    fm_rows_out([glu.v((ct * nseq + q * 4) * HT + 4, [[HT, 4], [1, 30]]) for ct in range(4)],
                                                o_cfs[l, q * 120:(q + 1) * 120, :], 120, rows_stg)
                            elif last_prompt:
                                fm_rows_out([glu.v(ct * HT + T, [[1, 30]]) for ct in range(4)], o_cfp[l], 30, rows_stg)
                            kb.act(ccq.v(), cc.v(), AF.Square)
                            ps1, ps2 = bank(), bank()
                            for ct in range(4):
                                kb.mm(ps1.v(0, [[1, NT]]), onesF.v(), hk(cc, ct), start=(ct == 0), stop=(ct == 3))
                            for ct in range(4):
                                kb.mm(ps2.v(0, [[1, NT]]), onesF.v(), hk(ccq, ct), start=(ct == 0), stop=(ct == 3))
                            kb.act(mean.v(), ps1.v(0, [[1, NT]]), AF.Copy, scale=1.0 / DCF)
                            kb.tt("dve", rs2.v(), mean.v(), mean.v(), ALU.mult)
                            kb.stt(rs2.v(), ps2.v(0, [[1, NT]]), 1.0 / DCF, rs2.v(), ALU.mult, ALU.subtract)
                            kb.act(rs2.v(), rs2.v(), AF.Sqrt, bias=con(3))
                            kb.op("dve", lambda e: e.reciprocal(out=rs2.v().ap, in_=rs2.v().ap), rd=[rs2.v()], wr=[rs2.v()])
                            for ct in range(4):
                                kb.tt("dve", sgt.v(), hk(cc, ct), mean.v(), ALU.subtract)
                                kb.tt("dve", sgt.v(), sgt.v(), rs2.v(), ALU.mult)
                                kb.act(hk(gC, ct), sgt.v(), AF.Silu, bias=pv(l, "cf_ln_b", ct), scale=pv(l, "cf_ln_g", ct))
                            merge(gC, 4, W["cf_w_out"][l], 2 * D, "cf_b_out", False, True)

                        for half in range(2):
                            wo = wload(W["w_o"][l], 0, 8, half * 512, 512)
                            for j in range(4):
                                ot = half * 4 + j
                                po = bank()
                                proj(po.v(0, [[1, NT]]), wo, j * 128, 128, 8, mbf)
                                kb.tt("dve", hk(h, ot), hk(h, ot), po.v(0, [[1, NT]]), ALU.add)
                    kb.barrier()

                    with ExitStack() as st:
                        f = kb.sb(st, "f", 128, 32 * NT, BF16)
                        rl = kb.sb(st, "rl", 128, NT, F32)
                        rmsnorm_to(xn, "norm2_g", l)
                        for c in range(8):
                            w1 = wload(W["mlp_w1"][l], 0, 8, c * 512, 512)
                            for j in range(4):
                                ft = c * 4 + j
                                pf = bank()
                                proj(pf.v(0, [[1, NT]]), w1, j * 128, 128, 8, xn)
                                kb.act(rl.v(), pf.v(0, [[1, NT]]), AF.Relu)
                                kb.tt("dve", hk(f, ft), rl.v(), pf.v(0, [[1, NT]]), ALU.mult)
                        for og_ in range(2):
                            pbs = [PB[0], PB[1], PB[2], PB[3]]
                            for kg in range(4):
                                w2 = wload(W["mlp_w2"][l], kg * 8, 8, og_ * 512, 512)
                                for j in range(4):
                                    for kt in range(8):
                                        kb.mm(pbs[j].v(0, [[1, NT]]), w2(kt, j * 128, 128), hk(f, kg * 8 + kt),
                                              start=(kg == 0 and kt == 0), stop=(kg == 3 and kt == 7))
                            for j in range(4):
                                ot = og_ * 4 + j
                                kb.tt("dve", hk(h, ot), hk(h, ot), pbs[j].v(0, [[1, NT]]), ALU.add)
                    kb.barrier()

                    with ExitStack() as st:
                        pst = kb.sb(st, "pst", 128, DPLE, F32)
                        pT = kb.sb(st, "pT", 128, 2 * NT, BF16)
                        sg2 = kb.sb(st, "sg2", 128, NT, F32)
                        p_rows = psm[l] if sample else pp[l, t0:t0 + TP]
                        for r0 in range(0, NT, 128):
                            nr = min(128, NT - r0)
                            kb.dma("sp", pst.v(0, [[1, DPLE]], 0, nr).ap, p_rows[r0:r0 + nr, :], "pst", wr=[pst.v()])
                            pb = bank()
                            for j in range(2):
                                kb.tr(pb.v(j * 128, [[1, nr]]), pst.v(j * 128, [[1, 128]], 0, nr), identF.v(0, [[1, nr]], 0, nr))
                            kb.evac(pT.v(r0, [[NT, 2], [1, nr]]), pb.v(0, [[128, 2], [1, nr]]))
                        rmsnorm_to(xn, "ple_norm_g", l)
                        wple = wload(W["ple_w"][l], 0, 2, 0, 1024)
                        for half in range(2):
                            wg = wload(W["ple_gate_w"][l], 0, 8, half * 512, 512)
                            for j in range(4):
                                ot = half * 4 + j
                                pg, pq = bank(), bank()
                                proj(pg.v(0, [[1, NT]]), wg, j * 128, 128, 8, xn)
                                proj(pq.v(0, [[1, NT]]), wple, ot * 128, 128, 2, pT)
                                kb.act(sg2.v(), pg.v(0, [[1, NT]]), AF.Sigmoid)
                                kb.tt("dve", sg2.v(), sg2.v(), pq.v(0, [[1, NT]]), ALU.mult)
                                kb.tt("dve", hk(h, ot), hk(h, ot), sg2.v(), ALU.add)
                    kb.barrier()

                with ExitStack() as st:
                    yfm = kb.sb(st, "yfm", 128, 8 * NT, F32)
                    yst = [kb.sb(st, "yst%d" % i, 128, D, F32) for i in range(2)]
                    kb.act(xn.v(), h.v(), AF.Square)
                    pb = bank()
                    for kt in range(8):
                        kb.mm(pb.v(0, [[1, NT]]), onesB.v(), hk(xn, kt), start=(kt == 0), stop=(kt == 7))
                    kb.act(rstd.v(), pb.v(0, [[1, NT]]), AF.Sqrt, bias=con(2), scale=1.0 / D)
                    kb.op("dve", lambda e: e.reciprocal(out=rstd.v().ap, in_=rstd.v().ap), rd=[rstd.v()], wr=[rstd.v()])
                    for kt in range(8):
                        kb.stt(hk(yfm, kt), hk(h, kt), finalg.v(kt, [[1, 1]]), rstd.v(), ALU.mult, ALU.mult)
                    for gi, r0 in enumerate(range(0, NT, 128)):
                        nr = min(128, NT - r0)
                        yt = yst[gi % 2]
                        for half in range(2):
                            pb = bank()
                            for j in range(4):
                                kt = half * 4 + j
                                kb.tr(pb.v(j * 128, [[1, 128]], 0, nr), yfm.v(kt * NT + r0, [[1, nr]]), identF.v())
                            kb.evac(yt.v(half * 512, [[1, 512]], 0, nr), pb.v(0, [[1, 512]], 0, nr))
                        kb.dma("sp", y_rows[r0:r0 + nr, :], yt.v(0, [[1, D]], 0, nr).ap, "yst%d" % (gi % 2), rd=[yt.v()])
                kb.barrier()

        for p_ in range(cfg.NPP):
            wstate["first"] = p_ == 0
            wstate["n"] = 0
            run_pass("P", p_ * TP)
            if p_ == 0:
                wstate["first"] = False
                wstate["n"] = 0
                run_pass("S", 0)
        kb.barrier()
    return nc


_OUT_NAMES = ["yp", "ys", "o_scp", "o_shp", "o_wkvp", "o_cfp", "o_scs", "o_shs", "o_wkvs", "o_cfs"]


def run(cfg, inputs):
    L, SEQ = cfg.L, cfg.SEQ
    nc = build(cfg)
    f32 = lambda a: np.ascontiguousarray(np.asarray(a, dtype=np.float32))
    in_maps = []
    wkeys = ["norm1_g", "w_in", "sc_conv_w", "sc_w_out", "rk_mu", "rk_w0", "rk_w2", "rk_a0", "rk_a2", "rk_v0", "rk_v2",
             "rk_g2", "rk_k_k", "rk_k_a", "rk_ln_g", "rk_ln_b", "rk_w_out", "cf_b_in", "cf_dw_w", "cf_dw_b", "cf_ln_g",
             "cf_ln_b", "cf_w_out", "cf_b_out", "w_o", "norm2_g", "mlp_w1", "mlp_w2", "ple_w", "ple_gate_w", "ple_norm_g"]
    shared = {k: f32(inputs[k]) for k in wkeys}
    shared["rk_r_k"] = f32(inputs["rk_r_k"]).reshape(L, D)
    shared["final_norm_g"] = f32(inputs["final_norm_g"]).reshape(1, D)
    for c in range(NCORES):
        b0 = c * 16
        m = dict(shared)
        m["xp"] = f32(inputs["x_prompt"][c])
        m["xs"] = f32(inputs["x_sample"][b0:b0 + 16]).reshape(64, D)
        m["pp"] = f32(inputs["p_prompt"][:, c])
        m["ps"] = f32(inputs["p_sample"][:, b0:b0 + 16]).reshape(L, 64, DPLE)
        m["ssc"] = f32(inputs["state_sconv"][:, b0:b0 + 16]).reshape(L, 32, DSC)
        m["ssh"] = f32(inputs["state_shift"][:, b0:b0 + 16])
        m["swkv"] = f32(inputs["state_wkv"][:, b0:b0 + 16])
        m["scf"] = f32(inputs["state_cconv"][:, b0:b0 + 16]).reshape(L, 480, DCF)
        in_maps.append(m)
    res = run_bass_kernel_spmd(nc, in_maps, core_ids=list(range(NCORES)))
    R = res.results
    y_prompt = np.stack([R[c]["yp"] for c in range(NCORES)], 0)
    y_sample = np.concatenate([R[c]["ys"].reshape(16, 4, D) for c in range(NCORES)], 0)
    sc_p = np.stack([R[c]["o_scp"] for c in range(NCORES)], 1)
    sh_p = np.stack([R[c]["o_shp"].reshape(L, RKP) for c in range(NCORES)], 1)
    wkv_p = np.stack([R[c]["o_wkvp"] for c in range(NCORES)], 1)
    cf_p = np.stack([R[c]["o_cfp"] for c in range(NCORES)], 1)
    sc_s = np.concatenate([R[c]["o_scs"].reshape(L, 16, 2, DSC) for c in range(NCORES)], 1)
    sh_s = np.concatenate([R[c]["o_shs"] for c in range(NCORES)], 1)
    wkv_s = np.concatenate([R[c]["o_wkvs"] for c in range(NCORES)], 1)
    cf_s = np.concatenate([R[c]["o_cfs"].reshape(L, 16, 30, DCF) for c in range(NCORES)], 1)
    outs = (y_prompt, y_sample, sc_p, sh_p, wkv_p, cf_p, sc_s, sh_s, wkv_s, cf_s)
    return tuple(np.ascontiguousarray(o, dtype=np.float32) for o in outs)


def kernel(**inputs):
    return run(Cfg(L=4, SEQ=2048, TP=512), inputs)
```

```python
from contextlib import ExitStack
import numpy as np
import concourse.bass as bass
import concourse.mybir as mybir
from concourse.bass_utils import run_bass_kernel_spmd

F32 = mybir.dt.float32
BF16 = mybir.dt.bfloat16
AF = mybir.ActivationFunctionType
ALU = mybir.AluOpType

D = 1024
KT = 8
DSC = 512
DCF = 512
DFF = 4096
DPLE = 256
NIN = 8992
RKP = 3360
H = 16
NCORES = 8
SC_OFF, RK_OFF, CF_OFF, GATE_OFF = 0, 1536, 4896, 5920
R_OFF, K_OFF, V_OFF, WLO_OFF, ALO_OFF, VLO_OFF, GLO_OFF = 0, 1024, 2048, 3072, 3136, 3200, 3232
RMS_EPS, LN_EPS, GN_EPS = 1e-6, 1e-5, 64e-5
EXPM05 = float(np.exp(-0.5))
EPOCH = 30000
import os
SCAN_SEQ = os.environ.get('SCAN_SEQ', '0') == '1'
NBANKROT = int(os.environ.get('NBANKROT', '6'))
PSUM_EXCL_READ = os.environ.get('PSUM_EXCL_READ', '1') == '1'
POOL_ELEM = os.environ.get('POOL_ELEM', '0') == '1'
CHAIN_PULL = int(os.environ.get('CHAIN_PULL', '2'))

SH_GROUPS = [(i * 128, 128) for i in range(24)] + [(WLO_OFF, 64), (ALO_OFF, 64), (VLO_OFF, 32), (GLO_OFF, 128)]
NSHG = len(SH_GROUPS)

PV_SPEC = [
    ("norm1_g", 8), ("sc_conv_w", 12), ("mu_rkv", 24), ("mu_wlo", 1), ("mu_alo", 1), ("mu_vlo", 1), ("mu_glo", 1),
    ("rk_w0", 8), ("rk_a0", 8), ("rk_v0", 8), ("rk_k_k", 8), ("rk_k_a", 8), ("rk_r_k", 8), ("rk_ln_g", 8),
    ("rk_ln_b", 8), ("cf_b_in", 8), ("cf_dw_w", 124), ("cf_dw_b", 4), ("cf_ln_g", 4), ("cf_ln_b", 4),
    ("cf_b_out", 8), ("norm2_g", 8), ("ple_norm_g", 8), ("omka", 8),
]
PV_OFF = {}
_o = 0
for _k, _n in PV_SPEC:
    PV_OFF[_k] = _o
    _o += _n
NPV = _o


class Dep:
    __slots__ = ("w", "r")

    def __init__(self):
        self.w = None
        self.r = {}


class DSem:
    def __init__(self, sem, glob=False):
        self.sem = sem
        self.val = 0
        self.glob = glob


class Tile:
    def __init__(self, kb, handle, P, F, dtype):
        self.kb = kb
        self.t = handle
        self.P = P
        self.F = F
        self.dtype = dtype
        self.regs = {}
        self.psum = False
        self.glob = False

    def v(self, off=0, dims=None, p0=0, np_=None):
        if np_ is None:
            np_ = self.P - p0
        if dims is None:
            dims = [[1, self.F - off]]
        ext = 1
        for s, c in dims:
            ext += abs(s) * (c - 1)
        ap = bass.AP(self.t, p0 * self.F + off, [[self.F, np_]] + [list(d) for d in dims])
        return View(self, ap, off, off + ext)


class View:
    __slots__ = ("tile", "ap", "c0", "c1")

    def __init__(self, tile, ap, c0, c1):
        self.tile = tile
        self.ap = ap
        self.c0 = c0
        self.c1 = c1


class KB:
    ENG = ("pe", "act", "dve", "pool", "sp")

    def __init__(self, nc):
        self.nc = nc
        self.eng = {"pe": nc.tensor, "act": nc.scalar, "dve": nc.vector, "pool": nc.gpsimd, "sp": nc.sync}
        self.cnt = {e: 0 for e in self.ENG}
        self.sems = {e: [] for e in self.ENG}
        self.known = {e: {} for e in self.ENG}
        self.stack = ExitStack()
        self.dsems = []
        self.dsem_by_key = {}
        self.nuid = 0
        self.rr = 0
        self.pending = {}

    def uid(self, s):
        self.nuid += 1
        return "%s_%d" % (s, self.nuid)

    def sb(self, stack, name, P, F, dtype):
        h = stack.enter_context(self.nc.sbuf_tensor(self.uid(name), [P, F], dtype))
        return Tile(self, h, P, F, dtype)

    def ps(self, stack, name, F, dtype):
        h = stack.enter_context(self.nc.psum_tensor(self.uid(name), [128, F], dtype))
        t = Tile(self, h, 128, F, dtype)
        t.psum = True
        return t

    def new_dsem(self, name):
        if name in self.dsem_by_key:
            return self.dsem_by_key[name]
        s = self.stack.enter_context(self.nc.semaphore(self.uid(name)))
        d = DSem(s, glob=name.startswith(("wsem", "wback")))
        self.dsems.append(d)
        self.dsem_by_key[name] = d
        return d

    def _sem(self, e, ep):
        while len(self.sems[e]) <= ep:
            self.sems[e].append(self.stack.enter_context(self.nc.semaphore(self.uid("s" + e))))
        return self.sems[e][ep]

    def _deps(self, view):
        t = view.tile
        key = (0, t.F) if t.psum else (view.c0, view.c1)
        own = t.regs.get(key)
        if own is None:
            own = t.regs[key] = Dep()
        over = [d for (a, b), d in t.regs.items() if a < view.c1 and view.c0 < b]
        return own, over

    def _wait_tok(self, e, tok):
        if tok[0] == "e":
            _, f, n = tok
            if self.known[e].get(f, 0) >= n:
                return
            self.known[e][f] = n
            ep = (n - 1) // EPOCH
            self.eng[e].wait_ge(self._sem(f, ep), (n - 1) % EPOCH + 1)
        else:
            _, ds, val = tok
            key = id(ds)
            if self.known[e].get(key, 0) >= val:
                return
            self.known[e][key] = val
            self.eng[e].wait_ge(ds.sem, val)

    def _hazards(self, e, rd, wr):
        toks = []
        rdo, wro = [], []
        for v in rd:
            own, over = self._deps(v)
            rdo.append(own)
            for d in over:
                if d.w is not None:
                    toks.append(d.w)
                if v.tile.psum and PSUM_EXCL_READ:
                    for tk in d.r.values():
                        if not (tk[0] == "e" and tk[1] == e):
                            toks.append(tk)
        for v in wr:
            own, over = self._deps(v)
            wro.append(own)
            for d in over:
                if d.w is not None and not (e == "pe" and d.w[0] == "e" and d.w[1] == e):
                    toks.append(d.w)
                for tk in d.r.values():
                    toks.append(tk)
        for tk in toks:
            self._wait_tok(e, tk)
        return rdo, wro

    def op(self, e, fn, rd=(), wr=()):
        rdo, wro = self._hazards(e, rd, wr)
        ins = fn(self.eng[e])
        n = self.cnt[e] = self.cnt[e] + 1
        ins.then_inc(self._sem(e, (n - 1) // EPOCH), 1)
        tok = ("e", e, n)
        for d in rdo:
            d.r[e] = tok
        for d in wro:
            d.w = tok
            d.r = {}
        return ins

    def dma(self, q, out_ap, in_ap, ds, rd=(), wr=()):
        if isinstance(ds, str):
            ds = self.new_dsem(ds)
        rdo, wro = self._hazards(q, rd, wr)
        ins = self.eng[q].dma_start(out=out_ap, in_=in_ap)
        ds.val += 16
        ins.then_inc(ds.sem, 16)
        tok = ("d", ds, ds.val)
        for d in rdo:
            d.r[("d", id(ds))] = tok
        for d in wro:
            d.w = tok
            d.r = {}
        return ins

    def barrier(self):
        for e in self.ENG:
            for f in self.ENG:
                if f != e and self.cnt[f] > 0:
                    self._wait_tok(e, ("e", f, self.cnt[f]))
            for ds in self.dsems:
                if ds.val > 0:
                    self._wait_tok(e, ("d", ds, ds.val))

    def mm(self, out, lhsT, rhs, start=True, stop=True):
        self.op("pe", lambda e: e.matmul(out.ap, lhsT.ap, rhs.ap, start=start, stop=stop), rd=[lhsT, rhs], wr=[out])

    def tr(self, out, in_, ident):
        self.op("pe", lambda e: e.transpose(out.ap, in_.ap, ident.ap), rd=[in_, ident], wr=[out])

    def act(self, out, in_, func, bias=None, scale=1.0):
        rd = [in_]
        kw = {}
        if isinstance(bias, View):
            rd.append(bias)
            kw["bias"] = bias.ap
        elif bias is not None:
            kw["bias"] = bias
        if isinstance(scale, View):
            rd.append(scale)
            kw["scale"] = scale.ap
        else:
            kw["scale"] = scale
        if func == AF.Copy and (isinstance(bias, View) or isinstance(scale, View)):
            func = AF.Identity
        self.op("act", lambda e: e.activation(out=out.ap, in_=in_.ap, func=func, **kw), rd=rd, wr=[out])

    def tt(self, eng, out, a, b, op):
        self.op(eng, lambda e: e.tensor_tensor(out=out.ap, in0=a.ap, in1=b.ap, op=op), rd=[a, b], wr=[out])

    def ts(self, eng, out, a, s1, op0, s2=None, op1=None):
        rd = [a]
        a1 = s1
        a2 = s2
        if isinstance(s1, View):
            rd.append(s1)
            a1 = s1.ap
        if isinstance(s2, View):
            rd.append(s2)
            a2 = s2.ap
        if op1 is None:
            self.op(eng, lambda e: e.tensor_scalar(out=out.ap, in0=a.ap, scalar1=a1, scalar2=None, op0=op0), rd=rd, wr=[out])
        else:
            self.op(eng, lambda e: e.tensor_scalar(out=out.ap, in0=a.ap, scalar1=a1, scalar2=a2, op0=op0, op1=op1), rd=rd, wr=[out])

    def stt(self, out, a, s, b, op0, op1):
        rd = [a, b]
        sa = s
        if isinstance(s, View):
            rd.append(s)
            sa = s.ap
        self.op("dve", lambda e: e.scalar_tensor_tensor(out=out.ap, in0=a.ap, scalar=sa, in1=b.ap, op0=op0, op1=op1), rd=rd, wr=[out])

    def copy(self, eng, out, in_):
        if eng == "act":
            self.act(out, in_, AF.Copy)
        else:
            self.op(eng, lambda e: e.tensor_copy(out=out.ap, in_=in_.ap), rd=[in_], wr=[out])

    def evac(self, out, in_):
        self.rr ^= 1
        self.copy("act" if self.rr else "dve", out, in_)

    def memset(self, eng, out, val):
        self.op(eng, lambda e: e.memset(out.ap, val), wr=[out])


class Cfg:
    def __init__(self, L=4, SEQ=2048, TP=512):
        self.L = L
        self.SEQ = SEQ
        self.TP = TP
        self.NPP = SEQ // TP


def build(cfg):
    L, SEQ, TP = cfg.L, cfg.SEQ, cfg.TP
    nc = bass.Bass("TRN2", target_bir_lowering=False)
    kb = KB(nc)
    dram = {}

    def din(name, shape):
        dram[name] = nc.dram_tensor(name, list(shape), F32, kind="ExternalInput").ap()
        return dram[name]

    def dout(name, shape):
        dram[name] = nc.dram_tensor(name, list(shape), F32, kind="ExternalOutput").ap()
        return dram[name]

    xp = din("xp", [SEQ, D])
    xs = din("xs", [64, D])
    pp = din("pp", [L, SEQ, DPLE])
    psm = din("ps", [L, 64, DPLE])
    ssc = din("ssc", [L, 32, DSC])
    ssh = din("ssh", [L, 16, RKP])
    swkv = din("swkv", [L, 16, H, 64, 64])
    scf = din("scf", [L, 480, DCF])
    W = {}
    for name, shape in [
        ("norm1_g", [L, D]), ("w_in", [L, D, NIN]), ("sc_conv_w", [L, 3, DSC]), ("sc_w_out", [L, DSC, D]),
        ("rk_mu", [L, RKP]), ("rk_w0", [L, D]), ("rk_w2", [L, 64, D]), ("rk_a0", [L, D]), ("rk_a2", [L, 64, D]),
        ("rk_v0", [L, D]), ("rk_v2", [L, 32, D]), ("rk_g2", [L, 128, D]), ("rk_k_k", [L, D]), ("rk_k_a", [L, D]),
        ("rk_r_k", [L, D]), ("rk_ln_g", [L, D]), ("rk_ln_b", [L, D]), ("rk_w_out", [L, D, D]),
        ("cf_b_in", [L, 2 * DCF]), ("cf_dw_w", [L, 31, DCF]), ("cf_dw_b", [L, DCF]), ("cf_ln_g", [L, DCF]),
        ("cf_ln_b", [L, DCF]), ("cf_w_out", [L, DCF, D]), ("cf_b_out", [L, D]), ("w_o", [L, D, D]),
        ("norm2_g", [L, D]), ("mlp_w1", [L, D, DFF]), ("mlp_w2", [L, DFF, D]), ("ple_w", [L, DPLE, D]),
        ("ple_gate_w", [L, D, D]), ("ple_norm_g", [L, D]), ("final_norm_g", [1, D]),
    ]:
        W[name] = din(name, shape)
    yp = dout("yp", [SEQ, D])
    ys = dout("ys", [64, D])
    o_scp = dout("o_scp", [L, 2, DSC])
    o_shp = dout("o_shp", [L, 1, RKP])
    o_wkvp = dout("o_wkvp", [L, H, 64, 64])
    o_cfp = dout("o_cfp", [L, 30, DCF])
    o_scs = dout("o_scs", [L, 32, DSC])
    o_shs = dout("o_shs", [L, 16, RKP])
    o_wkvs = dout("o_wkvs", [L, 16, H, 64, 64])
    o_cfs = dout("o_cfs", [L, 480, DCF])

    G = kb.stack
    with G:
        PB = [kb.ps(G, "pb%d" % i, 512, F32) for i in range(6)]
        BFA = kb.ps(G, "bfa", 1024, BF16)
        BFB = kb.ps(G, "bfb", 1024, BF16)
        BFS = [BFA, BFB]
        state = {"pbi": 0}

        def bank():
            if state.get("force") is not None:
                return state["force"]
            state["pbi"] = (state["pbi"] + 1) % NBANKROT
            return PB[state["pbi"]]

        identF = kb.sb(G, "identF", 128, 128, F32)
        identB = kb.sb(G, "identB", 128, 128, BF16)
        onesB = kb.sb(G, "onesB", 128, 128, BF16)
        onesF = kb.sb(G, "onesF", 128, 128, F32)
        blk1 = kb.sb(G, "blk1", 128, 128, BF16)
        hm = kb.sb(G, "hm", 128, 2, F32)
        CON = kb.sb(G, "con", 128, 8, F32)
        kb.memset("pool", identF.v(), 1.0)
        kb.op("pool", lambda e: e.affine_select(out=identF.v().ap, in_=identF.v().ap, pattern=[[1, 128]],
                                                compare_op=ALU.is_equal, fill=0.0, base=0, channel_multiplier=-1),
              rd=[identF.v()], wr=[identF.v()])
        kb.copy("dve", identB.v(), identF.v())
        kb.memset("pool", onesB.v(), 1.0)
        kb.memset("pool", onesF.v(), 1.0)
        kb.memset("pool", blk1.v(), 0.0)
        kb.memset("pool", blk1.v(0, [[1, 64]], 0, 64), 1.0)
        kb.memset("pool", blk1.v(64, [[1, 64]], 64, 64), 1.0)
        kb.memset("pool", hm.v(), 0.0)
        kb.memset("pool", hm.v(0, [[1, 1]], 0, 64), 1.0)
        kb.memset("pool", hm.v(1, [[1, 1]], 64, 64), 1.0)
        for i, val in enumerate([0.0, 1.0, RMS_EPS, LN_EPS, GN_EPS]):
            kb.memset("pool", CON.v(i, [[1, 1]]), val)

        def con(i):
            return CON.v(i, [[1, 1]])

        def make_masks(GS):
            NB = 2 * GS
            mU = kb.sb(G, "mU%d" % GS, NB, 2 * NB, F32)
            mL = kb.sb(G, "mL%d" % GS, NB, NB, F32)
            kb.memset("pool", mU.v(), 1.0)
            kb.memset("pool", mL.v(), 1.0)
            kb.op("pool", lambda e: e.affine_select(out=mU.v(0, [[1, NB]]).ap, in_=mU.v(0, [[1, NB]]).ap, pattern=[[1, NB]],
                                                    compare_op=ALU.is_gt, fill=0.0, base=0, channel_multiplier=-1),
                  rd=[mU.v()], wr=[mU.v()])
            kb.op("pool", lambda e: e.affine_select(out=mU.v(NB, [[1, NB]]).ap, in_=mU.v(NB, [[1, NB]]).ap, pattern=[[1, NB]],
                                                    compare_op=ALU.is_ge, fill=0.0, base=0, channel_multiplier=-1),
                  rd=[mU.v()], wr=[mU.v()])
            kb.op("pool", lambda e: e.affine_select(out=mL.v().ap, in_=mL.v().ap, pattern=[[-1, NB]],
                                                    compare_op=ALU.is_gt, fill=0.0, base=0, channel_multiplier=1),
                  rd=[mL.v()], wr=[mL.v()])
            kb.memset("pool", mU.v(GS, [[1, GS]], 0, GS), 0.0)
            kb.memset("pool", mU.v(NB + GS, [[1, GS]], 0, GS), 0.0)
            kb.memset("pool", mL.v(0, [[1, GS]], GS, GS), 0.0)
            return mU, mL

        masks = {64: make_masks(64), 32: make_masks(32)}

        PV = kb.sb(G, "pv", 128, L * NPV, F32)
        finalg = kb.sb(G, "finalg", 128, 8, F32)

        def pv(l, key, i=0):
            return PV.v(l * NPV + PV_OFF[key] + i, [[1, 1]])

        with ExitStack() as st:
            rows = [kb.sb(st, "prow%d" % i, 128, 128, F32) for i in range(3)]
            for l in range(L):
                for rt in rows:
                    kb.memset("dve", rt.v(), 0.0)
                srcs = {
                    "norm1_g": W["norm1_g"][l].rearrange("(r c) -> r c", c=128),
                    "sc_conv_w": W["sc_conv_w"][l].rearrange("k (t c) -> (k t) c", c=128),
                    "mu_rkv": W["rk_mu"][l, 0:3072].rearrange("(r c) -> r c", c=128),
                    "mu_wlo": W["rk_mu"][l:l + 1, WLO_OFF:WLO_OFF + 64],
                    "mu_alo": W["rk_mu"][l:l + 1, ALO_OFF:ALO_OFF + 64],
                    "mu_vlo": W["rk_mu"][l:l + 1, VLO_OFF:VLO_OFF + 32],
                    "mu_glo": W["rk_mu"][l:l + 1, GLO_OFF:GLO_OFF + 128],
                    "cf_dw_w": W["cf_dw_w"][l].rearrange("k (t c) -> (k t) c", c=128),
                }
                for key in ("rk_w0", "rk_a0", "rk_v0", "rk_k_k", "rk_k_a", "rk_r_k", "rk_ln_g", "rk_ln_b", "cf_b_in",
                            "cf_dw_b", "cf_ln_g", "cf_ln_b", "cf_b_out", "norm2_g", "ple_norm_g"):
                    srcs[key] = W[key][l].rearrange("(r c) -> r c", c=128)
                r0 = 0
                for key, n in PV_SPEC:
                    if key == "omka":
                        continue
                    src = srcs[key]
                    done = 0
                    while done < n:
                        ti, ro = divmod(r0 + done, 128)
                        m = min(n - done, 128 - ro)
                        ncol = src.shape[1]
                        kb.dma("sp", rows[ti].v(0, [[1, ncol]], ro, m).ap, src[done:done + m, :], "prow%d" % ti,
                               wr=[rows[ti].v()])
                        done += m
                    r0 += n
                nrow = r0
                for ti in range(3):
                    m = min(128, nrow - ti * 128)
                    if m <= 0:
                        break
                    pb = bank()
                    kb.tr(pb.v(0, [[1, m]]), rows[ti].v(0, [[1, 128]], 0, m), identF.v(0, [[1, m]], 0, m))
                    kb.copy("dve", PV.v(l * NPV + ti * 128, [[1, m]]), pb.v(0, [[1, m]]))
                kb.ts("dve", PV.v(l * NPV + PV_OFF["omka"], [[1, 8]]), PV.v(l * NPV + PV_OFF["rk_k_a"], [[1, 8]]),
                      -1.0, ALU.mult, 1.0, ALU.add)
            kb.memset("dve", rows[0].v(), 0.0)
            kb.dma("sp", rows[0].v(0, [[1, 128]], 0, 8).ap, W["final_norm_g"][0].rearrange("(r c) -> r c", c=128), "prow0",
                   wr=[rows[0].v()])
            pb = bank()
            kb.tr(pb.v(0, [[1, 8]]), rows[0].v(0, [[1, 128]], 0, 8), identF.v(0, [[1, 8]], 0, 8))
            kb.copy("dve", finalg.v(), pb.v(0, [[1, 8]]))
            kb.barrier()

        NSLOT = 4
        SLOT_E = 4096
        wslots = [kb.sb(G, "wslot%d" % i, 128, SLOT_E, BF16) for i in range(NSLOT)]
        for t_ in wslots:
            t_.glob = True
        wsems = [kb.new_dsem("wsem%d" % i) for i in range(NSLOT)]
        wstate = {"i": 0, "n": 0, "first": True}

        scratch = {}

        def wload(mat2d, k0, nk, c0, ncols, krows=128):
            assert nk * ncols <= SLOT_E and krows == 128
            i = wstate["i"] = (wstate["i"] + 1) % NSLOT
            n = wstate["n"] = wstate["n"] + 1
            sl = wslots[i]
            dst = sl.v(0, [[ncols, nk], [1, ncols]])
            flat = sl.v(0, [[1, nk * ncols]])
            if wstate["first"]:
                src = mat2d.rearrange("(kt p) n -> p kt n", p=128)[:, k0:k0 + nk, c0:c0 + ncols]
                kb.dma("pool", dst.ap, src, wsems[i], wr=[sl.v()])
                scratch[n] = nc.dram_tensor("wscr%d" % n, [128, nk * ncols], BF16, kind="Internal").ap()
                kb.dma("sp", scratch[n], flat.ap, "wback%d" % i, rd=[sl.v()])
            else:
                kb.dma("sp", flat.ap, scratch[n], "wsemH%d" % i, wr=[sl.v()])

            def acc(kt, m0, msz):
                return sl.v(kt * ncols + m0, [[1, msz]], 0, krows)
            return acc

        HSTP = kb.sb(G, "hstp", 128, L * 8 * 64, F32)
        CSC = kb.sb(G, "csc", 128, L * 4 * 2, F32)
        CCF = kb.sb(G, "ccf", 128, L * 4 * 30, F32)
        CSH = kb.sb(G, "csh", 128, L * NSHG, F32)
        for t_ in (HSTP, CSC, CCF, CSH):
            kb.memset("dve", t_.v(), 0.0)

        def run_pass(kind, t0):
            sample = kind == "S"
            nseq, T = (16, 4) if sample else (1, TP)
            NT = nseq * T
            GS = 32 if sample else 64
            NV = 4 if sample else 64
            NB = 2 * GS
            LEV = 2 if sample else 6
            nu = 16 if sample else TP // 64
            mU, mL = masks[GS]
            last_prompt = (not sample) and (t0 + TP == SEQ)
            EP = "dve" if (wstate["first"] or not POOL_ELEM) else "pool"
            x_rows = xs if sample else xp[t0:t0 + TP]
            y_rows = ys if sample else yp[t0:t0 + TP]
            with ExitStack() as PS:
                h = kb.sb(PS, "h", 128, 8 * NT, F32)
                xn = kb.sb(PS, "xn", 128, 8 * NT, BF16)
                vf = kb.sb(PS, "vf", 128, 8 * NT, F32)
                rstd = kb.sb(PS, "rstd", 128, NT, F32)
                rm = kb.sb(PS, "rm", 128, NT, F32)
                cptbox = {}
                kb.memset("dve", rm.v(), 1.0)
                kb.memset("dve", rm.v(0, [[NV, nu], [1, 1]]), 0.0)
                HS = kb.sb(PS, "hs", 128, 8 * 16 * 64, F32) if sample else None

                def hk(t_, kt, c0=0, n=None):
                    return t_.v(kt * NT + c0, [[1, NT - c0 if n is None else n]])

                with ExitStack() as st:
                    xst = [kb.sb(st, "xst%d" % i, 128, D, F32) for i in range(2)]
                    for gi, r0 in enumerate(range(0, NT, 128)):
                        nr = min(128, NT - r0)
                        xt = xst[gi % 2]
                        kb.dma("sp", xt.v(0, [[1, D]], 0, nr).ap, x_rows[r0:r0 + nr, :], "xst%d" % (gi % 2), wr=[xt.v()])
                        for half in range(2):
                            pb = bank()
                            for j in range(4):
                                kt = half * 4 + j
                                kb.tr(pb.v(j * 128, [[1, nr]]), xt.v(kt * 128, [[1, 128]], 0, nr), identF.v(0, [[1, nr]], 0, nr))
                            kb.evac(h.v(half * 4 * NT + r0, [[NT, 4], [1, nr]]), pb.v(0, [[128, 4], [1, nr]]))
                    kb.barrier()

                def rmsnorm_to(dst, gkey, l, fin=False):
                    kb.act(dst.v(), h.v(), AF.Square)
                    pb = bank()
                    for kt in range(8):
                        kb.mm(pb.v(0, [[1, NT]]), onesB.v(), hk(dst, kt), start=(kt == 0), stop=(kt == 7))
                    kb.act(rstd.v(), pb.v(0, [[1, NT]]), AF.Sqrt, bias=con(2), scale=1.0 / D)
                    kb.op("dve", lambda e: e.reciprocal(out=rstd.v().ap, in_=rstd.v().ap), rd=[rstd.v()], wr=[rstd.v()])
                    for kt in range(8):
                        g = finalg.v(kt, [[1, 1]]) if fin else pv(l, gkey, kt)
                        kb.stt(hk(dst, kt), hk(h, kt), g, rstd.v(), ALU.mult, ALU.mult)

                def proj(out_ps, wacc, m0, msz, nk, rhs_t, start=True, stop=True):
                    for kt in range(nk):
                        kb.mm(out_ps, wacc(kt, m0, msz), hk(rhs_t, kt), start=(start and kt == 0), stop=(stop and kt == nk - 1))

                def fm_rows_out(src_views, dst_rows_ap, nrows, stg):
                    pb = bank()
                    for ct, sv in enumerate(src_views):
                        kb.copy("dve", cptbox["t"].v(ct * 128, [[1, nrows]]), sv)
                        kb.tr(pb.v(ct * 128, [[1, 128]], 0, nrows), cptbox["t"].v(ct * 128, [[1, nrows]]), identF.v())
                    kb.evac(stg.v(0, [[1, 512]], 0, nrows), pb.v(0, [[1, 512]], 0, nrows))
                    kb.dma("sp", dst_rows_ap, stg.v(0, [[1, 512]], 0, nrows).ap, "rowstg", rd=[stg.v()])

                for l in range(L):
                    w_in = W["w_in"][l]
                    with ExitStack() as BR:
                        rows_stg = kb.sb(BR, "rowstg", 128, 512, F32)
                        NWS = 4 if sample else 1
                        wk_stg = [rows_stg] + [kb.sb(BR, "wkstg%d" % i_, 64, 512, F32) for i_ in range(1, NWS)]
                        wk_key = ["rowstg"] + ["wkstg%d" % i_ for i_ in range(1, NWS)]
                        og = kb.sb(BR, "og", 128, 8 * NT, BF16)
                        rmsnorm_to(xn, "norm1_g", l)

                        with ExitStack() as st:
                            ZW = nseq * (1 + T)
                            Z = kb.sb(st, "Z", 128, ZW, F32)
                            ZL = kb.sb(st, "ZL", 128, NSHG * nseq, F32)
                            lora = {k_: kb.sb(st, "lo" + k_, 128, NT, BF16) for k_ in ("w", "a", "v", "g")}
                            names32 = ["Dd", "R", "Kx", "E", "Lc", "Ep", "Emp", "A", "SV", "T2", "KK"]
                            B32 = {n_: kb.sb(st, n_, 128, NT, F32) for n_ in names32}
                            for alias, base_ in (("Lp", "Dd"), ("KP", "E"), ("SD", "SV"), ("Tt", "SV")):
                                B32[alias] = B32[base_]
                            names16 = ["ksq", "at", "bt", "kt", "rt"]
                            B16 = {n_: kb.sb(st, n_, 128, NT, BF16) for n_ in names16}
                            HPS = []
                            for i_ in range(2):
                                P_ = dict(
                                    Em=kb.sb(st, "Em", 128, NT, F32), Vx=kb.sb(st, "Vx", 128, NT, F32), Gt=kb.sb(st, "Gt", 128, NT, F32),
                                    T2p=kb.sb(st, "T2p", 128, NT, F32), rkb=kb.sb(st, "rkb", 128, NT, BF16),
                                    vbP=kb.sb(st, "vbP", 128, nu * GS, BF16), ARB=kb.sb(st, "ARB", 128, nu * 2 * NB, BF16),
                                    BBt=kb.sb(st, "BBt", 128, nu * NB, BF16), KBt=kb.sb(st, "KBt", 128, nu * NB, BF16),
                                    Otok=kb.sb(st, "Otok", NB, nu * 64, F32), Osq=kb.sb(st, "Osq", 128, max(NT, nu * 64), F32),
                                    Onorm=kb.sb(st, "Onorm", NB, nu * 64, BF16), stats=kb.sb(st, "stats", NB, 4 * nu, F32))
                                for t_ in (P_["vbP"], P_["ARB"], P_["BBt"], P_["KBt"]):
                                    kb.memset("dve", t_.v(), 0.0)
                                HPS.append(P_)
                            NSETS = 4 if sample else 3
                            S1BANKS = [PB[0], PB[1], PB[2], PB[5]]
                            USETS = []
                            for si in range(NSETS):
                                USETS.append(dict(
                                    NA=kb.sb(st, "NA", NB, 2 * NB, BF16), AK=kb.sb(st, "AK", NB, 2 * NB, BF16),
                                    M0=kb.sb(st, "M0", NB, NB, BF16),
                                    SP=[kb.sb(st, "SP", NB, 3 * NB, BF16) for _ in range(2)],
                                    Vs=kb.sb(st, "Vs", NB, 64, BF16),
                                    Xb=kb.sb(st, "Xb", NB, 64, BF16), Ub=kb.sb(st, "Ub", NB, 64, BF16),
                                    BKT=kb.sb(st, "BKT", NB, 256, BF16),
                                    Ht=kb.sb(st, "Ht", 128, 64, F32),
                                    S1=S1BANKS[si], si=si,
                                ))
                            HBF = [kb.sb(st, "Hbf", 128, 64, BF16) for _ in range(2)]
                            shst = kb.sb(st, "shst", 16, 512, F32) if sample else None
                            SH = kb.sb(st, "SH", 128, NSHG * 16, F32) if sample else None

                            if sample:
                                for c0 in range(0, RKP, 512):
                                    ncol = min(512, RKP - c0)
                                    kb.dma("sp", shst.v(0, [[1, ncol]]).ap, ssh[l, :, c0:c0 + ncol], "shst", wr=[shst.v()])
                                    pb = bank()
                                    for gi, (g0, msz) in enumerate(SH_GROUPS):
                                        if c0 <= g0 < c0 + 512:
                                            kb.tr(pb.v(0, [[1, 16]], 0, msz), shst.v(g0 - c0, [[1, msz]]), identF.v(0, [[1, 16]], 0, 16))
                                            kb.evac(SH.v(gi * 16, [[1, 16]], 0, msz), pb.v(0, [[1, 16]], 0, msz))
                                for s in range(16):
                                    for hh in range(2):
                                        wi_ = (s * 2 + hh) % NWS
                                        wst_ = wk_stg[wi_]
                                        kb.dma("sp", wst_.v(0, [[64, 8], [1, 64]], 0, 64).ap,
                                               swkv[l, s, hh * 8:hh * 8 + 8].rearrange("h i j -> i h j"), wk_key[wi_], wr=[wst_.v()])
                                        pb = bank()
                                        for q in range(4):
                                            kb.tr(pb.v(q * 64, [[1, 64]]), wst_.v(q * 128, [[1, 128]], 0, 64), identF.v(0, [[1, 64]], 0, 64))
                                        kb.evac(HS.v(s * 64 + hh * 4 * 16 * 64, [[16 * 64, 4], [1, 64]]), pb.v(0, [[64, 4], [1, 64]]))

                            def zproj(gi, wacc, m0, msz, dst, mukey, mui, func=None):
                                pz = bank()
                                proj(pz.v(0, [[1, NT]], 0, msz), wacc, m0, msz, 8, xn)
                                if sample:
                                    kb.copy("dve", Z.v(0, [[1 + T, 16], [1, 1]], 0, msz), SH.v(gi * 16, [[1, 16], [1, 1]], 0, msz))
                                else:
                                    kb.copy("dve", Z.v(0, [[1, 1]], 0, msz), CSH.v(l * NSHG + gi, [[1, 1]], 0, msz))
                                cur = Z.v(1, [[1 + T, nseq], [1, T]], 0, msz)
                                prev = Z.v(0, [[1 + T, nseq], [1, T]], 0, msz)
                                kb.copy("act", cur, pz.v(0, [[T, nseq], [1, T]], 0, msz))
                                Dd = B32["Dd"].v(0, [[T, nseq], [1, T]], 0, msz)
                                kb.tt(EP, Dd, prev, cur, ALU.subtract)
                                mu = PV.v(l * NPV + PV_OFF[mukey] + mui, [[1, 1]], 0, msz)
                                if func is None:
                                    kb.stt(dst.tile.v(dst.c0, [[T, nseq], [1, T]], 0, msz), Dd, mu, cur, ALU.mult, ALU.add)
                                else:
                                    kb.stt(Dd, Dd, mu, cur, ALU.mult, ALU.add)
                                    kb.act(dst, B32["Dd"].v(0, [[1, NT]], 0, msz), func)
                                kb.copy("dve", ZL.v(gi * nseq, [[1, nseq]], 0, msz), Z.v(T, [[1 + T, nseq]], 0, msz))
                                if not sample:
                                    kb.copy("dve", CSH.v(l * NSHG + gi, [[1, 1]], 0, msz), Z.v(T, [[1, 1]], 0, msz))

                            wl = wload(w_in, 0, 8, RK_OFF + WLO_OFF, RKP - WLO_OFF)
                            zproj(24, wl, 0, 64, lora["w"].v(0, [[1, NT]], 0, 64), "mu_wlo", 0, AF.Tanh)
                            zproj(25, wl, ALO_OFF - WLO_OFF, 64, lora["a"].v(0, [[1, NT]], 0, 64), "mu_alo", 0, AF.Copy)
                            zproj(26, wl, VLO_OFF - WLO_OFF, 32, lora["v"].v(0, [[1, NT]], 0, 32), "mu_vlo", 0, AF.Copy)
                            zproj(27, wl, GLO_OFF - WLO_OFF, 128, lora["g"].v(0, [[1, NT]]), "mu_glo", 0, AF.Sigmoid)
                            lw = kb.sb(st, "lw", 128, 4 * 1024, BF16)
                            for i_, (nm_, kr) in enumerate([("rk_w2", 64), ("rk_a2", 64), ("rk_v2", 32), ("rk_g2", 128)]):
                                kb.dma("pool", lw.v(i_ * 1024, [[1, 1024]], 0, kr).ap, W[nm_][l][0:kr, :], "lw", wr=[lw.v()])
                            wrkv = {}

                            def prep_gen(hp):
                                P = HPS[hp % 2]
                                ARB, BBt, KBt, vbP = P["ARB"], P["BBt"], P["KBt"], P["vbP"]
                                if hp % 4 == 0:
                                    for nm, off in (("r", R_OFF), ("k", K_OFF), ("v", V_OFF)):
                                        wrkv[nm] = wload(w_in, 0, 8, RK_OFF + off + (hp // 4) * 512, 512)
                                R, Kx, E, Lc, Lp, Ep, Emp, A, SV, SD, KK, Tt, KP, T2 = [
                                    B32[n_].v() for n_ in ["R", "Kx", "E", "Lc", "Lp", "Ep", "Emp", "A", "SV", "SD", "KK", "Tt", "KP", "T2"]]
                                Em, Gt = P["Em"].v(), P["Gt"].v()
                                Vx = hk(vf, hp) if l == 0 else P["Vx"].v()
                                m0 = (hp % 4) * 128
                                zproj(hp, wrkv["r"], m0, 128, R, "mu_rkv", hp)
                                yield
                                zproj(8 + hp, wrkv["k"], m0, 128, Kx, "mu_rkv", 8 + hp)
                                yield
                                zproj(16 + hp, wrkv["v"], m0, 128, Vx, "mu_rkv", 16 + hp)
                                yield
                                kb.act(B16["ksq"].v(), Kx, AF.Square, scale=pv(l, "rk_k_k", hp))
                                yield
                                pn = bank()
                                kb.mm(pn.v(0, [[1, NT]]), blk1.v(), B16["ksq"].v())
                                yield
                                kb.act(SD, pn.v(0, [[1, NT]]), AF.Sqrt)
                                yield
                                kb.ts("dve", SD, SD, 1e-12, ALU.max)
                                yield
                                kb.op("dve", lambda e: e.reciprocal(out=SD.ap, in_=SD.ap), rd=[SD], wr=[SD])
                                yield
                                kb.stt(KK, Kx, pv(l, "rk_k_k", hp), SD, ALU.mult, ALU.mult)
                                yield
                                yield
                                pw = bank()
                                kb.mm(pw.v(0, [[1, NT]]), lw.v(0 * 1024 + hp * 128, [[1, 128]], 0, 64), lora["w"].v(0, [[1, NT]], 0, 64))
                                yield
                                kb.act(E, pw.v(0, [[1, NT]]), AF.Sigmoid, bias=pv(l, "rk_w0", hp))
                                yield
                                pa = bank()
                                kb.mm(pa.v(0, [[1, NT]]), lw.v(1 * 1024 + hp * 128, [[1, 128]], 0, 64), lora["a"].v(0, [[1, NT]], 0, 64))
                                yield
                                kb.act(A, pa.v(0, [[1, NT]]), AF.Sigmoid, bias=pv(l, "rk_a0", hp))
                                yield
                                if l > 0:
                                    pvv = bank()
                                    kb.mm(pvv.v(0, [[1, NT]]), lw.v(2 * 1024 + hp * 128, [[1, 128]], 0, 32), lora["v"].v(0, [[1, NT]], 0, 32))
                                    kb.act(SV, pvv.v(0, [[1, NT]]), AF.Sigmoid, bias=pv(l, "rk_v0", hp))
                                    kb.tt("dve", T2, hk(vf, hp), Vx, ALU.subtract)
                                    kb.tt("dve", T2, T2, SV, ALU.mult)
                                    kb.tt("dve", Vx, Vx, T2, ALU.add)
                                kb.op("dve", lambda e: e.tensor_tensor_scan(out=Lc.ap, data0=rm.v().ap, data1=E.ap, initial=0.0,
                                                                            op0=ALU.mult, op1=ALU.add), rd=[rm.v(), E], wr=[Lc])
                                yield
                                kb.act(Ep, Lc, AF.Exp, scale=EXPM05)
                                yield
                                kb.act(Em, Lc, AF.Exp, scale=-EXPM05)
                                yield
                                kb.tt(EP, Lp, Lc, E, ALU.subtract)
                                yield
                                kb.act(Emp, Lp, AF.Exp, scale=-EXPM05)
                                yield
                                yield
                                pgt = bank()
                                kb.mm(pgt.v(0, [[1, NT]]), lw.v(3 * 1024 + hp * 128, [[1, 128]]), lora["g"].v())
                                yield
                                kb.copy("act", Gt, pgt.v(0, [[1, NT]]))
                                yield
                                yield
                                kb.ts("dve", Tt, A, pv(l, "rk_k_a", hp), ALU.mult, pv(l, "omka", hp), ALU.add)
                                yield
                                kb.tt(EP, KP, Kx, Tt, ALU.mult)
                                yield
                                kb.stt(B16["at"].v(), KK, -1.0, Emp, ALU.mult, ALU.mult)
                                yield
                                kb.tt(EP, T2, KK, A, ALU.mult)
                                yield
                                kb.tt(EP, B16["bt"].v(), T2, Ep, ALU.mult)
                                yield
                                yield
                                kb.tt(EP, B16["kt"].v(), KP, Ep, ALU.mult)
                                yield
                                kb.tt(EP, B16["rt"].v(), R, Em, ALU.mult)
                                yield
                                kb.stt(P["rkb"].v(), R, pv(l, "rk_r_k", hp), KP, ALU.mult, ALU.mult)
                                yield
                                kb.copy("act", vbP.v(0, [[GS, nu], [1, NV]]), Vx.tile.v(Vx.c0, [[NV, nu], [1, NV]]))
                                yield
                                yield
                                for src, dstt, doff in ((B16["at"], ARB, 0), (B16["rt"], ARB, NB)):
                                    for g in range(2):
                                        kb.copy(EP if g == 0 else "act",
                                                dstt.v(doff + g * GS, [[2 * NB, nu], [1, NV]], 64 * g, 64),
                                                src.v(0, [[NV, nu], [1, NV]], 64 * g, 64))
                                for src, dstt in ((B16["bt"], BBt), (B16["kt"], KBt)):
                                    for g in range(2):
                                        kb.copy(EP if g == 0 else "act",
                                                dstt.v(g * GS, [[NB, nu], [1, NV]], 64 * g, 64),
                                                src.v(0, [[NV, nu], [1, NV]], 64 * g, 64))

                                yield

                            def units_gen(hp):
                                P = HPS[hp % 2]
                                ARB, BBt, KBt, vbP, Otok = P["ARB"], P["BBt"], P["KBt"], P["vbP"], P["Otok"]

                                def part_a(u, S):
                                    si = S["si"]
                                    S1 = S["S1"]
                                    ev = "act" if si % 2 == 0 else "dve"
                                    aB = ARB.v(u * 2 * NB, [[1, NB]])
                                    arB = ARB.v(u * 2 * NB, [[1, 2 * NB]])
                                    bB = BBt.v(u * NB, [[1, NB]])
                                    kB_ = KBt.v(u * NB, [[1, NB]])
                                    NA, AK, M0 = S["NA"], S["AK"], S["M0"]
                                    P1 = S1.v(0, [[1, 2 * NB]], 0, NB)
                                    P2 = S1.v(256, [[1, 2 * NB]], 0, NB)
                                    P3 = PB[3].v(si * 128 + (64 if sample else 0), [[1, NB]], 0, NB)
                                    kb.mm(P1, bB, arB)
                                    kb.mm(P2, kB_, arB)
                                    kb.mm(P3, aB, bB)
                                    bfo = (si % 3) * 320
                                    BF = BFS[1] if si < 3 else BFS[0]
                                    for g in range(2):
                                        kb.tr(BF.v(bfo, [[1, 64]], g * GS, GS), vbP.v(u * GS, [[1, GS]], 64 * g, 64),
                                              identB.v(64 * g, [[1, 64]], 64 * g, 64))
                                    kb.tr(BF.v(bfo + 64, [[1, 128]], 0, NB), bB, identB.v())
                                    kb.tr(BF.v(bfo + 192, [[1, 128]], 0, NB), kB_, identB.v())
                                    kb.tt("dve", NA.v(), P1, mU.v(), ALU.mult)
                                    kb.tt("dve", AK.v(), P2, mU.v(), ALU.mult)
                                    kb.tt("dve", M0.v(), P3, mL.v(), ALU.mult)
                                    kb.copy("act", S["Vs"].v(), BF.v(bfo, [[1, 64]], 0, NB))
                                    kb.copy("act", S["BKT"].v(), BF.v(bfo + 64, [[1, 256]], 0, NB))
                                    yield
                                    Nk, Mk = NA.v(0, [[1, NB]]), M0.v()
                                    SP = S["SP"][1]
                                    kb.mm(S1.v(NB, [[1, NB]], 0, NB), Mk, Nk)
                                    kb.mm(S1.v(2 * NB, [[1, NB]], 0, NB), Nk, Mk)
                                    kb.tt("dve", SP.v(0, [[1, NB]]), Nk, identB.v(0, [[1, NB]], 0, NB), ALU.add)
                                    kb.copy(ev, SP.v(NB, [[1, 2 * NB]]), S1.v(NB, [[1, 2 * NB]], 0, NB))
                                    yield
                                    for hh in range(2, LEV + 1):
                                        last = hh == LEV
                                        SN, Nk, Mk = SP.v(0, [[1, NB]]), SP.v(NB, [[1, NB]]), SP.v(2 * NB, [[1, NB]])
                                        kb.mm(S1.v(0, [[1, NB]], 0, NB), identB.v(0, [[1, NB]], 0, NB), SN, start=True, stop=False)
                                        kb.mm(S1.v(0, [[1, NB]], 0, NB), Mk, SN, start=False, stop=True)
                                        SPn = S["SP"][hh % 2]
                                        if last:
                                            kb.copy(ev, SPn.v(0, [[1, NB]]), S1.v(0, [[1, NB]], 0, NB))
                                        else:
                                            kb.mm(S1.v(NB, [[1, NB]], 0, NB), Mk, Nk)
                                            kb.mm(S1.v(2 * NB, [[1, NB]], 0, NB), Nk, Mk)
                                            kb.copy(ev, SPn.v(), S1.v(0, [[1, 3 * NB]], 0, NB))
                                        SP = SPn
                                        yield
                                    S["TT"] = SP.v(0, [[1, NB]])

                                def part_b(u, S):
                                    hoff = u * 64 + hp * 16 * 64 if sample else (l * 8 + hp) * 64
                                    Hst = (HS if sample else HSTP).v(hoff, [[1, 64]])
                                    aB = ARB.v(u * 2 * NB, [[1, NB]])
                                    rB = ARB.v(u * 2 * NB + NB, [[1, NB]])
                                    NA, AK, Vs, Xb, Ub, Ht, BKT = (S[k_] for k_ in ("NA", "AK", "Vs", "Xb", "Ub", "Ht", "BKT"))
                                    Hbf = HBF[u % 2]
                                    PBX = PB[4]
                                    qo = (u % 2) * 256
                                    if sample or u == 0:
                                        kb.copy("act", Hbf.v(), Hst)
                                    PX = PBX.v(qo, [[1, 64]], 0, NB)
                                    kb.mm(PX, aB, Hbf.v(), start=True, stop=False)
                                    kb.mm(PX, AK.v(0, [[1, NB]]), Vs.v(), start=False, stop=True)
                                    kb.copy("act", Xb.v(), PX)
                                    yield
                                    PU = PBX.v(qo + 64, [[1, 64]], 0, NB)
                                    kb.mm(PU, S["TT"], Xb.v())
                                    kb.copy("act", Ub.v(), PU)
                                    yield
                                    PH = PBX.v(qo + 128, [[1, 64]])
                                    kb.mm(PH, BKT.v(0, [[1, 128]]), Ub.v(), start=True, stop=False)
                                    kb.mm(PH, BKT.v(128, [[1, 128]]), Vs.v(), start=False, stop=True)
                                    PO = PBX.v(qo + 192, [[1, 64]], 0, NB)
                                    kb.mm(PO, rB, Hbf.v(), start=True, stop=False)
                                    kb.mm(PO, NA.v(NB, [[1, NB]]), Ub.v(), start=False, stop=False)
                                    kb.mm(PO, AK.v(NB, [[1, NB]]), Vs.v(), start=False, stop=True)
                                    kb.tt("dve", Ht.v(), PH, Hst, ALU.add)
                                    wc = P["Em"].v(u * NV + NV - 1, [[1, 1]])
                                    if not sample and u + 1 < nu:
                                        kb.act(HBF[(u + 1) % 2].v(), Ht.v(), AF.Copy, scale=wc)
                                    kb.act(Hst, Ht.v(), AF.Copy, scale=wc)
                                    kb.copy("act", Otok.v(u * 64, [[1, 64]]), PO)
                                    yield

                                a_next = 0
                                b_next = 0
                                b_fin = 0
                                a_act = []
                                a_done = set()
                                b_act = []
                                nbmax = 2 if sample else 1
                                while b_fin < nu:
                                    while len(a_act) < 2 and a_next < nu and a_next < b_fin + NSETS:
                                        a_act.append((a_next, part_a(a_next, USETS[a_next % NSETS])))
                                        a_next += 1
                                    for (ua, ga) in list(a_act):
                                        try:
                                            next(ga)
                                        except StopIteration:
                                            a_act.remove((ua, ga))
                                            a_done.add(ua)
                                    while len(b_act) < nbmax and b_next < nu and b_next in a_done:
                                        b_act.append(part_b(b_next, USETS[b_next % NSETS]))
                                        b_next += 1
                                    for gb in list(b_act):
                                        try:
                                            next(gb)
                                        except StopIteration:
                                            b_act.remove(gb)
                                            b_fin += 1
                                    yield

                            def post_gen(hp):
                                P = HPS[hp % 2]
                                Otok, Osq, Onorm, stats = P["Otok"], P["Osq"], P["Onorm"], P["stats"]
                                Gt, T2 = P["Gt"].v(), P["T2p"].v()
                                onf = P["Osq"].v(0, [[1, NT]])
                                Vx = hk(vf, hp) if l == 0 else P["Vx"].v()
                                s1 = stats.v(0, [[1, nu]])
                                s2 = stats.v(nu, [[1, nu]])
                                mn = stats.v(2 * nu, [[1, nu]])
                                rsd = stats.v(3 * nu, [[1, nu]])
                                O3 = Otok.v(0, [[64, nu], [1, 64]])
                                kb.op("dve", lambda e: e.tensor_reduce(out=s1.ap, in_=O3.ap, axis=mybir.AxisListType.X, op=ALU.add),
                                      rd=[O3], wr=[s1])
                                yield
                                kb.act(Osq.v(0, [[1, nu * 64]], 0, NB), Otok.v(), AF.Square)
                                yield
                                Q3 = Osq.v(0, [[64, nu], [1, 64]], 0, NB)
                                kb.op("dve", lambda e: e.tensor_reduce(out=s2.ap, in_=Q3.ap, axis=mybir.AxisListType.X, op=ALU.add),
                                      rd=[Q3], wr=[s2])
                                yield
                                kb.ts("dve", mn, s1, 1.0 / 64, ALU.mult)
                                yield
                                kb.tt("dve", s1, mn, mn, ALU.mult)
                                yield
                                kb.stt(s2, s2, 1.0 / 64, s1, ALU.mult, ALU.subtract)
                                yield
                                kb.act(rsd, s2, AF.Sqrt, bias=CON.v(4, [[1, 1]], 0, NB))
                                yield
                                kb.op("dve", lambda e: e.reciprocal(out=rsd.ap, in_=rsd.ap), rd=[rsd], wr=[rsd])
                                yield
                                yield
                                kb.tt("dve", Osq.v(0, [[64, nu], [1, 64]], 0, NB), O3, stats.v(2 * nu, [[1, nu], [0, 64]]), ALU.subtract)
                                yield
                                kb.tt("dve", Onorm.v(0, [[64, nu], [1, 64]]), Osq.v(0, [[64, nu], [1, 64]], 0, NB),
                                      stats.v(3 * nu, [[1, nu], [0, 64]]), ALU.mult)
                                yield
                                yield
                                for u in range(nu):
                                    for g in range(2):
                                        kb.tr(BFS[0].v(512 + u * GS, [[1, GS]], 64 * g, 64), Onorm.v(u * 64, [[1, 64]], g * GS, GS),
                                              identB.v(g * GS, [[1, GS]], g * GS, GS))
                                kb.act(onf.tile.v(0, [[NV, nu], [1, NV]]), BFS[0].v(512, [[GS, nu], [1, NV]]), AF.Identity,
                                       bias=pv(l, "rk_ln_b", hp), scale=pv(l, "rk_ln_g", hp))
                                yield
                                yield
                                pbn = bank()
                                kb.mm(pbn.v(0, [[1, NT]]), blk1.v(), P["rkb"].v())
                                yield
                                kb.tt("dve", T2, pbn.v(0, [[1, NT]]), Vx, ALU.mult)
                                yield
                                kb.tt("dve", T2, T2, onf, ALU.add)
                                yield
                                kb.tt("dve", hk(og, hp), T2, Gt, ALU.mult)
                                yield
                                yield

                            state["force"] = PB[3] if sample else PB[5]
                            for stg_ in range(10):
                                gens = []
                                if 0 <= stg_ - 1 < 8:
                                    gens.append(units_gen(stg_ - 1))
                                def chain_(stg_=stg_):
                                    if 0 <= stg_ - 2 < 8:
                                        yield from post_gen(stg_ - 2)
                                    if stg_ < 8:
                                        yield from prep_gen(stg_)
                                gens.append(chain_())
                                ug_ = gens[0] if len(gens) == 2 else None
                                cg_ = gens[-1]
                                while ug_ is not None or cg_ is not None:
                                    if ug_ is not None:
                                        try:
                                            next(ug_)
                                        except StopIteration:
                                            ug_ = None
                                    for _ in range(CHAIN_PULL if ug_ is not None else 1000000):
                                        if cg_ is None:
                                            break
                                        try:
                                            next(cg_)
                                        except StopIteration:
                                            cg_ = None
                            state["force"] = None

                            if sample or last_prompt:
                                dst = o_shs if sample else o_shp
                                nr = nseq
                                for c0 in range(0, RKP, 512):
                                    ncol = min(512, RKP - c0)
                                    pb = bank()
                                    for gi, (g0, msz) in enumerate(SH_GROUPS):
                                        if c0 <= g0 < c0 + 512:
                                            kb.tr(pb.v(g0 - c0, [[1, msz]], 0, nr), ZL.v(gi * nseq, [[1, nseq]], 0, msz),
                                                  identF.v(0, [[1, msz]], 0, msz))
                                    kb.evac(rows_stg.v(0, [[1, ncol]], 0, nr), pb.v(0, [[1, ncol]], 0, nr))
                                    kb.dma("sp", dst[l, :, c0:c0 + ncol], rows_stg.v(0, [[1, ncol]], 0, nr).ap, "rowstg", rd=[rows_stg.v()])
                            if sample or last_prompt:
                                for s in range(nseq):
                                    for hh in range(2):
                                        pb = bank()
                                        for q in range(4):
                                            hp = hh * 4 + q
                                            src = HS.v(s * 64 + hp * 16 * 64, [[1, 64]]) if sample else HSTP.v((l * 8 + hp) * 64, [[1, 64]])
                                            kb.tr(pb.v(q * 128, [[1, 128]], 0, 64), src, identF.v())
                                        wi_ = (s * 2 + hh) % NWS
                                        wst_ = wk_stg[wi_]
                                        kb.evac(wst_.v(0, [[1, 512]], 0, 64), pb.v(0, [[1, 512]], 0, 64))
                                        d_ = (o_wkvs[l, s] if sample else o_wkvp[l])[hh * 8:hh * 8 + 8].rearrange("h i j -> i h j")
                                        kb.dma("sp", d_, wst_.v(0, [[64, 8], [1, 64]], 0, 64).ap, wk_key[wi_], rd=[wst_.v()])

                        MS = BR
                        macc = kb.sb(MS, "macc", 128, 8 * NT, F32)
                        mbf = kb.sb(MS, "mbf", 128, 8 * NT, BF16)
                        tmpA = kb.sb(MS, "tmpA", 128, NT, F32)
                        tmpB = kb.sb(MS, "tmpB", 128, max(NT, 512), F32)
                        cptbox["t"] = tmpB

                        def merge(xin, nkx, wmat, gate_off, bias_key, first, last):
                            for half in range(2):
                                wy = wload(wmat, 0, nkx, half * 512, 512) if nkx == 8 else None
                                if wy is None and half == 0:
                                    wy_full = wload(wmat, 0, nkx, 0, 1024)
                                wg = wload(w_in, 0, 8, GATE_OFF + gate_off + half * 512, 512)
                                for j in range(4):
                                    ot = half * 4 + j
                                    py = bank()
                                    pg = bank()
                                    if wy is not None:
                                        proj(py.v(0, [[1, NT]]), wy, j * 128, 128, nkx, xin)
                                    else:
                                        proj(py.v(0, [[1, NT]]), wy_full, ot * 128, 128, nkx, xin)
                                    proj(pg.v(0, [[1, NT]]), wg, j * 128, 128, 8, xn)
                                    kb.act(tmpA.v(), pg.v(0, [[1, NT]]), AF.Sigmoid)
                                    b = pv(l, bias_key, ot) if bias_key else 0.0
                                    if first:
                                        kb.stt(hk(macc, ot), py.v(0, [[1, NT]]), b, tmpA.v(), ALU.add, ALU.mult)
                                    else:
                                        kb.stt(tmpB.v(0, [[1, NT]]), py.v(0, [[1, NT]]), b, tmpA.v(), ALU.add, ALU.mult)
                                        kb.tt("dve", hk(mbf if last else macc, ot), hk(macc, ot), tmpB.v(0, [[1, NT]]), ALU.add)

                        kb.barrier()
                        merge(og, 8, W["rk_w_out"][l], D, None, True, False)

                        with ExitStack() as st:
                            gA = kb.sb(st, "gA", 128, 4 * NT, BF16)
                            cub = kb.sb(st, "cub", 128, 4 * nseq * (2 + T), F32)
                            csb = kb.sb(st, "csb", 128, NT, F32)
                            cv = kb.sb(st, "cv", 128, NT, F32)
                            if sample:
                                kb.dma("sp", rows_stg.v(0, [[1, 512]], 0, 32).ap, ssc[l], "rowstg", wr=[rows_stg.v()])
                                pb = bank()
                                for ct in range(4):
                                    kb.tr(pb.v(ct * 32, [[1, 32]]), rows_stg.v(ct * 128, [[1, 128]], 0, 32), identF.v(0, [[1, 32]], 0, 32))
                                    kb.evac(cub.v(ct * nseq * (2 + T), [[2 + T, 16], [1, 2]]), pb.v(ct * 32, [[2, 16], [1, 2]]))
                            else:
                                for ct in range(4):
                                    kb.copy("dve", cub.v(ct * (2 + T), [[1, 2]]), CSC.v((l * 4 + ct) * 2, [[1, 2]]))
                            wz = [wload(w_in, 0, 8, SC_OFF + i * 512, 512) for i in range(3)]
                            for ct in range(4):
                                pbb, pc, pu = bank(), bank(), bank()
                                proj(pbb.v(0, [[1, NT]]), wz[0], ct * 128, 128, 8, xn)
                                proj(pc.v(0, [[1, NT]]), wz[1], ct * 128, 128, 8, xn)
                                proj(pu.v(0, [[1, NT]]), wz[2], ct * 128, 128, 8, xn)
                                kb.copy("act", csb.v(), pc.v(0, [[1, NT]]))
                                base = ct * nseq * (2 + T)
                                cur = cub.v(base + 2, [[2 + T, nseq], [1, T]])
                                kb.tt("dve", cur, csb.v(0, [[T, nseq], [1, T]]), pu.v(0, [[T, nseq], [1, T]]), ALU.mult)
                                cvv = cv.v(0, [[T, nseq], [1, T]])
                                kb.ts("dve", cvv, cub.v(base + 0, [[2 + T, nseq], [1, T]]), pv(l, "sc_conv_w", 0 * 4 + ct), ALU.mult)
                                kb.stt(cvv, cub.v(base + 1, [[2 + T, nseq], [1, T]]), pv(l, "sc_conv_w", 1 * 4 + ct), cvv, ALU.mult, ALU.add)
                                kb.stt(cvv, cur, pv(l, "sc_conv_w", 2 * 4 + ct), cvv, ALU.mult, ALU.add)
                                kb.tt("dve", hk(gA, ct), cv.v(), pbb.v(0, [[1, NT]]), ALU.mult)
                                if not sample:
                                    kb.copy("dve", CSC.v((l * 4 + ct) * 2, [[1, 2]]), cub.v(base + T, [[1, 2]]))
                            if sample:
                                fm_rows_out([cub.v(ct * nseq * (2 + T) + T, [[2 + T, 16], [1, 2]]) for ct in range(4)],
                                            o_scs[l], 32, rows_stg)
                            elif last_prompt:
                                fm_rows_out([cub.v(ct * (2 + T) + T, [[1, 2]]) for ct in range(4)], o_scp[l], 2, rows_stg)
                            merge(gA, 4, W["sc_w_out"][l], 0, None, False, False)

                        with ExitStack() as st:
                            gC = kb.sb(st, "gC", 128, 4 * NT, BF16)
                            HT = 30 + T
                            glu = kb.sb(st, "glu", 128, 4 * nseq * HT, F32)
                            glb = kb.sb(st, "glb", 128, 4 * nseq * HT, BF16)
                            cc = kb.sb(st, "cc", 128, 4 * NT, F32)
                            ccq = kb.sb(st, "ccq", 128, 4 * NT, F32)
                            dg = kb.sb(st, "dg", 128, 31 * 128, BF16)
                            sgt = kb.sb(st, "sgt", 128, NT, F32)
                            mean = kb.sb(st, "mean", 128, NT, F32)
                            rs2 = kb.sb(st, "rs2", 128, NT, F32)
                            if sample:
                                for q in range(4):
                                    kb.dma("sp", rows_stg.v(0, [[1, 512]], 0, 120).ap, scf[l, q * 120:(q + 1) * 120, :], "rowstg",
                                           wr=[rows_stg.v()])
                                    pb = bank()
                                    for ct in range(4):
                                        kb.tr(pb.v(ct * 120, [[1, 120]]), rows_stg.v(ct * 128, [[1, 128]], 0, 120),
                                              identF.v(0, [[1, 120]], 0, 120))
                                        kb.evac(glu.v((ct * nseq + q * 4) * HT, [[HT, 4], [1, 30]]), pb.v(ct * 120, [[30, 4], [1, 30]]))
                            else:
                                for ct in range(4):
                                    kb.copy("dve", glu.v(ct * HT, [[1, 30]]), CCF.v((l * 4 + ct) * 30, [[1, 30]]))
                            wz = [wload(w_in, 0, 8, CF_OFF + i * 512, 512) for i in range(2)]
                            for ct in range(4):
                                p1, p2 = bank(), bank()
                                proj(p1.v(0, [[1, NT]]), wz[0], ct * 128, 128, 8, xn)
                                proj(p2.v(0, [[1, NT]]), wz[1], ct * 128, 128, 8, xn)
                                kb.act(sgt.v(), p2.v(0, [[1, NT]]), AF.Sigmoid, bias=pv(l, "cf_b_in", 4 + ct))
                                base = ct * nseq * HT
                                cur = glu.v(base + 30, [[HT, nseq], [1, T]])
                                kb.stt(cur, p1.v(0, [[T, nseq], [1, T]]), pv(l, "cf_b_in", ct), sgt.v(0, [[T, nseq], [1, T]]), ALU.add, ALU.mult)
                                kb.copy("act", glb.v(base, [[1, nseq * HT]]), glu.v(base, [[1, nseq * HT]]))
                                for k_ in range(31):
                                    kb.ts("dve", dg.v(k_ * 128, [[1, 128]]), identB.v(), pv(l, "cf_dw_w", k_ * 4 + ct), ALU.mult)
                                pcv = bank()
                                for k_ in range(31):
                                    kb.mm(pcv.v(0, [[T, nseq], [1, T]]), dg.v(k_ * 128, [[1, 128]]),
                                          glb.v(base + k_, [[HT, nseq], [1, T]]), start=(k_ == 0), stop=(k_ == 30))
                                kb.act(hk(cc, ct), pcv.v(0, [[1, NT]]), AF.Identity, bias=pv(l, "cf_dw_b", ct))
                                if not sample:
                                    kb.copy("dve", CCF.v((l * 4 + ct) * 30, [[1, 30]]), glu.v(base + T, [[1, 30]]))
                            if sample:
                                for q in range(4):
                                    fm_rows_out([glu.v((ct * nseq + q * 4) * HT + 4, [[HT, 4], [1, 30]]) for ct in range(4)],
                                                o_cfs[l, q * 120:(q + 1) * 120, :], 120, rows_stg)
                            elif last_prompt:
                                fm_rows_out([glu.v(ct * HT + T, [[1, 30]]) for ct in range(4)], o_cfp[l], 30, rows_stg)
                            kb.act(ccq.v(), cc.v(), AF.Square)
                            ps1, ps2 = bank(), bank()
                            for ct in range(4):
                                kb.mm(ps1.v(0, [[1, NT]]), onesF.v(), hk(cc, ct), start=(ct == 0), stop=(ct == 3))
                            for ct in range(4):
                                kb.mm(ps2.v(0, [[1, NT]]), onesF.v(), hk(ccq, ct), start=(ct == 0), stop=(ct == 3))
                            kb.act(mean.v(), ps1.v(0, [[1, NT]]), AF.Copy, scale=1.0 / DCF)
                            kb.tt("dve", rs2.v(), mean.v(), mean.v(), ALU.mult)
                            kb.stt(rs2.v(), ps2.v(0, [[1, NT]]), 1.0 / DCF, rs2.v(), ALU.mult, ALU.subtract)
                            kb.act(rs2.v(), rs2.v(), AF.Sqrt, bias=con(3))
                            kb.op("dve", lambda e: e.reciprocal(out=rs2.v().ap, in_=rs2.v().ap), rd=[rs2.v()], wr=[rs2.v()])
                            for ct in range(4):
                                kb.tt("dve", sgt.v(), hk(cc, ct), mean.v(), ALU.subtract)
                                kb.tt("dve", sgt.v(), sgt.v(), rs2.v(), ALU.mult)
                                kb.act(hk(gC, ct), sgt.v(), AF.Silu, bias=pv(l, "cf_ln_b", ct), scale=pv(l, "cf_ln_g", ct))
                            merge(gC, 4, W["cf_w_out"][l], 2 * D, "cf_b_out", False, True)

                        for half in range(2):
                            wo = wload(W["w_o"][l], 0, 8, half * 512, 512)
                            for j in range(4):
                                ot = half * 4 + j
                                po = bank()
                                proj(po.v(0, [[1, NT]]), wo, j * 128, 128, 8, mbf)
                                kb.tt("dve", hk(h, ot), hk(h, ot), po.v(0, [[1, NT]]), ALU.add)
                    kb.barrier()

                    with ExitStack() as st:
                        f = kb.sb(st, "f", 128, 32 * NT, BF16)
                        rl = kb.sb(st, "rl", 128, NT, F32)
                        rmsnorm_to(xn, "norm2_g", l)
                        for c in range(8):
                            w1 = wload(W["mlp_w1"][l], 0, 8, c * 512, 512)
                            for j in range(4):
                                ft = c * 4 + j
                                pf = bank()
                                proj(pf.v(0, [[1, NT]]), w1, j * 128, 128, 8, xn)
                                kb.act(rl.v(), pf.v(0, [[1, NT]]), AF.Relu)
                                kb.tt("dve", hk(f, ft), rl.v(), pf.v(0, [[1, NT]]), ALU.mult)
                        for og_ in range(2):
                            pbs = [PB[0], PB[1], PB[2], PB[3]]
                            for kg in range(4):
                                w2 = wload(W["mlp_w2"][l], kg * 8, 8, og_ * 512, 512)
                                for j in range(4):
                                    for kt in range(8):
                                        kb.mm(pbs[j].v(0, [[1, NT]]), w2(kt, j * 128, 128), hk(f, kg * 8 + kt),
                                              start=(kg == 0 and kt == 0), stop=(kg == 3 and kt == 7))
                            for j in range(4):
                                ot = og_ * 4 + j
                                kb.tt("dve", hk(h, ot), hk(h, ot), pbs[j].v(0, [[1, NT]]), ALU.add)
                    kb.barrier()

                    with ExitStack() as st:
                        pst = kb.sb(st, "pst", 128, DPLE, F32)
                        pT = kb.sb(st, "pT", 128, 2 * NT, BF16)
                        sg2 = kb.sb(st, "sg2", 128, NT, F32)
                        p_rows = psm[l] if sample else pp[l, t0:t0 + TP]
                        for r0 in range(0, NT, 128):
                            nr = min(128, NT - r0)
                            kb.dma("sp", pst.v(0, [[1, DPLE]], 0, nr).ap, p_rows[r0:r0 + nr, :], "pst", wr=[pst.v()])
                            pb = bank()
                            for j in range(2):
                                kb.tr(pb.v(j * 128, [[1, nr]]), pst.v(j * 128, [[1, 128]], 0, nr), identF.v(0, [[1, nr]], 0, nr))
                            kb.evac(pT.v(r0, [[NT, 2], [1, nr]]), pb.v(0, [[128, 2], [1, nr]]))
                        rmsnorm_to(xn, "ple_norm_g", l)
                        wple = wload(W["ple_w"][l], 0, 2, 0, 1024)
                        for half in range(2):
                            wg = wload(W["ple_gate_w"][l], 0, 8, half * 512, 512)
                            for j in range(4):
                                ot = half * 4 + j
                                pg, pq = bank(), bank()
                                proj(pg.v(0, [[1, NT]]), wg, j * 128, 128, 8, xn)
                                proj(pq.v(0, [[1, NT]]), wple, ot * 128, 128, 2, pT)
                                kb.act(sg2.v(), pg.v(0, [[1, NT]]), AF.Sigmoid)
                                kb.tt("dve", sg2.v(), sg2.v(), pq.v(0, [[1, NT]]), ALU.mult)
                                kb.tt("dve", hk(h, ot), hk(h, ot), sg2.v(), ALU.add)
                    kb.barrier()

                with ExitStack() as st:
                    yfm = kb.sb(st, "yfm", 128, 8 * NT, F32)
                    yst = [kb.sb(st, "yst%d" % i, 128, D, F32) for i in range(2)]
                    kb.act(xn.v(), h.v(), AF.Square)
                    pb = bank()
                    for kt in range(8):
                        kb.mm(pb.v(0, [[1, NT]]), onesB.v(), hk(xn, kt), start=(kt == 0), stop=(kt == 7))
                    kb.act(rstd.v(), pb.v(0, [[1, NT]]), AF.Sqrt, bias=con(2), scale=1.0 / D)
                    kb.op("dve", lambda e: e.reciprocal(out=rstd.v().ap, in_=rstd.v().ap), rd=[rstd.v()], wr=[rstd.v()])
                    for kt in range(8):
                        kb.stt(hk(yfm, kt), hk(h, kt), finalg.v(kt, [[1, 1]]), rstd.v(), ALU.mult, ALU.mult)
                    for gi, r0 in enumerate(range(0, NT, 128)):
                        nr = min(128, NT - r0)
                        yt = yst[gi % 2]
                        for half in range(2):
                            pb = bank()
                            for j in range(4):
                                kt = half * 4 + j
                                kb.tr(pb.v(j * 128, [[1, 128]], 0, nr), yfm.v(kt * NT + r0, [[1, nr]]), identF.v())
                            kb.evac(yt.v(half * 512, [[1, 512]], 0, nr), pb.v(0, [[1, 512]], 0, nr))
                        kb.dma("sp", y_rows[r0:r0 + nr, :], yt.v(0, [[1, D]], 0, nr).ap, "yst%d" % (gi % 2), rd=[yt.v()])
                kb.barrier()

        for p_ in range(cfg.NPP):
            wstate["first"] = p_ == 0
            wstate["n"] = 0
            run_pass("P", p_ * TP)
            if p_ == 0:
                wstate["first"] = False
                wstate["n"] = 0
                run_pass("S", 0)
        kb.barrier()
    return nc


_OUT_NAMES = ["yp", "ys", "o_scp", "o_shp", "o_wkvp", "o_cfp", "o_scs", "o_shs", "o_wkvs", "o_cfs"]


def run(cfg, inputs):
    L, SEQ = cfg.L, cfg.SEQ
    nc = build(cfg)
    f32 = lambda a: np.ascontiguousarray(np.asarray(a, dtype=np.float32))
    in_maps = []
    wkeys = ["norm1_g", "w_in", "sc_conv_w", "sc_w_out", "rk_mu", "rk_w0", "rk_w2", "rk_a0", "rk_a2", "rk_v0", "rk_v2",
             "rk_g2", "rk_k_k", "rk_k_a", "rk_ln_g", "rk_ln_b", "rk_w_out", "cf_b_in", "cf_dw_w", "cf_dw_b", "cf_ln_g",
             "cf_ln_b", "cf_w_out", "cf_b_out", "w_o", "norm2_g", "mlp_w1", "mlp_w2", "ple_w", "ple_gate_w", "ple_norm_g"]
    shared = {k: f32(inputs[k]) for k in wkeys}
    shared["rk_r_k"] = f32(inputs["rk_r_k"]).reshape(L, D)
    shared["final_norm_g"] = f32(inputs["final_norm_g"]).reshape(1, D)
    for c in range(NCORES):
        b0 = c * 16
        m = dict(shared)
        m["xp"] = f32(inputs["x_prompt"][c])
        m["xs"] = f32(inputs["x_sample"][b0:b0 + 16]).reshape(64, D)
        m["pp"] = f32(inputs["p_prompt"][:, c])
        m["ps"] = f32(inputs["p_sample"][:, b0:b0 + 16]).reshape(L, 64, DPLE)
        m["ssc"] = f32(inputs["state_sconv"][:, b0:b0 + 16]).reshape(L, 32, DSC)
        m["ssh"] = f32(inputs["state_shift"][:, b0:b0 + 16])
        m["swkv"] = f32(inputs["state_wkv"][:, b0:b0 + 16])
        m["scf"] = f32(inputs["state_cconv"][:, b0:b0 + 16]).reshape(L, 480, DCF)
        in_maps.append(m)
    res = run_bass_kernel_spmd(nc, in_maps, core_ids=list(range(NCORES)))
    R = res.results
    y_prompt = np.stack([R[c]["yp"] for c in range(NCORES)], 0)
    y_sample = np.concatenate([R[c]["ys"].reshape(16, 4, D) for c in range(NCORES)], 0)
    sc_p = np.stack([R[c]["o_scp"] for c in range(NCORES)], 1)
    sh_p = np.stack([R[c]["o_shp"].reshape(L, RKP) for c in range(NCORES)], 1)
    wkv_p = np.stack([R[c]["o_wkvp"] for c in range(NCORES)], 1)
    cf_p = np.stack([R[c]["o_cfp"] for c in range(NCORES)], 1)
    sc_s = np.concatenate([R[c]["o_scs"].reshape(L, 16, 2, DSC) for c in range(NCORES)], 1)
    sh_s = np.concatenate([R[c]["o_shs"] for c in range(NCORES)], 1)
    wkv_s = np.concatenate([R[c]["o_wkvs"] for c in range(NCORES)], 1)
    cf_s = np.concatenate([R[c]["o_cfs"].reshape(L, 16, 30, DCF) for c in range(NCORES)], 1)
    outs = (y_prompt, y_sample, sc_p, sh_p, wkv_p, cf_p, sc_s, sh_s, wkv_s, cf_s)
    return tuple(np.ascontiguousarray(o, dtype=np.float32) for o in outs)


def kernel(**inputs):
    return run(Cfg(L=4, SEQ=2048, TP=512), inputs)
```
